# Optimizing a Trainium2 kernel written in Bass

```python
import math
import jax, jax.numpy as jnp
from jax import lax
import numpy as np

D_MODEL = 1024
BATCH = 4
SEQ = 4096
DEPTH = 4

CHUNK = 64
Q_BLOCK = 128
D_MIX = D_MODEL
HEAD_DIM = 64
D_FOX = D_MIX // 2
FOX_HEADS = D_FOX // HEAD_DIM
D_S5 = D_MIX // 4
S5_GROUP_CH = 16
S5_GROUPS = D_S5 // S5_GROUP_CH
S5_STATE = 64
D_RET = D_MIX // 4
RET_HEADS = D_RET // HEAD_DIM
ROPE_BASE = 10000.0
EPS = 1e-6
SPLIT_SIZES = (D_FOX, D_FOX, D_FOX, FOX_HEADS, D_S5, D_RET, D_RET, D_RET, D_MIX)
D_IN_PROJ = 3 * D_FOX + FOX_HEADS + D_S5 + 3 * D_RET + D_MIX

kernel_name = 'hybrid_fox_s5_retention_block'


def rms_norm(x, g):
    xf = x.astype(jnp.float32)
    y = xf * lax.rsqrt(jnp.mean(xf * xf, axis=-1, keepdims=True) + EPS)
    return (y * g.astype(jnp.float32)).astype(x.dtype)


def split_cols(proj):
    idx = [int(i) for i in np.cumsum(SPLIT_SIZES)[:-1]]
    return jnp.split(proj, idx, axis=-1)


def rotary(x):
    L, d = x.shape[1], x.shape[3]
    half = d // 2
    freqs = ROPE_BASE ** (-jnp.arange(half, dtype=jnp.float32) / half)
    ang = jnp.arange(L, dtype=jnp.float32)[:, None] * freqs[None, :]
    cos = jnp.cos(ang)[None, :, None, :]
    sin = jnp.sin(ang)[None, :, None, :]
    xf = x.astype(jnp.float32)
    x1, x2 = xf[..., :half], xf[..., half:]
    return jnp.concatenate([x1 * cos - x2 * sin, x1 * sin + x2 * cos], axis=-1)


def forgetting_attention(q, k, v, logf):
    B, L, H, dh = q.shape
    nb = L // Q_BLOCK
    scale = 1.0 / math.sqrt(dh)
    qh = q.transpose(0, 2, 1, 3)
    kh = k.transpose(0, 2, 1, 3)
    vh = v.transpose(0, 2, 1, 3)
    c = jnp.cumsum(logf, axis=1).transpose(0, 2, 1)
    qb = qh.reshape(B, H, nb, Q_BLOCK, dh).transpose(2, 0, 1, 3, 4)
    cb = c.reshape(B, H, nb, Q_BLOCK).transpose(2, 0, 1, 3)
    key_pos = jnp.arange(L)

    def block(args):
        qi, ci, bi = args
        s = jnp.einsum('bhqd,bhkd->bhqk', qi, kh).astype(jnp.float32) * scale
        s = s + ci[..., None] - c[:, :, None, :]
        q_pos = bi * Q_BLOCK + jnp.arange(Q_BLOCK)
        s = jnp.where(key_pos[None, :] <= q_pos[:, None], s, -jnp.inf)
        p = jax.nn.softmax(s, axis=-1)
        return jnp.einsum('bhqk,bhkd->bhqd', p.astype(vh.dtype), vh)

    o = lax.map(block, (qb, cb, jnp.arange(nb)))
    return o.transpose(1, 0, 3, 2, 4).reshape(B, L, H * dh)


def complex_affine_combine(e1, e2):
    a1r, a1i, b1r, b1i = e1
    a2r, a2i, b2r, b2i = e2
    ar = a2r * a1r - a2i * a1i
    ai = a2r * a1i + a2i * a1r
    br = a2r * b1r - a2i * b1i + b2r
    bi = a2r * b1i + a2i * b1r + b2i
    return (ar, ai, br, bi)


def s5_layer(u, a_re, a_im, b_re, b_im, c_re, c_im, d, log_dt, w_glu):
    B, L, _ = u.shape
    uf = u.astype(jnp.float32)
    ug = uf.reshape(B, L, S5_GROUPS, S5_GROUP_CH)
    dt = jnp.exp(log_dt.astype(jnp.float32))[:, None]
    ar = a_re.astype(jnp.float32)
    ai = a_im.astype(jnp.float32)
    mag = jnp.exp(ar * dt)
    lr = mag * jnp.cos(ai * dt)
    li = mag * jnp.sin(ai * dt)
    den = ar * ar + ai * ai
    fr = ((lr - 1.0) * ar + li * ai) / den
    fi = (li * ar - (lr - 1.0) * ai) / den
    br = b_re.astype(jnp.float32)
    bi = b_im.astype(jnp.float32)
    bbr = fr[..., None] * br - fi[..., None] * bi
    bbi = fr[..., None] * bi + fi[..., None] * br
    bu_r = jnp.einsum('gph,blgh->blgp', bbr, ug)
    bu_i = jnp.einsum('gph,blgh->blgp', bbi, ug)
    lr_t = jnp.broadcast_to(lr, bu_r.shape)
    li_t = jnp.broadcast_to(li, bu_r.shape)
    _, _, xr, xi = lax.associative_scan(complex_affine_combine, (lr_t, li_t, bu_r, bu_i), axis=1)
    y = (jnp.einsum('ghp,blgp->blgh', c_re.astype(jnp.float32), xr)
         - jnp.einsum('ghp,blgp->blgh', c_im.astype(jnp.float32), xi))
    y = y.reshape(B, L, D_S5) + d.astype(jnp.float32) * uf
    y = jax.nn.gelu(y)
    y = y * jax.nn.sigmoid(y @ w_glu.astype(jnp.float32))
    return y.astype(u.dtype)


def retention(q, k, v, gn_w):
    B, L, H, dk = q.shape
    dv = v.shape[-1]
    nc = L // CHUNK
    scale = 1.0 / math.sqrt(dk)
    log_gamma = jnp.log1p(-(2.0 ** (-5.0 - jnp.arange(H, dtype=jnp.float32))))

    def chunks(t):
        return t.astype(jnp.float32).reshape(B, nc, CHUNK, H, t.shape[-1]).transpose(1, 0, 3, 2, 4)

    qc, kc, vc = chunks(q), chunks(k), chunks(v)
    pos = jnp.arange(CHUNK, dtype=jnp.float32)
    dmat = jnp.exp(log_gamma[:, None, None] * jnp.abs(pos[:, None] - pos[None, :]))
    scores = jnp.einsum('cbhnd,cbhmd->cbhnm', qc, kc) * scale * dmat
    inner = jnp.einsum('cbhnm,cbhme->cbhne', scores, vc)
    wk = jnp.exp(log_gamma[:, None] * (CHUNK - 1.0 - pos)[None, :])
    wq = jnp.exp(log_gamma[:, None] * (pos + 1.0)[None, :])
    upd = jnp.einsum('cbhmd,cbhme->cbhde', kc * wk[None, None, :, :, None], vc)
    g_chunk = jnp.exp(log_gamma * CHUNK)[:, None, None]

    def step(state, u_i):
        return g_chunk * state + u_i, state

    _, prev_states = lax.scan(step, jnp.zeros((B, H, dk, dv), jnp.float32), upd)
    cross = jnp.einsum('cbhnd,cbhde->cbhne', qc * wq[None, None, :, :, None], prev_states) * scale
    o = (inner + cross).transpose(1, 0, 3, 2, 4).reshape(B, L, H, dv)
    mu = jnp.mean(o, axis=-1, keepdims=True)
    var = jnp.mean(jnp.square(o - mu), axis=-1, keepdims=True)
    o = (o - mu) * lax.rsqrt(var + EPS)
    return (o.reshape(B, L, H * dv) * gn_w.astype(jnp.float32)).astype(v.dtype)


def setup_inputs(seed: int = 0) -> dict:
    key = jax.random.key(seed)
    ks = jax.random.split(key, 17)
    f32 = jnp.float32
    x = jax.random.normal(ks[0], (BATCH, SEQ, D_MODEL), f32)
    norm_w = 1.0 + 0.02 * jax.random.normal(ks[1], (DEPTH, D_MODEL), f32)
    w_in = jax.random.normal(ks[2], (DEPTH, D_MODEL, D_IN_PROJ), f32) * D_MODEL ** -0.5
    fox_b_f = 3.0 + 0.5 * jax.random.normal(ks[3], (DEPTH, FOX_HEADS), f32)
    s5_a_re = -0.5 + 0.01 * jax.random.normal(ks[4], (DEPTH, S5_GROUPS, S5_STATE), f32)
    s5_a_im = (math.pi * jnp.arange(S5_STATE, dtype=f32)[None, None, :]
               + 0.01 * jax.random.normal(ks[5], (DEPTH, S5_GROUPS, S5_STATE), f32))
    s5_b_re = jax.random.normal(ks[6], (DEPTH, S5_GROUPS, S5_STATE, S5_GROUP_CH), f32) * (2 * S5_GROUP_CH) ** -0.5
    s5_b_im = jax.random.normal(ks[7], (DEPTH, S5_GROUPS, S5_STATE, S5_GROUP_CH), f32) * (2 * S5_GROUP_CH) ** -0.5
    s5_c_re = jax.random.normal(ks[8], (DEPTH, S5_GROUPS, S5_GROUP_CH, S5_STATE), f32) * S5_STATE ** -0.5
    s5_c_im = jax.random.normal(ks[9], (DEPTH, S5_GROUPS, S5_GROUP_CH, S5_STATE), f32) * S5_STATE ** -0.5
    s5_d = jax.random.normal(ks[10], (DEPTH, D_S5), f32)
    s5_log_dt = jax.random.uniform(ks[11], (DEPTH, S5_GROUPS), f32, math.log(1e-3), math.log(1e-1))
    s5_w_glu = jax.random.normal(ks[12], (DEPTH, D_S5, D_S5), f32) * D_S5 ** -0.5
    ret_gn_w = 1.0 + 0.02 * jax.random.normal(ks[13], (DEPTH, D_RET), f32)
    w_out = jax.random.normal(ks[14], (DEPTH, D_MIX, D_MODEL), f32) * (0.5 * D_MIX ** -0.5)
    final_norm_w = 1.0 + 0.02 * jax.random.normal(ks[15], (D_MODEL,), f32)
    return {'x': x, 'norm_w': norm_w, 'w_in': w_in, 'fox_b_f': fox_b_f,
            's5_a_re': s5_a_re, 's5_a_im': s5_a_im, 's5_b_re': s5_b_re, 's5_b_im': s5_b_im,
            's5_c_re': s5_c_re, 's5_c_im': s5_c_im, 's5_d': s5_d, 's5_log_dt': s5_log_dt,
            's5_w_glu': s5_w_glu, 'ret_gn_w': ret_gn_w, 'w_out': w_out, 'final_norm_w': final_norm_w}


def reference(x, norm_w, w_in, fox_b_f, s5_a_re, s5_a_im, s5_b_re, s5_b_im,
              s5_c_re, s5_c_im, s5_d, s5_log_dt, s5_w_glu, ret_gn_w, w_out, final_norm_w):
    B, L, _ = x.shape
    for l in range(DEPTH):
        h = rms_norm(x, norm_w[l])
        proj = h @ w_in[l]
        fq, fk, fv, flog, su, rq, rk, rv, gate = split_cols(proj)
        logf = jax.nn.log_sigmoid(flog.astype(jnp.float32) + fox_b_f[l].astype(jnp.float32))
        y_fox = forgetting_attention(fq.reshape(B, L, FOX_HEADS, HEAD_DIM),
                                     fk.reshape(B, L, FOX_HEADS, HEAD_DIM),
                                     fv.reshape(B, L, FOX_HEADS, HEAD_DIM), logf)
        y_s5 = s5_layer(su, s5_a_re[l], s5_a_im[l], s5_b_re[l], s5_b_im[l],
                        s5_c_re[l], s5_c_im[l], s5_d[l], s5_log_dt[l], s5_w_glu[l])
        y_ret = retention(rotary(rq.reshape(B, L, RET_HEADS, HEAD_DIM)),
                          rotary(rk.reshape(B, L, RET_HEADS, HEAD_DIM)),
                          rv.reshape(B, L, RET_HEADS, HEAD_DIM), ret_gn_w[l])
        y = jnp.concatenate([y_fox.astype(x.dtype), y_s5.astype(x.dtype), y_ret.astype(x.dtype)], axis=-1)
        y = y * jax.nn.silu(gate)
        x = x + y @ w_out[l]
    return rms_norm(x, final_norm_w)
```

```python
import math
import os
from contextlib import ExitStack

import numpy as np
import ml_dtypes

import concourse.bass as bass
import concourse.mybir as mybir
from concourse.bass_utils import run_bass_kernel_spmd

F32 = mybir.dt.float32
BF16 = mybir.dt.bfloat16
AF = mybir.ActivationFunctionType
ALU = mybir.AluOpType
AX = mybir.AxisListType

D_MODEL = 1024
BATCH = 4
SEQ = 4096
DEPTH = 4
HD = 64
EPS = 1e-6
NCORES = 8

_NP_BF16 = ml_dtypes.bfloat16


SAME_ENGINE_NOSYNC = tuple(os.environ.get("NOSYNC", "pe,sp").split(","))


class Prog:
    ENGS = ("pe", "act", "dve", "pool", "sp")

    def __init__(self, n_dma_sems=24):
        self.nc = bass.Bass("TRN2", target_bir_lowering=False)
        nc = self.nc
        self.es = ExitStack()
        self.eng = {"pe": nc.tensor, "act": nc.scalar, "dve": nc.vector,
                    "pool": nc.gpsimd, "sp": nc.sync}
        self.sem = {e: self.es.enter_context(nc.semaphore("c_" + e)) for e in self.ENGS}
        self.cnt = {e: 0 for e in self.ENGS}
        self.dsem = [self.es.enter_context(nc.semaphore("d%d" % i)) for i in range(n_dma_sems)]
        self.dcnt = [0] * n_dma_sems
        self.dnext = 0
        self.known = {e: {} for e in self.ENGS}
        self.last_w = {}
        self.readers = {}
        self.out_tokens = []
        self.n_ins = 0
        self.es_cur = self.es
        self.prefix = ""
        self.io = {}
        self.ccsem = self.es.enter_context(nc.semaphore("c_cc"))
        self.cccnt = 0

    def scope(self, name, io=None):
        prog = self

        class _Scope:
            def __enter__(self_s):
                self_s.old = (prog.es_cur, prog.prefix, prog.io)
                prog.es_cur = ExitStack()
                prog.prefix = name
                prog.io = dict(io or {})
                return prog

            def __exit__(self_s, *exc):
                if exc[0] is None:
                    prog.barrier()
                prog.es_cur.close()
                prog.es_cur, prog.prefix, prog.io = self_s.old
                return False

        return _Scope()

    def barrier(self):
        toks = [(e, self.cnt[e]) for e in self.ENGS if self.cnt[e] > 0]
        toks += [(i, 16 * c) for i, c in enumerate(self.dcnt) if c > 0]
        if self.cccnt > 0:
            toks.append(("cc", self.cccnt))
        for e in self.ENGS:
            for t in toks:
                if t[0] != e:
                    self._wait(e, t)
        self.last_w = {}
        self.readers = {}

    def collective(self, kind, ins, outs, groups, reads=(), writes=()):
        self._deps("pool", list(reads), list(writes))
        ins_ = self.nc.gpsimd.collective_compute(kind, ALU.bypass, replica_groups=groups, ins=ins, outs=outs)
        self.cccnt += 1
        ins_.then_inc(self.ccsem, 1)
        tok = ("cc", self.cccnt)
        self._commit(tok, list(reads), list(writes))
        return tok

    def dram_tmp(self, name, shape, dt):
        return self.nc.dram_tensor(name, list(shape), dt, kind="Internal").ap()

    def sb(self, name, shape, dt):
        return self.es_cur.enter_context(self.nc.sbuf_tensor(self.prefix + name, list(shape), dt))

    def ps(self, name, shape, dt):
        return self.es_cur.enter_context(self.nc.psum_tensor(self.prefix + name, list(shape), dt))

    def dram_in(self, name, shape, dt):
        if name in self.io:
            return self.io[name]
        return self.nc.dram_tensor(name, list(shape), dt, kind="ExternalInput").ap()

    def dram_out(self, name, shape, dt):
        if name in self.io:
            return self.io[name]
        return self.nc.dram_tensor(name, list(shape), dt, kind="ExternalOutput").ap()

    def _semh(self, key):
        if key == "cc":
            return self.ccsem
        if isinstance(key, str):
            return self.sem[key]
        return self.dsem[key]

    def _wait(self, e, tok):
        key, val = tok
        if self.known[e].get(key, 0) >= val:
            return
        if key == e and e in SAME_ENGINE_NOSYNC:
            return
        self.eng[e].wait_ge(self._semh(key), val)
        self.known[e][key] = val

    def _deps(self, e, reads, writes):
        toks = []
        for k in reads:
            if k in self.last_w:
                toks.append(self.last_w[k])
        for k in writes:
            if k in self.last_w:
                toks.append(self.last_w[k])
            toks.extend(self.readers.get(k, ()))
        best = {}
        for key, val in toks:
            if best.get(key, 0) < val:
                best[key] = val
        for key, val in best.items():
            self._wait(e, (key, val))

    def _commit(self, tok, reads, writes):
        for k in reads:
            self.readers.setdefault(k, []).append(tok)
        for k in writes:
            self.last_w[k] = tok
            self.readers[k] = []

    @staticmethod
    def _excl(reads, writes):
        px = [k for k in reads if k.startswith("P:")]
        if px:
            writes = list(writes) + px
        return list(reads), list(writes)

    def op(self, e, fn, reads=(), writes=()):
        reads, writes = self._excl(reads, writes)
        self._deps(e, reads, writes)
        ins = fn(self.eng[e])
        self.cnt[e] += 1
        ins.then_inc(self.sem[e], 1)
        tok = (e, self.cnt[e])
        self._commit(tok, reads, writes)
        self.n_ins += 1
        return tok

    def copy(self, e, out, in_, reads=(), writes=()):
        if e == "act":
            return self.op(e, lambda g: g.copy(out=out, in_=in_), reads, writes)
        return self.op(e, lambda g: g.tensor_copy(out=out, in_=in_), reads, writes)

    def dma(self, out, in_, reads=(), writes=(), q="sp", is_output=False, **kw):
        self._deps(q, reads, writes)
        i = self.dnext
        self.dnext = (self.dnext + 1) % len(self.dsem)
        if self.dcnt[i] > 0:
            self._wait(q, (i, 16 * self.dcnt[i]))
        ins = self.eng[q].dma_start(out=out, in_=in_, **kw)
        self.dcnt[i] += 1
        ins.then_inc(self.dsem[i], 16)
        tok = (i, 16 * self.dcnt[i])
        self._commit(tok, reads, writes)
        if is_output:
            self.out_tokens.append(tok)
        self.n_ins += 1
        return tok

    def finish(self):
        best = {}
        for key, val in self.out_tokens:
            if best.get(key, 0) < val:
                best[key] = val
        for key, val in best.items():
            self._wait("sp", (key, val))
        self.es.close()
        return self.nc


def _rr(lst, state, name):
    i = state.get(name, 0)
    state[name] = i + 1
    return lst[i % len(lst)]


A_GROUPS = ([("q", h, 64) for h in range(4)] + [("k", 0, 68)] + [("k", h, 64) for h in (1, 2, 3)]
            + [("v", 0, 128), ("v", 1, 128), ("rq", 0, 128), ("rqs", 0, 128), ("rk", 0, 128),
               ("rks", 0, 128), ("rv", 0, 128), ("su", 0, 128), ("su", 1, 128)]
            + [("g", i, 128) for i in range(4)])
A_NCOL = sum(g[2] for g in A_GROUPS)


def build_phase_a(L=SEQ, stop=99, ngrp=99):
    P = Prog()
    nc = P.nc
    NB = L // 512
    x = P.dram_in("x", [L, D_MODEL], F32)
    wA = P.dram_in("wA", [D_MODEL, A_NCOL], F32)
    normw = P.dram_in("normw", [128, 8], F32)
    bf = P.dram_in("bf", [128, 1], F32)
    ident_in = P.dram_in("ident", [128, 128], F32)
    cosT = P.dram_in("cosT", [128, L], F32)
    sinT = P.dram_in("sinT", [128, L], F32)
    qT = P.dram_out("qT", [4, 65, L], BF16)
    kT = P.dram_out("kT", [4, 65, L], BF16)
    vT = P.dram_out("vT", [256, L], BF16)
    cneg_o = P.dram_out("cneg", [4, L], F32)
    rqT = P.dram_out("rqT", [128, L], BF16)
    rkT = P.dram_out("rkT", [128, L], BF16)
    rvT = P.dram_out("rvT", [128, L], BF16)
    suT = P.dram_out("suT", [256, L], F32)
    suTb = P.dram_out("suTb", [256, L], BF16)
    sgT = P.dram_out("sgT", [512, L], F32)

    Wb = P.sb("Wb", [128, 8, A_NCOL], BF16)
    wst = [P.sb("wst%d" % i, [128, A_NCOL], F32) for i in range(2)]
    gsb = P.sb("gsb", [128, 8], F32)
    ident_f = P.sb("ident_f", [128, 128], F32)
    ident = P.sb("ident_b", [128, 128], BF16)
    bsb = P.sb("bsb", [128, 1], F32)
    negb = P.sb("negb", [128, 1], F32)
    xt = [P.sb("xt%d" % i, [128, D_MODEL], F32) for i in range(3)]
    sqj = P.sb("sqj", [128, D_MODEL], F32)
    ss = [P.sb("ss%d" % i, [128, 1], F32) for i in range(3)]
    sd = [P.sb("sd%d" % i, [128, 1], F32) for i in range(3)]
    rs = [P.sb("rs%d" % i, [128, 1], F32) for i in range(3)]
    hb = [P.sb("hb%d" % i, [128, D_MODEL], BF16) for i in range(2)]
    hT = [P.sb("hT%d" % i, [128, 8, 512], BF16) for i in range(2)]
    cosb = [P.sb("cosb%d" % i, [128, 512], F32) for i in range(2)]
    sinb = [P.sb("sinb%d" % i, [128, 512], F32) for i in range(2)]
    ob16 = [P.sb("ob16_%d" % i, [128, 512], BF16) for i in range(6)]
    of32 = [P.sb("of32_%d" % i, [128, 512], F32) for i in range(6)]
    flog = P.sb("flog", [128, L], F32)
    ework = P.sb("ework", [128, L], F32)
    onesf = P.sb("onesf", [128, L], F32)
    chat = P.sb("chat", [128, L], BF16)
    onesb = P.sb("onesb", [128, L], BF16)
    ptr = [P.ps("ptr%d" % i, [128, 8 * 128], BF16) for i in range(2)]
    pp = [P.ps("pp%d" % i, [128, 512], F32) for i in range(6)]
    st = {}

    P.dma(gsb[:], normw, writes=["gsb"])
    P.dma(ident_f[:], ident_in, writes=["ident_f"])
    P.dma(bsb[:], bf, writes=["bsb"])
    P.op("dve", lambda e: e.tensor_copy(out=ident[:], in_=ident_f[:]), reads=["ident_f"], writes=["ident"])
    P.op("dve", lambda e: e.tensor_scalar_mul(out=negb[:], in0=bsb[:], scalar1=-1.0),
         reads=["bsb"], writes=["negb"])
    P.op("pool", lambda e: e.memset(onesf[64:68, :], 1.0), writes=["onesf"])
    P.op("pool", lambda e: e.memset(onesb[64:68, :], 1.0), writes=["onesb"])
    for kt in range(8):
        w = wst[kt % 2]
        wk = "wst%d" % (kt % 2)
        P.dma(w[:], wA[kt * 128:(kt + 1) * 128, :], writes=[wk])
        P.op("dve", lambda e, w=w, kt=kt: e.tensor_scalar_mul(out=Wb[:, kt, :], in0=w[:], scalar1=gsb[:, kt:kt + 1]),
             reads=[wk, "gsb"], writes=["Wb%d" % kt])
    wkeys = ["Wb%d" % kt for kt in range(8)]
    if stop <= 0:
        return P.finish()

    for blk in range(NB):
        hTb = hT[blk % 2]
        hk = "hT%d" % (blk % 2)
        c0 = blk * 512
        cb, sbn = cosb[blk % 2], sinb[blk % 2]
        ck, sk = "cosb%d" % (blk % 2), "sinb%d" % (blk % 2)
        P.dma(cb[:], cosT[:, c0:c0 + 512], writes=[ck])
        P.dma(sbn[:], sinT[:, c0:c0 + 512], writes=[sk])
        for ti in range(4):
            n = blk * 4 + ti
            xi = n % 3
            P.dma(xt[xi][:], x[n * 128:(n + 1) * 128, :], writes=["xt%d" % xi])
            P.op("act", lambda e, xi=xi: e.activation(out=sqj[:], in_=xt[xi][:], func=AF.Square, accum_out=ss[xi][:]),
                 reads=["xt%d" % xi], writes=["sqj", "ss%d" % xi])
            P.op("act", lambda e, xi=xi: e.activation(out=sd[xi][:], in_=ss[xi][:], func=AF.Sqrt,
                                                        scale=1.0 / D_MODEL, bias=EPS),
                 reads=["ss%d" % xi], writes=["sd%d" % xi])
            P.op("dve", lambda e, xi=xi: e.reciprocal(out=rs[xi][:], in_=sd[xi][:]),
                 reads=["sd%d" % xi], writes=["rs%d" % xi])
            hi = n % 2
            P.op("dve", lambda e, xi=xi, hi=hi: e.tensor_scalar_mul(out=hb[hi][:], in0=xt[xi][:], scalar1=rs[xi][:, 0:1]),
                 reads=["xt%d" % xi, "rs%d" % xi], writes=["hb%d" % hi])
            pt = ptr[n % 2]
            ptk = "P:ptr%d" % (n % 2)
            for kt in range(8):
                P.op("pe", lambda e, kt=kt, pt=pt, hi=hi: e.transpose(out=pt[:, kt * 128:(kt + 1) * 128],
                                                                    in_=hb[hi][:, kt * 128:(kt + 1) * 128],
                                                                    identity=ident[:]),
                     reads=["hb%d" % hi, "ident"], writes=[ptk])
            P.copy("act" if ti % 2 else "dve", hTb[:, :, ti * 128:(ti + 1) * 128],
                   pt[:].rearrange("p (k t) -> p k t", k=8), reads=[ptk], writes=[hk + "_%d" % ti])
        hkeys = [hk + "_%d" % ti for ti in range(4)]
        if stop <= 1:
            continue

        col = 0
        pend = {}
        for gi, (kind, idx, M) in enumerate(A_GROUPS):
            if stop == 2 and gi >= ngrp:
                col += M
                continue
            pi = st.get("pp", 0) % 6
            st["pp"] = pi + 1
            ps_t = pp[pi]
            pk = "P:pp%d" % pi
            for kt in range(8):
                P.op("pe", lambda e, kt=kt, col=col, M=M, ps_t=ps_t: e.matmul(
                    ps_t[0:M, :], lhsT=Wb[:, kt, col:col + M], rhs=hTb[:, kt, :],
                    start=(kt == 0), stop=(kt == 7)),
                    reads=hkeys + [wkeys[kt]], writes=[pk])
            col += M
            oi = st.get("ob", 0) % 6
            st["ob"] = oi + 1
            o16, o32 = ob16[oi], of32[oi]
            k16, k32 = "ob16_%d" % oi, "of32_%d" % oi
            if kind == "q":
                P.op("act", lambda e, ps_t=ps_t, o16=o16: e.mul(out=o16[0:64, :], in_=ps_t[0:64, :], mul=0.125),
                     reads=[pk], writes=[k16])
                P.dma(qT[idx, 0:64, c0:c0 + 512], o16[0:64, :], reads=[k16], is_output=True)
            elif kind == "k":
                P.op("dve", lambda e, ps_t=ps_t, o16=o16: e.tensor_copy(out=o16[0:64, :], in_=ps_t[0:64, :]),
                     reads=[pk], writes=[k16])
                P.dma(kT[idx, 0:64, c0:c0 + 512], o16[0:64, :], reads=[k16], is_output=True)
                if idx == 0:
                    P.op("act", lambda e, ps_t=ps_t: e.copy(out=flog[64:68, c0:c0 + 512], in_=ps_t[64:68, :]),
                         reads=[pk], writes=["flog%d" % blk])
            elif kind in ("v", "rv"):
                P.op("dve", lambda e, ps_t=ps_t, o16=o16: e.tensor_copy(out=o16[:], in_=ps_t[:]),
                     reads=[pk], writes=[k16])
                dst = vT[idx * 128:(idx + 1) * 128, c0:c0 + 512] if kind == "v" else rvT[:, c0:c0 + 512]
                P.dma(dst, o16[:], reads=[k16], is_output=True)
            elif kind in ("rq", "rk"):
                P.op("dve", lambda e, ps_t=ps_t, o32=o32: e.tensor_tensor(out=o32[:], in0=ps_t[:], in1=cb[:], op=ALU.mult),
                     reads=[pk, ck], writes=[k32])
                pend[kind] = (o32, k32)
            elif kind in ("rqs", "rks"):
                base = kind[:2]
                t1, t1k = pend[base]
                P.op("dve", lambda e, ps_t=ps_t, o32=o32: e.tensor_tensor(out=o32[:], in0=ps_t[:], in1=sbn[:], op=ALU.mult),
                     reads=[pk, sk], writes=[k32])
                P.op("pool", lambda e, t1=t1, o32=o32, o16=o16: e.tensor_tensor(out=o16[:], in0=t1[:], in1=o32[:], op=ALU.add),
                     reads=[t1k, k32], writes=[k16])
                dst = rqT if base == "rq" else rkT
                P.dma(dst[:, c0:c0 + 512], o16[:], reads=[k16], is_output=True)
            elif kind == "su":
                P.op("act", lambda e, ps_t=ps_t, o32=o32: e.copy(out=o32[:], in_=ps_t[:]), reads=[pk], writes=[k32])
                P.dma(suT[idx * 128:(idx + 1) * 128, c0:c0 + 512], o32[:], reads=[k32], is_output=True)
                P.op("dve", lambda e, o32=o32, o16=o16: e.tensor_copy(out=o16[:], in_=o32[:]), reads=[k32], writes=[k16])
                P.dma(suTb[idx * 128:(idx + 1) * 128, c0:c0 + 512], o16[:], reads=[k16], is_output=True)
            elif kind == "g":
                P.op("act", lambda e, ps_t=ps_t, o32=o32: e.activation(out=o32[:], in_=ps_t[:], func=AF.Silu),
                     reads=[pk], writes=[k32])
                P.dma(sgT[idx * 128:(idx + 1) * 128, c0:c0 + 512], o32[:], reads=[k32], is_output=True)

    if stop <= 2:
        return P.finish()
    fk = ["flog%d" % b for b in range(NB)]
    s4 = slice(64, 68)
    P.op("act", lambda e: e.activation(out=ework[s4, :], in_=flog[s4, :], func=AF.Exp, scale=-1.0, bias=negb[s4, 0:1]),
         reads=fk + ["negb"], writes=["ework"])
    P.op("act", lambda e: e.activation(out=flog[s4, :], in_=ework[s4, :], func=AF.Ln, scale=1.0, bias=1.0),
         reads=["ework"], writes=["sp"])
    P.op("dve", lambda e: e.tensor_tensor_scan(out=ework[s4, :], data0=onesf[s4, :], data1=flog[s4, :], initial=0.0,
                                               op0=ALU.mult, op1=ALU.add),
         reads=["sp", "onesf"], writes=["cneg"])
    P.op("dve", lambda e: e.tensor_scalar_mul(out=chat[s4, :], in0=ework[s4, :], scalar1=-1.0),
         reads=["cneg"], writes=["chat"])
    P.dma(cneg_o, ework[s4, :], reads=["cneg"], is_output=True)
    P.dma(qT[:, 64, :], chat[s4, :], reads=["chat"], is_output=True)
    P.dma(kT[:, 64, :], onesb[s4, :], reads=["onesb"], is_output=True)
    return P.finish()


def _a_cols(r):
    fq, fk, fv, flog, su, rq, rk, rv, gate = 0, 512, 1024, 1536, 1544, 1800, 2056, 2312, 2568
    cols = []
    heads = [4 * r + i for i in range(4)]
    for h in heads:
        cols += list(range(fq + 64 * h, fq + 64 * h + 64))
    cols += list(range(fk + 64 * heads[0], fk + 64 * heads[0] + 64)) + list(range(flog + 4 * r, flog + 4 * r + 4))
    for h in heads[1:]:
        cols += list(range(fk + 64 * h, fk + 64 * h + 64))
    cols += list(range(fv + 256 * r, fv + 256 * r + 256))
    rh = [2 * r, 2 * r + 1]

    def plain(base):
        c = []
        for h in rh:
            c += list(range(base + 64 * h, base + 64 * h + 64))
        return c

    def swapped(base):
        c = []
        for h in rh:
            c += list(range(base + 64 * h + 32, base + 64 * h + 64)) + list(range(base + 64 * h, base + 64 * h + 32))
        return c

    cols += plain(rq) + swapped(rq) + plain(rk) + swapped(rk) + plain(rv)
    cols += list(range(su, su + 256))
    cols += list(range(gate + 256 * r, gate + 256 * r + 256))
    cols += list(range(gate + 512 + 128 * r, gate + 512 + 128 * r + 128))
    cols += list(range(gate + 768 + 128 * r, gate + 768 + 128 * r + 128))
    assert len(cols) == A_NCOL
    return np.array(cols)


def _rope_tables(L):
    half = 32
    freqs = (10000.0 ** (-np.arange(half, dtype=np.float32) / half)).astype(np.float32)
    ang = np.arange(L, dtype=np.float32)[None, :] * freqs[:, None]
    cos = np.cos(ang).astype(np.float32)
    sin = np.sin(ang).astype(np.float32)
    cosT = np.concatenate([cos, cos, cos, cos], axis=0)
    sinT = np.concatenate([-sin, sin, -sin, sin], axis=0)
    return np.ascontiguousarray(cosT), np.ascontiguousarray(sinT)


KROWS = 68


def build_fox(L=SEQ, NH=4, P=None, side=None):
    own = P is None
    P = P or Prog()
    NKT = L // 128
    NQB = L // 512
    qT = P.dram_in("qT", [NH, KROWS, L], BF16)
    kT = P.dram_in("kT", [NH, KROWS, L], BF16)
    V = P.dram_in("V", [NH, L, 64], BF16)
    cn = P.dram_in("cnegTM", [NH, 128, NKT], F32)
    mask_in = P.dram_in("maskneg", [128, 128], F32)
    ident_in = P.dram_in("ident", [128, 128], F32)
    oT = P.dram_out("oT", [NH, 65, L], F32)

    qh = [P.sb("qh%d" % i, [KROWS, L], BF16) for i in range(2)]
    kh = [P.sb("kh%d" % i, [KROWS, L], BF16) for i in range(2)]
    va = [P.sb("va%d" % i, [128, NKT, 65], BF16) for i in range(2)]
    cnh = [P.sb("cnh%d" % i, [128, NKT], F32) for i in range(2)]
    mf = P.sb("mf", [128, 128], F32)
    mb = P.sb("mb", [128, 128], BF16)
    idf = P.sb("idf", [128, 128], F32)
    idb = P.sb("idb", [128, 128], BF16)
    pt = [P.sb("pt%d" % i, [128, 512], BF16) for i in range(4)]
    ob = [P.sb("ob%d" % i, [65, 512], F32) for i in range(2)]
    NS = 3 if side is not None else 4
    sps = [P.ps("sps%d" % i, [128, 512], F32) for i in range(NS)]
    ops = [P.ps("ops%d" % i, [128, 512], F32) for i in range(2)]

    P.dma(mf[:], mask_in, writes=["mf"])
    P.dma(idf[:], ident_in, writes=["idf"])
    P.op("dve", lambda e: e.tensor_copy(out=mb[:], in_=mf[:]), reads=["mf"], writes=["mb"])
    P.op("dve", lambda e: e.tensor_copy(out=idb[:], in_=idf[:]), reads=["idf"], writes=["idb"])
    for i in range(2):
        P.op("pool", lambda e, i=i: e.memset(va[i][:, :, 64:65], 1.0), writes=["va1_%d" % i])

    tiles = []
    for h in range(NH):
        for qb in range(NQB):
            nk = 4 * (qb + 1)
            for kt in range(nk):
                tiles.append((h, qb, kt, nk))

    loaded = set()

    def load_head(h):
        if h in loaded or h >= NH:
            return
        loaded.add(h)
        b = h % 2
        P.dma(qh[b][:], qT[h], writes=["qh%d" % b])
        P.dma(kh[b][:], kT[h], writes=["kh%d" % b])
        P.dma(va[b][:, :, 0:64], V[h].rearrange("(k p) d -> p k d", p=128), writes=["va%d" % b])
        if KROWS == 65:
            P.dma(cnh[b][:], cn[h], writes=["cnh%d" % b])

    def emit_s(i):
        h, qb, kt, nk = tiles[i]
        b = h % 2
        load_head(h)
        jj = kt - 4 * qb
        lo = 128 * jj if jj > 0 else 0
        s = sps[i % NS]
        sk = "P:sps%d" % (i % NS)
        q0 = qb * 512
        P.op("pe", lambda e: e.matmul(s[:, lo:512], lhsT=kh[b][0:KROWS, kt * 128:(kt + 1) * 128],
                                      rhs=qh[b][0:KROWS, q0 + lo:q0 + 512], start=True, stop=(jj < 0)),
             reads=["kh%d" % b, "qh%d" % b], writes=[sk])
        if jj >= 0:
            P.op("pe", lambda e: e.matmul(s[:, lo:lo + 128], lhsT=idb[:], rhs=mb[:], start=False, stop=True),
                 reads=["idb", "mb"], writes=[sk])

    def emit_rest(i):
        h, qb, kt, nk = tiles[i]
        b = h % 2
        jj = kt - 4 * qb
        lo = 128 * jj if jj > 0 else 0
        s = sps[i % NS]
        sk = "P:sps%d" % (i % NS)
        p = pt[i % 4]
        pk = "pt%d" % (i % 4)
        o = ops[(h * NQB + qb) % 2]
        ok = "P:ops%d" % ((h * NQB + qb) % 2)
        if KROWS == 65:
            P.op("act", lambda e: e.activation(out=p[:, lo:512], in_=s[:, lo:512], func=AF.Exp,
                                               bias=cnh[b][:, kt:kt + 1], scale=1.0),
                 reads=[sk, "cnh%d" % b], writes=[pk])
        else:
            P.op("act", lambda e: e.activation(out=p[:, lo:512], in_=s[:, lo:512], func=AF.Exp),
                 reads=[sk], writes=[pk])
        P.op("pe", lambda e: e.matmul(o[0:65, lo:512], lhsT=va[b][:, kt, 0:65], rhs=p[:, lo:512],
                                      start=(kt == 0), stop=(kt == nk - 1)),
             reads=[pk, "va%d" % b, "va1_%d" % b], writes=[ok])
        if kt == nk - 1:
            oi = (h * NQB + qb) % 2
            P.op("dve", lambda e: e.tensor_copy(out=ob[oi][:], in_=o[0:65, :]), reads=[ok], writes=["ob%d" % oi])
            P.dma(oT[h, :, qb * 512:(qb + 1) * 512], ob[oi][:], reads=["ob%d" % oi], is_output=True, q="pool")
            if qb == NQB - 1:
                load_head(h + 2)

    LA = 2
    load_head(0)
    load_head(1)
    n = len(tiles)
    for i in range(min(LA, n)):
        emit_s(i)
    step = 3
    for i in range(n):
        if i + LA < n:
            emit_s(i + LA)
        emit_rest(i)
        if side is not None and i % step == step - 1:
            next(side, None)
    if side is not None:
        for _ in side:
            pass
    return P.finish() if own else None


def _causal_maskneg():
    s = np.arange(128)[:, None]
    t = np.arange(128)[None, :]
    return np.where(s <= t, 0.0, -30000.0).astype(np.float32)


def _ret_gen(L=SEQ, stop=99, P=None, shared=False):
    own = P is None
    P = P or Prog()
    NBLK = L // 128
    rqT = P.dram_in("rqT", [128, L], BF16)
    rkT = P.dram_in("rkT", [128, L], BF16)
    Ktm = P.dram_in("Ktm", [L, 128], BF16)
    Vtm = P.dram_in("Vtm", [L, 128], BF16)
    DT_in = P.dram_in("DT", [128, 256], F32)
    wq_in = P.dram_in("wqT", [128, 128], F32)
    wk_in = P.dram_in("wk", [128, 2], F32)
    gb_in = P.dram_in("gblk", [128, 1], F32)
    BD_in = P.dram_in("BD", [128, 128], F32)
    gn_in = P.dram_in("gnw", [128, 1], F32)
    yT = P.dram_out("yretT", [128, L], F32)

    rq = P.sb("rq", [128, L], BF16)
    rkp = [P.sb("rkp%d" % i, [128, L], BF16) for i in range(2)]
    ks = P.sb("ks", [128, NBLK, 128], BF16)
    vs = P.sb("vs", [128, NBLK, 128], BF16)
    vpad = P.sb("vpad", [128, NBLK, 2, 128], BF16)
    kpad = P.sb("kpad", [128, NBLK, 2, 128], BF16)
    DT = P.sb("DTs", [128, 256], F32)
    wq = P.sb("wqs", [128, 128], F32)
    wk = P.sb("wks", [128, 2], F32)
    gb = P.sb("gbs", [128, 1], F32)
    BD = P.sb("BDs", [128, 128], F32)
    gn = P.sb("gns", [128, 1], F32)
    S = P.sb("S", [128, 64], F32)
    Sbf = [P.sb("Sbf%d" % i, [128, 128], BF16) for i in range(2)]
    AD = [P.sb("AD%d" % i, [128, 256], BF16) for i in range(2)]
    qp = [P.sb("qp%d" % i, [128, 128], BF16) for i in range(2)]
    oT = P.sb("oT", [128, L], F32)
    sq = [P.sb("sq%d" % i, [128, 512], F32) for i in range(2)]
    mS = [P.sb("mS%d" % i, [128, 512], F32) for i in range(2)]
    t1 = [P.sb("t1_%d" % i, [128, 512], F32) for i in range(2)]
    t2 = [P.sb("t2_%d" % i, [128, 512], F32) for i in range(2)]
    yo = [P.sb("yo%d" % i, [128, 512], F32) for i in range(2)]
    nb_ = 1 if shared else 2
    psA = [P.ps("psA%d" % i, [128, 512], F32) for i in range(nb_)] * (2 // nb_)
    psO = [P.ps("psO%d" % i, [128, 512], F32) for i in range(nb_)] * (2 // nb_)
    psU = [P.ps("psU%d" % i, [128, 512], F32) for i in range(nb_)] * (2 // nb_)
    if shared:
        psM, psQ = psA[0], psO[0]
        psMk, psQk = "P:psA0", "P:psO0"
    else:
        psM = P.ps("psM", [128, 512], F32)
        psQ = P.ps("psQ", [128, 512], F32)
        psMk, psQk = "P:psM", "P:psQ"
    cpe = "dve" if shared else "act"

    P.dma(rq[:], rqT, writes=["rq"])
    for h in range(2):
        hs = slice(64 * h, 64 * h + 64)
        zs = slice(64 * (1 - h), 64 * (1 - h) + 64)
        P.op("pool", lambda e, h=h, zs=zs: e.memset(rkp[h][zs, :], 0.0), writes=["rkz%d" % h])
        P.dma(rkp[h][hs, :], rkT[hs, :], writes=["rk%d" % h])
    P.dma(ks[:], Ktm.rearrange("(b p) d -> p b d", p=128), writes=["ks"])
    P.dma(vs[:], Vtm.rearrange("(b p) d -> p b d", p=128), writes=["vs"])
    for t, src, k in ((DT, DT_in, "DT"), (wq, wq_in, "wq"), (wk, wk_in, "wk"), (gb, gb_in, "gb"),
                      (BD, BD_in, "BD"), (gn, gn_in, "gn")):
        P.dma(t[:], src, writes=[k])
    P.op("pool", lambda e: e.memset(vpad[:], 0.0), writes=["vpad"])
    P.op("pool", lambda e: e.memset(kpad[:], 0.0), writes=["kpad"])
    P.op("pool", lambda e: e.memset(S[:], 0.0), writes=["S"])
    for i in range(2):
        P.op("pool", lambda e, i=i: e.memset(Sbf[i][:], 0.0), writes=["Sbf%d" % i])
    for h in range(2):
        hs = slice(64 * h, 64 * h + 64)
        P.op("dve", lambda e, h=h, hs=hs: e.tensor_copy(out=vpad[:, :, h, hs], in_=vs[:, :, hs]),
             reads=["vs", "vpad"], writes=["vpad"])
        P.op("dve", lambda e, h=h, hs=hs: e.tensor_scalar_mul(out=kpad[:, :, h, hs], in0=ks[:, :, hs], scalar1=wk[:, h:h + 1]),
             reads=["ks", "wk", "kpad"], writes=["kpad"])

    if stop <= 0:
        P.dma(yT[:, 0:128], DT[:, 0:128], reads=["DT"], is_output=True)
        return
    for blk in range(NBLK):
        c0 = blk * 128
        cs = slice(c0, c0 + 128)
        a = psA[blk % 2]
        ak = "P:psA%d" % (blk % nb_)
        for h in range(2):
            hs = slice(64 * h, 64 * h + 64)
            P.op("pe", lambda e, h=h, hs=hs: e.matmul(a[:, h * 128:(h + 1) * 128], lhsT=rkp[h][:, cs], rhs=rq[:, cs],
                                                     start=True, stop=True),
                 reads=["rk%d" % h, "rkz%d" % h, "rq"], writes=[ak])
        yield
        ad = AD[blk % 2]
        adk = "AD%d" % (blk % 2)
        P.op("dve", lambda e: e.tensor_tensor(out=ad[:], in0=a[:, 0:256], in1=DT[:], op=ALU.mult),
             reads=[ak, "DT"], writes=[adk])
        if stop <= 1:
            continue
        qpb = qp[blk % 2]
        qpk = "qp%d" % (blk % 2)
        P.op("pool", lambda e: e.tensor_tensor(out=qpb[:], in0=rq[:, cs], in1=wq[:], op=ALU.mult),
             reads=["rq", "wq"], writes=[qpk])
        yield
        o = psO[blk % 2]
        ok = "P:psO%d" % (blk % nb_)
        cur, nxt = Sbf[blk % 2], Sbf[(blk + 1) % 2]
        curk, nxtk = "Sbf%d" % (blk % 2), "Sbf%d" % ((blk + 1) % 2)
        P.op("pe", lambda e: e.matmul(o[:, 0:128], lhsT=vpad[:, blk, 0, :], rhs=ad[:, 0:128], start=True, stop=False),
             reads=["vpad", adk], writes=[ok])
        P.op("pe", lambda e: e.matmul(o[:, 0:128], lhsT=vpad[:, blk, 1, :], rhs=ad[:, 128:256], start=False, stop=False),
             reads=["vpad", adk], writes=[ok])
        P.op("pe", lambda e: e.matmul(o[:, 0:128], lhsT=cur[:], rhs=qpb[:], start=False, stop=True),
             reads=[curk, qpk], writes=[ok])
        if blk < NBLK - 1 and stop > 2:
            u = psU[blk % 2]
            uk = "P:psU%d" % (blk % nb_)
            P.op("pe", lambda e: e.matmul(u[:, 0:64], lhsT=kpad[:, blk, 0, :], rhs=vs[:, blk, 0:64], start=True, stop=False),
                 reads=["kpad", "vs"], writes=[uk])
            P.op("pe", lambda e: e.matmul(u[:, 0:64], lhsT=kpad[:, blk, 1, :], rhs=vs[:, blk, 64:128], start=False, stop=True),
                 reads=["kpad", "vs"], writes=[uk])
        yield
        P.copy(cpe, oT[:, cs], o[:, 0:128], reads=[ok], writes=["oT%d" % (blk // 4)])
        if blk < NBLK - 1 and stop > 2:
            u = psU[blk % 2]
            uk = "P:psU%d" % (blk % nb_)
            P.op("dve", lambda e: e.scalar_tensor_tensor(out=S[:], in0=S[:], scalar=gb[:, 0:1], in1=u[:, 0:64],
                                                         op0=ALU.mult, op1=ALU.add),
                 reads=["S", "gb", uk], writes=["S"])
            P.copy("pool" if shared else "act", nxt[0:64, 0:64], S[0:64, :], reads=["S"], writes=[nxtk])
            P.op("pool", lambda e: e.tensor_copy(out=nxt[64:128, 64:128], in_=S[64:128, :]), reads=["S"], writes=[nxtk])
        yield

    if stop <= 3:
        P.dma(yT[:, 0:128], oT[:, 0:128], reads=["oT0"], is_output=True)
        return P.finish() if own else None
    for c in range(L // 512):
        cs = slice(c * 512, (c + 1) * 512)
        i = c % 2
        ok = ["oT%d" % c]
        if shared:
            P.op("pool", lambda e: e.tensor_tensor(out=sq[i][:], in0=oT[:, cs], in1=oT[:, cs], op=ALU.mult), reads=ok, writes=["sq%d" % i])
        else:
            P.op("act", lambda e: e.activation(out=sq[i][:], in_=oT[:, cs], func=AF.Square), reads=ok, writes=["sq%d" % i])
        yield
        P.op("pe", lambda e: e.matmul(psM[:], lhsT=BD[:], rhs=oT[:, cs], start=True, stop=True),
             reads=ok + ["BD"], writes=[psMk])
        P.op("pe", lambda e: e.matmul(psQ[:], lhsT=BD[:], rhs=sq[i][:], start=True, stop=True),
             reads=["sq%d" % i, "BD"], writes=[psQk])
        yield
        P.copy(cpe, mS[i][:], psM[:], reads=[psMk], writes=["mS%d" % i])
        P.op("dve", lambda e: e.tensor_tensor(out=t1[i][:], in0=mS[i][:], in1=mS[i][:], op=ALU.mult),
             reads=["mS%d" % i], writes=["t1_%d" % i])
        P.op("dve", lambda e: e.tensor_tensor(out=t1[i][:], in0=psQ[:], in1=t1[i][:], op=ALU.subtract),
             reads=[psQk, "t1_%d" % i], writes=["t1_%d" % i])
        P.op("act", lambda e: e.activation(out=t1[i][:], in_=t1[i][:], func=AF.Sqrt, scale=1.0, bias=EPS),
             reads=["t1_%d" % i], writes=["t1_%d" % i])
        P.op("dve", lambda e: e.reciprocal(out=t1[i][:], in_=t1[i][:]), reads=["t1_%d" % i], writes=["t1_%d" % i])
        P.op("pool", lambda e: e.tensor_tensor(out=t2[i][:], in0=oT[:, cs], in1=mS[i][:], op=ALU.subtract),
             reads=ok + ["mS%d" % i], writes=["t2_%d" % i])
        P.op("dve", lambda e: e.scalar_tensor_tensor(out=yo[i][:], in0=t2[i][:], scalar=gn[:, 0:1], in1=t1[i][:],
                                                     op0=ALU.mult, op1=ALU.mult),
             reads=["t2_%d" % i, "t1_%d" % i, "gn"], writes=["yo%d" % i])
        P.dma(yT[:, cs], yo[i][:], reads=["yo%d" % i], is_output=True, q="pool")
        yield
    if own:
        P.nc_done = P.finish()


def build_ret(L=SEQ, stop=99, P=None):
    own = P is None
    P = P or Prog()
    for _ in _ret_gen(L, stop, P=P):
        pass
    return P.finish() if own else None


def _ret_tables(r):
    pos = np.arange(128, dtype=np.float64)
    DT = np.zeros((128, 2, 128), np.float64)
    wq = np.zeros((128, 128), np.float64)
    wk = np.zeros((128, 2), np.float64)
    gb = np.zeros((128, 1), np.float64)
    for hh in range(2):
        H = 2 * r + hh
        lg = np.log1p(-(2.0 ** (-5.0 - H)))
        s = pos[:, None]
        t = pos[None, :]
        ok = (np.floor(s / 64) <= np.floor(t / 64))
        DT[:, hh, :] = np.where(ok, np.exp(lg * np.abs(t - s)), 0.0) * 0.125
        wq[64 * hh:64 * hh + 64, :] = np.exp(lg * (pos + 1.0))[None, :] * 0.125
        wk[:, hh] = np.exp(lg * (127.0 - pos))
        gb[64 * hh:64 * hh + 64, 0] = np.exp(lg * 128.0)
    BD = np.zeros((128, 128), np.float32)
    BD[:64, :64] = 1.0 / 64
    BD[64:, 64:] = 1.0 / 64
    return (DT.reshape(128, 256).astype(np.float32), wq.astype(np.float32), wk.astype(np.float32),
            gb.astype(np.float32), BD)


TWO_PI = 2.0 * math.pi


def _kl(k):
    return list(k) if isinstance(k, (list, tuple)) else [k]


class _V:
    def __init__(self, P):
        self.P = P

    def tt(self, e, out, a, b, op):
        self.P.op(e, lambda g: g.tensor_tensor(out=out[0], in0=a[0], in1=b[0], op=op),
                  reads=_kl(a[1]) + _kl(b[1]), writes=_kl(out[1]))

    def ts(self, e, out, a, s1, op0, s2=None, op1=None, extra=()):
        if op1 is None:
            self.P.op(e, lambda g: g.tensor_scalar(out=out[0], in0=a[0], scalar1=s1, scalar2=None, op0=op0),
                      reads=_kl(a[1]) + list(extra), writes=_kl(out[1]))
        else:
            self.P.op(e, lambda g: g.tensor_scalar(out=out[0], in0=a[0], scalar1=s1, scalar2=s2, op0=op0, op1=op1),
                      reads=_kl(a[1]) + list(extra), writes=_kl(out[1]))

    def stt(self, out, a, scalar, b, op0, op1, extra=()):
        self.P.op("dve", lambda g: g.scalar_tensor_tensor(out=out[0], in0=a[0], scalar=scalar, in1=b[0], op0=op0, op1=op1),
                  reads=_kl(a[1]) + _kl(b[1]) + list(extra), writes=_kl(out[1]))

    def act(self, out, a, func, scale=1.0, bias=0.0, extra=()):
        self.P.op("act", lambda g: g.activation(out=out[0], in_=a[0], func=func, scale=scale, bias=bias),
                  reads=_kl(a[1]) + list(extra), writes=_kl(out[1]))

    def cp(self, e, out, a):
        self.P.copy(e, out[0], a[0], reads=_kl(a[1]), writes=_kl(out[1]))


def _range_reduce(V, x, ki, kf, r, y, m, sarg, carg):
    V.ts("dve", ki, x, 1.0 / TWO_PI, ALU.mult)
    V.cp("dve", kf, ki)
    V.stt(r, kf, -TWO_PI, x, ALU.mult, ALU.add)

    def wrap(dst, src):
        V.ts("dve", m, src, math.pi, ALU.is_gt, -TWO_PI, ALU.mult)
        V.tt("dve", y, src, m, ALU.add)
        V.ts("dve", m, y, -math.pi, ALU.is_lt, TWO_PI, ALU.mult)
        V.tt("dve", dst, y, m, ALU.add)

    wrap(sarg, r)
    V.ts("dve", r, r, math.pi / 2, ALU.add)
    wrap(carg, r)


def build_s5(L=SEQ, P=None):
    own = P is None
    P = P or Prog()
    V = _V(P)
    NJ = L // 8
    NG = 16
    Ub_in = P.dram_in("Ub", [NG, 128, NJ], BF16)
    U32_in = P.dram_in("U32", [NG, 128, NJ], F32)
    are_in = P.dram_in("are", [128, 8], F32)
    aim_in = P.dram_in("aim", [128, 8], F32)
    ldt_in = P.dram_in("ldt", [128, 8], F32)
    bre_in = P.dram_in("bre", [128, 8, 16], F32)
    bim_in = P.dram_in("bim", [128, 8, 16], F32)
    cre_in = P.dram_in("cre", [128, 8, 16], F32)
    cim_in = P.dram_in("cim", [128, 8, 16], F32)
    drep_in = P.dram_in("drep", [128, NG], F32)
    nvec_in = P.dram_in("nvec", [128, 24], F32)
    jvec_in = P.dram_in("jvec", [128, NJ], F32)
    mask_in = P.dram_in("mask01", [128, 128], F32)
    ident_in = P.dram_in("ident", [128, 128], F32)
    GY = P.dram_out("GY", [NG, 128, NJ], F32)
    GYb = P.dram_out("GYb", [NG, 128, NJ], BF16)

    def sbt(name, shape, dt=F32):
        return P.sb("s_" + name, shape, dt)

    are, aim, ldt = sbt("are", [128, 8]), sbt("aim", [128, 8]), sbt("ldt", [128, 8])
    bre, bim = sbt("bre", [128, 8, 16]), sbt("bim", [128, 8, 16])
    cre, cim = sbt("cre", [128, 8, 16]), sbt("cim", [128, 8, 16])
    drep = sbt("drep", [128, NG])
    nvec = sbt("nvec", [128, 24])
    jvec = sbt("jvec", [128, NJ])
    mask = sbt("mask", [128, 128])
    idf = sbt("idf", [128, 128])
    for t, src, k in ((are, are_in, "are"), (aim, aim_in, "aim"), (ldt, ldt_in, "ldt"), (bre, bre_in, "bre"),
                      (bim, bim_in, "bim"), (cre, cre_in, "cre"), (cim, cim_in, "cim"), (drep, drep_in, "drep"),
                      (nvec, nvec_in, "nvec"), (jvec, jvec_in, "jvec"), (mask, mask_in, "mask"), (idf, ident_in, "idf")):
        P.dma(t[:], src, writes=[k])

    dt = sbt("dt", [128, 8]); lm = sbt("lm", [128, 8]); th = sbt("th", [128, 8])
    V.act((dt[:], "dt"), (ldt[:], "ldt"), AF.Exp)
    V.tt("dve", (lm[:], "lm"), (are[:], "are"), (dt[:], "dt"), ALU.mult)
    V.tt("dve", (th[:], "th"), (aim[:], "aim"), (dt[:], "dt"), ALU.mult)
    NE = 24
    em = sbt("em", [128, 8, NE]); ea = sbt("ea", [128, 8, NE])
    eki = sbt("eki", [128, 8, NE], mybir.dt.int32); ekf = sbt("ekf", [128, 8, NE]); er = sbt("er", [128, 8, NE])
    ey = sbt("ey", [128, 8, NE]); emk = sbt("emk", [128, 8, NE]); esa = sbt("esa", [128, 8, NE]); eca = sbt("eca", [128, 8, NE])
    Ere = sbt("Ere", [128, 8, NE]); Eim = sbt("Eim", [128, 8, NE])
    nb = nvec[:, :].unsqueeze(1).to_broadcast([128, 8, NE])
    V.tt("dve", (em[:], "em"), (lm[:, :].unsqueeze(2).to_broadcast([128, 8, NE]), "lm"), (nb, "nvec"), ALU.mult)
    V.tt("dve", (ea[:], "ea"), (th[:, :].unsqueeze(2).to_broadcast([128, 8, NE]), "th"), (nb, "nvec"), ALU.mult)
    V.act((em[:], "em"), (em[:], "em"), AF.Exp)
    _range_reduce(V, (ea[:], "ea"), (eki[:], "eki"), (ekf[:], "ekf"), (er[:], "er"), (ey[:], "ey"), (emk[:], "emk"),
                  (esa[:], "esa"), (eca[:], "eca"))
    V.act((esa[:], "esa"), (esa[:], "esa"), AF.Sin)
    V.act((eca[:], "eca"), (eca[:], "eca"), AF.Sin)
    V.tt("dve", (Ere[:], "Ere"), (em[:], "em"), (eca[:], "eca"), ALU.mult)
    V.tt("dve", (Eim[:], "Eim"), (em[:], "em"), (esa[:], "esa"), ALU.mult)
    lr1 = sbt("lr1", [128, 8]); den = sbt("den", [128, 8]); tmpa = sbt("tmpa", [128, 8]); tmpb = sbt("tmpb", [128, 8])
    fr = sbt("fr", [128, 8]); fi = sbt("fi", [128, 8])
    li = (Eim[:, :, 16], "Eim")
    V.ts("dve", (lr1[:], "lr1"), (Ere[:, :, 16], "Ere"), -1.0, ALU.add)
    V.tt("dve", (den[:], "den"), (are[:], "are"), (are[:], "are"), ALU.mult)
    V.tt("dve", (tmpa[:], "tmpa"), (aim[:], "aim"), (aim[:], "aim"), ALU.mult)
    V.tt("dve", (den[:], "den"), (den[:], "den"), (tmpa[:], "tmpa"), ALU.add)
    P.op("dve", lambda g: g.reciprocal(out=den[:], in_=den[:]), reads=["den"], writes=["den"])
    V.tt("dve", (tmpa[:], "tmpa"), (lr1[:], "lr1"), (are[:], "are"), ALU.mult)
    V.tt("dve", (tmpb[:], "tmpb"), li, (aim[:], "aim"), ALU.mult)
    V.tt("dve", (tmpa[:], "tmpa"), (tmpa[:], "tmpa"), (tmpb[:], "tmpb"), ALU.add)
    V.tt("dve", (fr[:], "fr"), (tmpa[:], "tmpa"), (den[:], "den"), ALU.mult)
    V.tt("dve", (tmpa[:], "tmpa"), li, (are[:], "are"), ALU.mult)
    V.tt("dve", (tmpb[:], "tmpb"), (lr1[:], "lr1"), (aim[:], "aim"), ALU.mult)
    V.tt("dve", (tmpa[:], "tmpa"), (tmpa[:], "tmpa"), (tmpb[:], "tmpb"), ALU.subtract)
    V.tt("dve", (fi[:], "fi"), (tmpa[:], "tmpa"), (den[:], "den"), ALU.mult)
    bbr = sbt("bbr", [128, 8, 16]); bbi = sbt("bbi", [128, 8, 16]); t16a = sbt("t16a", [128, 8, 16]); t16b = sbt("t16b", [128, 8, 16])
    frb = (fr[:, :].unsqueeze(2).to_broadcast([128, 8, 16]), "fr")
    fib = (fi[:, :].unsqueeze(2).to_broadcast([128, 8, 16]), "fi")
    V.tt("dve", (t16a[:], "t16a"), frb, (bre[:], "bre"), ALU.mult)
    V.tt("dve", (t16b[:], "t16b"), fib, (bim[:], "bim"), ALU.mult)
    V.tt("dve", (bbr[:], "bbr"), (t16a[:], "t16a"), (t16b[:], "t16b"), ALU.subtract)
    V.tt("dve", (t16a[:], "t16a"), frb, (bim[:], "bim"), ALU.mult)
    V.tt("dve", (t16b[:], "t16b"), fib, (bre[:], "bre"), ALU.mult)
    V.tt("dve", (bbi[:], "bbi"), (t16a[:], "t16a"), (t16b[:], "t16b"), ALU.add)

    NJ_ = NJ
    rho = sbt("rho", [128, 8, NJ]); cosM = sbt("cosM", [128, 8, NJ]); sinM = sbt("sinM", [128, 8, NJ])
    WR = sbt("WR", [128, 8, NJ]); WI = sbt("WI", [128, 8, NJ]); ZR = sbt("ZR", [128, 8, NJ]); ZI = sbt("ZI", [128, 8, NJ])
    if 8 * NJ >= 1024:
        big1v = WR[:].rearrange("p a j -> p (a j)")[:, 0:1024].rearrange("p (a s h) -> p a s h", a=8, s=8)
        big2v = WI[:].rearrange("p a j -> p (a j)")[:, 0:1024].rearrange("p (a s h) -> p a s h", a=8, s=8)
    else:
        big1v = sbt("big1", [128, 8, 8, 16])[:]
        big2v = sbt("big2", [128, 8, 8, 16])[:]
    B1 = (big1v, "WR")
    B2 = (big2v, "WI")

    def cprod(n0, xr, xi, xrk, xik, outr, outi, neg_im):
        er_b = (Ere[:, :, n0:n0 + 8].unsqueeze(3).to_broadcast([128, 8, 8, 16]), "Ere")
        ei_b = (Eim[:, :, n0:n0 + 8].unsqueeze(3).to_broadcast([128, 8, 8, 16]), "Eim")
        xr_b = (xr[:, :, :].unsqueeze(2).to_broadcast([128, 8, 8, 16]), xrk)
        xi_b = (xi[:, :, :].unsqueeze(2).to_broadcast([128, 8, 8, 16]), xik)
        V.tt("dve", B1, er_b, xr_b, ALU.mult)
        V.tt("dve", B2, ei_b, xi_b, ALU.mult)
        V.tt("pool", outr, B1, B2, ALU.subtract)
        V.tt("dve", B1, er_b, xi_b, ALU.mult)
        V.tt("dve", B2, ei_b, xr_b, ALU.mult)
        if neg_im:
            V.stt(outi, B1, -1.0, B2, ALU.mult, ALU.subtract)
        else:
            V.tt("pool", outi, B1, B2, ALU.add)

    bsr = sbt("bsr", [128, 8, 128], BF16); bsi = sbt("bsi", [128, 8, 128], BF16)
    wir = sbt("wir", [128, 8, 128]); wii = sbt("wii", [128, 8, 128])
    wor = sbt("wor", [128, 8, 128], BF16); woi = sbt("woi", [128, 8, 128], BF16)

    def v4(t):
        return t[:].rearrange("p a (s h) -> p a s h", s=8)

    cprod(0, bbr, bbi, "bbr", "bbi", (v4(bsr), "bsr"), (v4(bsi), "bsi"), False)
    cprod(8, bbr, bbi, "bbr", "bbi", (v4(wir), "wir"), (v4(wii), "wii"), False)
    cprod(16, cre, cim, "cre", "cim", (v4(wor), "wor"), (v4(woi), "woi"), True)

    wop_r = sbt("wop_r", [128, NG, 128], BF16); wop_i = sbt("wop_i", [128, NG, 128], BF16)
    wip_r = sbt("wip_r", [128, NG, 128], BF16); wip_i = sbt("wip_i", [128, NG, 128], BF16)
    m0 = sbt("m0", [128, NG, 128], BF16)
    for t, k in ((wop_r, "wop_r"), (wop_i, "wop_i"), (wip_r, "wip_r"), (wip_i, "wip_i")):
        P.op("pool", lambda e, t=t: e.memset(t[:], 0.0), writes=[k])
    pst = [P.ps("pst%d" % i, [128, 512], F32) for i in range(2)]
    psy = [P.ps("psy%d" % i, [128, 512], F32) for i in range(2)]
    pss = [P.ps("pss%d" % i, [128, 512], F32) for i in range(4)]
    for g in range(NG):
        Pp, gp = g // 2, g % 2
        hs = slice(64 * gp, 64 * gp + 64)
        P.op("pool", lambda e, g=g, Pp=Pp, hs=hs: e.tensor_copy(out=wop_r[hs, g, :], in_=wor[hs, Pp, :]),
             reads=["wor", "wop_r"], writes=["wop_r"])
        P.op("pool", lambda e, g=g, Pp=Pp, hs=hs: e.tensor_copy(out=wop_i[hs, g, :], in_=woi[hs, Pp, :]),
             reads=["woi", "wop_i"], writes=["wop_i"])
    for Pp in range(8):
        for (src, srck, dst, dstk) in ((wir, "wir", wip_r, "wip_r"), (wii, "wii", wip_i, "wip_i")):
            ti = st_i = (Pp * 2 + (0 if src is wir else 1)) % 2
            tp = pst[ti]
            tk = "P:pst%d" % ti
            P.op("pe", lambda e, src=src, Pp=Pp, tp=tp: e.transpose(out=tp[:, 0:128], in_=src[:, Pp, :], identity=idf[:]),
                 reads=[srck, "idf"], writes=[tk])
            for gp in range(2):
                g = 2 * Pp + gp
                cs = slice(64 * gp, 64 * gp + 64)
                P.copy("act" if gp else "dve", dst[:, g, cs], tp[:, cs], reads=[tk, dstk], writes=[dstk])
    for g in range(NG):
        Pp = g // 2
        tp = pst[g % 2]
        tk = "P:pst%d" % (g % 2)
        P.op("pe", lambda e, g=g, Pp=Pp, tp=tp: e.matmul(tp[:, 0:128], lhsT=bsr[:, Pp, :], rhs=wop_r[:, g, :], start=True, stop=False),
             reads=["bsr", "wop_r"], writes=[tk])
        P.op("pe", lambda e, g=g, Pp=Pp, tp=tp: e.matmul(tp[:, 0:128], lhsT=bsi[:, Pp, :], rhs=wop_i[:, g, :], start=False, stop=True),
             reads=["bsi", "wop_i"], writes=[tk])
        P.op("dve", lambda e, g=g, tp=tp: e.tensor_tensor(out=m0[:, g, :], in0=tp[:, 0:128], in1=mask[:], op=ALU.mult),
             reads=[tk, "mask", "m0"], writes=["m0"])

    NT = 8 * NJ
    kiB_ap = ZI[:].bitcast(mybir.dt.int32)
    ph = sbt("ph", [128, 8]); phi_ = sbt("phi_", [128, 8], mybir.dt.int32); phf = sbt("phf", [128, 8]); rho1 = sbt("rho1", [128, 8])
    V.ts("dve", (ph[:], "ph"), (th[:], "th"), 8.0, ALU.mult)
    V.ts("dve", (phi_[:], "phi_"), (ph[:], "ph"), 1.0 / TWO_PI, ALU.mult)
    V.cp("dve", (phf[:], "phf"), (phi_[:], "phi_"))
    V.stt((ph[:], "ph"), (phf[:], "phf"), -TWO_PI, (ph[:], "ph"), ALU.mult, ALU.add)
    V.act((rho1[:], "rho1"), (lm[:], "lm"), AF.Exp, scale=8.0)
    V.cp("dve", (rho[:], "rho"), (rho1[:, :].unsqueeze(2).to_broadcast([128, 8, NJ]), "rho1"))
    P.op("pool", lambda e: e.memset(rho[:, :, 0:1], 0.0), reads=["rho"], writes=["rho"])
    V.tt("dve", (WR[:], "WR"), (ph[:, :].unsqueeze(2).to_broadcast([128, 8, NJ]), "ph"),
         (jvec[:, :].unsqueeze(1).to_broadcast([128, 8, NJ]), "jvec"), ALU.mult)
    _range_reduce(V, (WR[:], "WR"), (kiB_ap, "ZI"), (WI[:], "WI"), (ZR[:], "ZR"), (ZI[:], "ZI"), (WI[:], "WI"),
                  (sinM[:], "sinM"), (cosM[:], "cosM"))
    V.act((sinM[:], "sinM"), (sinM[:], "sinM"), AF.Sin)
    V.act((cosM[:], "cosM"), (cosM[:], "cosM"), AF.Sin)

    ub = [sbt("ub%d" % i, [128, NJ], BF16) for i in range(4)]
    u32 = [sbt("u32_%d" % i, [128, NJ]) for i in range(2)]
    if NJ >= 512:
        ta = [ZR[:, i, 0:512] for i in range(2)]
        tb = [ZI[:, i, 0:512] for i in range(2)]
        P.op("dve", lambda e: e.memset(ZR[:, 0:2, 0:1], 0.0), reads=["ZR", "ZI"], writes=["ta0", "ta1", "tb0", "tb1", "ZR"])
        P.op("dve", lambda e: e.memset(ZI[:, 0:2, 0:1], 0.0), reads=["ZR", "ZI"], writes=["ta0", "ta1", "tb0", "tb1", "ZI"])
    else:
        ta = [sbt("ta%d" % i, [128, 512])[:] for i in range(2)]
        tb = [sbt("tb%d" % i, [128, 512])[:] for i in range(2)]
    NC5 = NJ // 512 if NJ >= 512 else 1
    CW = min(NJ, 512)
    for Pp in range(8):
        for gp in range(2):
            g = 2 * Pp + gp
            P.dma(ub[g % 4][:], Ub_in[g], writes=["ub%d" % (g % 4)])
        for c in range(NC5):
            cs = slice(c * CW, (c + 1) * CW)
            sr, si = pss[(2 * Pp) % 4], pss[(2 * Pp + 1) % 4]
            srk, sik = "P:pss%d" % ((2 * Pp) % 4), "P:pss%d" % ((2 * Pp + 1) % 4)
            for (ps_, psk, wt, wtk) in ((sr, srk, wip_r, "wip_r"), (si, sik, wip_i, "wip_i")):
                for gp in range(2):
                    g = 2 * Pp + gp
                    P.op("pe", lambda e, ps_=ps_, wt=wt, g=g, gp=gp: e.matmul(ps_[:, 0:CW], lhsT=wt[:, g, :], rhs=ub[g % 4][:, cs],
                                                                           start=(gp == 0), stop=(gp == 1)),
                         reads=[wtk, "ub%d" % (g % 4)], writes=[psk])
            i = Pp % 2
            cM, sM = (cosM[:, Pp, cs], "cosM"), (sinM[:, Pp, cs], "sinM")
            V.tt("dve", (ta[i][:, 0:CW], "ta%d" % i), (sr[:, 0:CW], srk), cM, ALU.mult)
            V.tt("dve", (tb[i][:, 0:CW], "tb%d" % i), (si[:, 0:CW], sik), sM, ALU.mult)
            V.tt("pool", (WR[:, Pp, cs], "WRo"), (ta[i][:, 0:CW], "ta%d" % i), (tb[i][:, 0:CW], "tb%d" % i), ALU.add)
            V.tt("dve", (ta[i][:, 0:CW], "ta%d" % i), (si[:, 0:CW], sik), cM, ALU.mult)
            V.tt("dve", (tb[i][:, 0:CW], "tb%d" % i), (sr[:, 0:CW], srk), sM, ALU.mult)
            V.tt("pool", (WI[:, Pp, cs], "WIo"), (ta[i][:, 0:CW], "ta%d" % i), (tb[i][:, 0:CW], "tb%d" % i), ALU.subtract)

    def flat(t):
        return t[:].rearrange("p a j -> p (a j)")

    P.op("dve", lambda e: e.tensor_tensor_scan(out=flat(ZR), data0=flat(rho), data1=flat(WR), initial=0.0, op0=ALU.mult, op1=ALU.add),
         reads=["rho", "WRo", "WR", "ZR"], writes=["ZR", "ta0", "ta1"])
    P.op("dve", lambda e: e.tensor_tensor_scan(out=flat(ZI), data0=flat(rho), data1=flat(WI), initial=0.0, op0=ALU.mult, op1=ALU.add),
         reads=["rho", "WIo", "WI", "ZI"], writes=["ZI", "tb0", "tb1"])
    XR = sbt("XR", [128, 8, NJ], BF16); XI = sbt("XI", [128, 8, NJ], BF16)
    P.op("pool", lambda e: e.memset(XR[:, :, 0:1], 0.0), writes=["XR"])
    P.op("pool", lambda e: e.memset(XI[:, :, 0:1], 0.0), writes=["XI"])
    n1 = NJ - 1
    zr, zi = (ZR[:, :, 0:n1], "ZR"), (ZI[:, :, 0:n1], "ZI")
    cm, sm = (cosM[:, :, 0:n1], "cosM"), (sinM[:, :, 0:n1], "sinM")
    V.tt("dve", (WR[:, :, 0:n1], "WR2"), zr, cm, ALU.mult)
    V.tt("pool", (WI[:, :, 0:n1], "WI2"), zi, sm, ALU.mult)
    V.tt("dve", (XR[:, :, 1:NJ], "XR"), (WR[:, :, 0:n1], "WR2"), (WI[:, :, 0:n1], "WI2"), ALU.subtract)
    V.tt("dve", (WR[:, :, 0:n1], "WR2"), zr, sm, ALU.mult)
    V.tt("pool", (WI[:, :, 0:n1], "WI2"), zi, cm, ALU.mult)
    V.tt("dve", (XI[:, :, 1:NJ], "XI"), (WR[:, :, 0:n1], "WR2"), (WI[:, :, 0:n1], "WI2"), ALU.add)

    NBUF = 3
    yv = [sbt("yv%d" % i, [128, 512]) for i in range(NBUF)]
    go = [sbt("go%d" % i, [128, 512]) for i in range(NBUF)]
    gob = [sbt("gob%d" % i, [128, 512], BF16) for i in range(NBUF)]
    GC = math.sqrt(2.0 / math.pi)
    for g in range(NG):
        Pp = g // 2
        ui, u3 = g % 4, g % 2
        P.dma(ub[ui][:], Ub_in[g], writes=["ub%d" % ui])
        P.dma(u32[u3][:], U32_in[g], writes=["u32_%d" % u3])
        for c in range(NC5):
            cs = slice(c * CW, (c + 1) * CW)
            i = (g * NC5 + c) % NBUF
            y = psy[i % 2]
            yk = "P:psy%d" % (i % 2)
            P.op("pe", lambda e: e.matmul(y[:, 0:CW], lhsT=m0[:, g, :], rhs=ub[ui][:, cs], start=True, stop=False),
                 reads=["m0", "ub%d" % ui], writes=[yk])
            P.op("pe", lambda e: e.matmul(y[:, 0:CW], lhsT=wop_r[:, g, :], rhs=XR[:, Pp, cs], start=False, stop=False),
                 reads=["wop_r", "XR"], writes=[yk])
            P.op("pe", lambda e: e.matmul(y[:, 0:CW], lhsT=wop_i[:, g, :], rhs=XI[:, Pp, cs], start=False, stop=True),
                 reads=["wop_i", "XI"], writes=[yk])
            W = slice(0, CW)
            yvk, gok, gobk = "yv%d" % i, "go%d" % i, "gob%d" % i
            V.stt((yv[i][:, W], yvk), (u32[u3][:, cs], "u32_%d" % u3), drep[:, g:g + 1], (y[:, W], yk), ALU.mult, ALU.add,
                  extra=["drep"])
            V.act((go[i][:, W], gok), (yv[i][:, W], yvk), AF.Gelu_apprx_tanh)
            V.cp("pool" if g % 2 else "dve", (gob[i][:, W], gobk), (go[i][:, W], gok))
            P.dma(GY[g][:, cs], go[i][:, W], reads=[gok], is_output=True, q="pool")
            P.dma(GYb[g][:, cs], gob[i][:, W], reads=[gobk], is_output=True, q="pool")
    return P.finish() if own else None


def _s5_host_inputs(inp, l, L):
    def pair(a):
        a = np.asarray(a, np.float32)
        tail = a.shape[2:]
        return np.ascontiguousarray(a.reshape(8, 2, 64, *tail).transpose(1, 2, 0, *range(3, 3 + len(tail))).reshape(128, 8, *tail))
    d = {}
    d["are"] = pair(inp["s5_a_re"][l])
    d["aim"] = pair(inp["s5_a_im"][l])
    d["ldt"] = pair(np.repeat(inp["s5_log_dt"][l][:, None], 64, axis=1))
    d["bre"] = pair(inp["s5_b_re"][l])
    d["bim"] = pair(inp["s5_b_im"][l])
    d["cre"] = pair(np.transpose(inp["s5_c_re"][l], (0, 2, 1)))
    d["cim"] = pair(np.transpose(inp["s5_c_im"][l], (0, 2, 1)))
    dd = np.asarray(inp["s5_d"][l], np.float32).reshape(16, 16)
    d["drep"] = np.ascontiguousarray(np.tile(dd.T, (8, 1)))
    nv = np.concatenate([-(np.arange(8) + 1.0), 7.0 - np.arange(8), np.arange(8) + 1.0]).astype(np.float32)
    d["nvec"] = np.ascontiguousarray(np.tile(nv[None, :], (128, 1)))
    d["jvec"] = np.ascontiguousarray(np.tile((np.arange(L // 8, dtype=np.float32) + 1.0)[None, :], (128, 1)))
    s = np.arange(128)[:, None] // 16
    t = np.arange(128)[None, :] // 16
    d["mask01"] = (t >= s).astype(np.float32)
    d["ident"] = np.eye(128, dtype=np.float32)
    return d


def _shuffle_in(suT, L):
    a = np.asarray(suT).reshape(16, 16, L // 8, 8)
    return np.ascontiguousarray(a.transpose(0, 3, 1, 2).reshape(16, 128, L // 8))


def _shuffle_out(G, L):
    a = np.asarray(G).reshape(16, 8, 16, L // 8)
    return np.ascontiguousarray(a.transpose(0, 2, 3, 1).reshape(256, L))


def build_phase_c(Tc=2048, final=False):
    P = Prog()
    V = _V(P)
    NCH = Tc // 512
    oT8 = P.dram_in("oT8", [8, 65, Tc], F32)
    yret = P.dram_in("yret", [256, Tc], F32)
    gy = P.dram_in("gy", [256, Tc], F32)
    gyb = P.dram_in("gyb", [256, Tc], BF16)
    sg = P.dram_in("sg", [1024, Tc], F32)
    wglu = P.dram_in("wglu", [256, 256], F32)
    wout = P.dram_in("wout", [1024, 1024], F32)
    x = P.dram_in("x", [Tc, D_MODEL], F32)
    fnw = P.dram_in("fnw", [128, D_MODEL], F32)
    xo = P.dram_out("xo", [Tc, D_MODEL], F32)

    Wo = P.sb("Wo", [128, 8, 1024], BF16)
    Wg = P.sb("Wg", [128, 2, 256], BF16)
    wst = [P.sb("wst%d" % i, [128, 1024], F32) for i in range(2)]
    fn = P.sb("fn", [128, D_MODEL], F32)
    yT = [P.sb("yT%d" % i, [128, 8, 512], BF16) for i in range(2)]
    la = [P.sb("la%d" % i, [128, 512], F32) for i in range(2)]
    lb = [P.sb("lb%d" % i, [128, 512], F32) for i in range(2)]
    lc = [P.sb("lc%d" % i, [128, 512], F32) for i in range(2)]
    gbs = [P.sb("gbs%d" % i, [128, 2, 512], BF16) for i in range(2)]
    xt = [P.sb("xt%d" % i, [128, D_MODEL], F32) for i in range(2)]
    xn = [P.sb("xn%d" % i, [128, D_MODEL], F32) for i in range(2)]
    sqj = P.sb("sqj", [128, D_MODEL], F32)
    ss = [P.sb("ss%d" % i, [128, 1], F32) for i in range(2)]
    psg = [P.ps("psg%d" % i, [128, 512], F32) for i in range(2)]
    pso = [P.ps("pso%d" % i, [128, 512], F32) for i in range(4)]

    if final:
        P.dma(fn[:], fnw, writes=["fn"])
    for kt in range(8):
        w, wk = wst[kt % 2], "wst%d" % (kt % 2)
        P.dma(w[:], wout[kt * 128:(kt + 1) * 128, :], writes=[wk])
        P.copy("dve", Wo[:, kt, :], w[:], reads=[wk], writes=["Wo"])
    for kt in range(2):
        w, wk = wst[kt % 2], "wst%d" % (kt % 2)
        P.dma(w[:, 0:256], wglu[kt * 128:(kt + 1) * 128, :], writes=[wk])
        P.copy("dve", Wg[:, kt, :], w[:, 0:256], reads=[wk], writes=["Wg"])

    cnt = 0
    for c in range(NCH):
        cs = slice(c * 512, (c + 1) * 512)
        y = yT[c % 2]
        yk = "yT%d" % (c % 2)
        P.dma(gbs[c % 2][:], gyb[:, cs].rearrange("(k p) t -> p k t", p=128), writes=["gbs%d" % (c % 2)])
        for kt in range(8):
            i = cnt % 2
            cnt += 1
            A, B, C = (la[i][:], "la%d" % i), (lb[i][:], "lb%d" % i), (lc[i][:], "lc%d" % i)
            P.dma(lc[i][:], sg[kt * 128:(kt + 1) * 128, cs], writes=[C[1]])
            yo = (y[:, kt, :], yk + "_%d" % kt)
            if kt < 4:
                for hh in range(2):
                    h = 2 * kt + hh
                    ps_ = slice(64 * hh, 64 * hh + 64)
                    P.dma(la[i][ps_, :], oT8[h, 0:64, cs], writes=[A[1]])
                    P.dma(lb[i][ps_, :], oT8[h, 64:65, cs].to_broadcast([64, 512]), writes=[B[1]])
                P.op("dve", lambda e, i=i: e.reciprocal(out=lb[i][:], in_=lb[i][:]), reads=[B[1]], writes=[B[1]])
                V.tt("dve", A, A, B, ALU.mult)
                V.tt("pool", yo, A, C, ALU.mult)
            elif kt < 6:
                j = kt - 4
                P.dma(la[i][:], gy[j * 128:(j + 1) * 128, cs], writes=[A[1]])
                pg, pgk = psg[j], "P:psg%d" % j
                for k2 in range(2):
                    P.op("pe", lambda e, k2=k2, j=j, pg=pg: e.matmul(pg[:], lhsT=Wg[:, k2, j * 128:(j + 1) * 128],
                                                                  rhs=gbs[c % 2][:, k2, :], start=(k2 == 0), stop=(k2 == 1)),
                         reads=["Wg", "gbs%d" % (c % 2)], writes=[pgk])
                V.act(B, (pg[:], pgk), AF.Sigmoid)
                V.tt("dve", A, A, B, ALU.mult)
                V.tt("pool", yo, A, C, ALU.mult)
            else:
                j = kt - 6
                P.dma(la[i][:], yret[j * 128:(j + 1) * 128, cs], writes=[A[1]])
                V.tt("pool", yo, A, C, ALU.mult)
        ykeys = [yk + "_%d" % kt for kt in range(8)]
        for tt in range(4):
            n = c * 4 + tt
            xi = n % 2
            P.dma(xt[xi][:], x[n * 128:(n + 1) * 128, :], writes=["xt%d" % xi])
            for ch in range(2):
                po = pso[(n * 2 + ch) % 4]
                pok = "P:pso%d" % ((n * 2 + ch) % 4)
                for kt in range(8):
                    P.op("pe", lambda e, kt=kt, po=po, ch=ch, tt=tt: e.matmul(
                        po[:], lhsT=y[:, kt, tt * 128:(tt + 1) * 128], rhs=Wo[:, kt, ch * 512:(ch + 1) * 512],
                        start=(kt == 0), stop=(kt == 7)), reads=[ykeys[kt], "Wo"], writes=[pok])
                V.tt("dve", (xn[xi][:, ch * 512:(ch + 1) * 512], "xn%d" % xi), (po[:], pok),
                     (xt[xi][:, ch * 512:(ch + 1) * 512], "xt%d" % xi), ALU.add)
            if final:
                P.op("act", lambda e, xi=xi: e.activation(out=sqj[:], in_=xn[xi][:], func=AF.Square, accum_out=ss[xi][:]),
                     reads=["xn%d" % xi], writes=["sqj", "ss%d" % xi])
                P.op("act", lambda e, xi=xi: e.activation(out=ss[xi][:], in_=ss[xi][:], func=AF.Sqrt, scale=1.0 / D_MODEL, bias=EPS),
                     reads=["ss%d" % xi], writes=["ss%d" % xi])
                P.op("dve", lambda e, xi=xi: e.reciprocal(out=ss[xi][:], in_=ss[xi][:]), reads=["ss%d" % xi], writes=["ss%d" % xi])
                V.stt((xn[xi][:], "xn%d" % xi), (xn[xi][:], "xn%d" % xi), ss[xi][:, 0:1], (fn[:], "fn"), ALU.mult, ALU.mult,
                      extra=["ss%d" % xi])
            P.dma(xo[n * 128:(n + 1) * 128, :], xn[xi][:], reads=["xn%d" % xi], is_output=True)
    return P.finish()


def _run(nc, in_maps):
    res = run_bass_kernel_spmd(nc, in_maps, core_ids=list(range(len(in_maps))))
    return res.results


def _c(a):
    return np.ascontiguousarray(a)


F_GROUPS = ([("q", h, 64) for h in range(4)] + [("k", 0, 68)] + [("k", h, 64) for h in (1, 2, 3)]
            + [("rq", 0, 128), ("rqs", 0, 128), ("rk", 0, 128), ("rks", 0, 128), ("su", 0, 128), ("su", 1, 128)]
            + [("g", i, 128) for i in range(4)])
F_NF = sum(g[2] for g in F_GROUPS)
F_NCOL = F_NF + 384


def emit_phase_a(P, L=SEQ):
    NB = L // 512
    NJ = L // 8
    x = P.io["x"]; wA = P.io["wA"]; normw = P.io["normw"]; bf = P.io["bf"]; ident_in = P.io["ident"]
    cosT = P.io["cosT"]; sinT = P.io["sinT"]
    qT = P.io["qT"]; kT = P.io["kT"]; Vfox = P.io["Vfox"]; Vtm = P.io["Vtm"]; Ktm = P.io["Ktm"]
    cneg_o = P.io["cneg"]; rqT = P.io["rqT"]; rkT = P.io["rkT"]; suP = P.io["suP"]; suPb = P.io["suPb"]; sgT = P.io["sgT"]

    Wb = P.sb("Wb", [128, 8, F_NCOL], BF16)
    wst = [P.sb("wst%d" % i, [128, F_NCOL], F32) for i in range(2)]
    gsb = P.sb("gsb", [128, 8], F32)
    ident_f = P.sb("ident_f", [128, 128], F32)
    ident = P.sb("ident_b", [128, 128], BF16)
    bsb = P.sb("bsb", [128, 1], F32)
    negb = P.sb("negb", [128, 1], F32)
    xt = [P.sb("xt%d" % i, [128, D_MODEL], F32) for i in range(3)]
    sqj = P.sb("sqj", [128, D_MODEL], F32)
    ss = [P.sb("ss%d" % i, [128, 1], F32) for i in range(3)]
    sd = [P.sb("sd%d" % i, [128, 1], F32) for i in range(3)]
    rs = [P.sb("rs%d" % i, [128, 1], F32) for i in range(3)]
    hb = [P.sb("hb%d" % i, [128, D_MODEL], BF16) for i in range(8)]
    hT = [P.sb("hT%d" % i, [128, 8, 512], BF16) for i in range(2)]
    cosb = [P.sb("cosb%d" % i, [128, 512], F32) for i in range(2)]
    sinb = [P.sb("sinb%d" % i, [128, 512], F32) for i in range(2)]
    ob16 = [P.sb("ob16_%d" % i, [128, 512], BF16) for i in range(6)]
    of32 = [P.sb("of32_%d" % i, [128, 512], F32) for i in range(6)]
    ov16 = [P.sb("ov16_%d" % i, [128, 384], BF16) for i in range(2)]
    kt16 = [P.sb("kt16_%d" % i, [128, 512], BF16) for i in range(2)]
    flog = P.sb("flog", [128, L], F32)
    ework = P.sb("ework", [128, L], F32)
    onesf = P.sb("onesf", [128, L], F32)
    chat = P.sb("chat", [128, L], BF16)
    ptr = [P.ps("ptr%d" % i, [128, 8 * 128], BF16) for i in range(2)]
    pp = [P.ps("pp%d" % i, [128, 512], F32) for i in range(6)]
    st = {}

    P.dma(gsb[:], normw, writes=["gsb"])
    P.dma(ident_f[:], ident_in, writes=["ident_f"])
    P.dma(bsb[:], bf, writes=["bsb"])
    P.op("dve", lambda e: e.tensor_copy(out=ident[:], in_=ident_f[:]), reads=["ident_f"], writes=["ident"])
    P.op("dve", lambda e: e.tensor_scalar_mul(out=negb[:], in0=bsb[:], scalar1=-1.0), reads=["bsb"], writes=["negb"])
    P.op("pool", lambda e: e.memset(onesf[64:68, :], 1.0), writes=["onesf"])
    pA = P.sb("pA", [128, L], BF16)
    pB = P.sb("pB", [128, L], BF16)
    carry = P.sb("carry", [128, 1], F32)
    P.op("pool", lambda e: e.memset(pA[64:68, :], 1.0), writes=["pA"])
    P.dma(kT[:, 64, :], pA[64:68, :], reads=["pA"], writes=["kT64"], q="pool")
    for rr in (65, 66, 67):
        P.dma(qT[:, rr, :], pA[64:68, :], reads=["pA"], writes=["qT%d" % rr], q="pool")
    for kt in range(8):
        w = wst[kt % 2]
        wk = "wst%d" % (kt % 2)
        P.dma(w[:], wA[kt * 128:(kt + 1) * 128, :], writes=[wk])
        P.op("dve", lambda e, w=w, kt=kt: e.tensor_scalar_mul(out=Wb[:, kt, :], in0=w[:], scalar1=gsb[:, kt:kt + 1]),
             reads=[wk, "gsb"], writes=["Wb%d" % kt])
    wkeys = ["Wb%d" % kt for kt in range(8)]

    def ne_block(blk):
        for ti in range(4):
            n = blk * 4 + ti
            xi = n % 3
            P.dma(xt[xi][:], x[n * 128:(n + 1) * 128, :], writes=["xt%d" % xi])
            P.op("act", lambda e, xi=xi: e.activation(out=sqj[:], in_=xt[xi][:], func=AF.Square, accum_out=ss[xi][:]),
                 reads=["xt%d" % xi], writes=["sqj", "ss%d" % xi])
            P.op("act", lambda e, xi=xi: e.activation(out=sd[xi][:], in_=ss[xi][:], func=AF.Sqrt, scale=1.0 / D_MODEL, bias=EPS),
                 reads=["ss%d" % xi], writes=["sd%d" % xi])
            P.op("dve", lambda e, xi=xi: e.reciprocal(out=rs[xi][:], in_=sd[xi][:]), reads=["sd%d" % xi], writes=["rs%d" % xi])
            hi = n % 8
            P.op("dve", lambda e, xi=xi, hi=hi: e.tensor_scalar_mul(out=hb[hi][:], in0=xt[xi][:], scalar1=rs[xi][:, 0:1]),
                 reads=["xt%d" % xi, "rs%d" % xi], writes=["hb%d" % hi])

    def norm_block(blk):
        hTb = hT[blk % 2]
        hk = "hT%d" % (blk % 2)
        c0 = blk * 512
        cb, sbn = cosb[blk % 2], sinb[blk % 2]
        ck, sk = "cosb%d" % (blk % 2), "sinb%d" % (blk % 2)
        P.dma(cb[:], cosT[:, c0:c0 + 512], writes=[ck])
        P.dma(sbn[:], sinT[:, c0:c0 + 512], writes=[sk])
        for ti in range(4):
            n = blk * 4 + ti
            hi = n % 8
            pt = ptr[n % 2]
            ptk = "P:ptr%d" % (n % 2)
            for kt in range(8):
                P.op("pe", lambda e, kt=kt, pt=pt, hi=hi: e.transpose(out=pt[:, kt * 128:(kt + 1) * 128],
                                                                    in_=hb[hi][:, kt * 128:(kt + 1) * 128], identity=ident[:]),
                     reads=["hb%d" % hi, "ident"], writes=[ptk])
            P.copy("act" if ti % 2 else "dve", hTb[:, :, ti * 128:(ti + 1) * 128],
                   pt[:].rearrange("p (k t) -> p k t", k=8), reads=[ptk], writes=[hk + "_%d" % ti])
            pi = st.get("pp", 0) % 6
            st["pp"] = pi + 1
            pv, pvk = pp[pi], "P:pp%d" % pi
            for kt in range(8):
                P.op("pe", lambda e, kt=kt, pv=pv, ti=ti: e.matmul(pv[:, 0:384], lhsT=hTb[:, kt, ti * 128:(ti + 1) * 128],
                                                                 rhs=Wb[:, kt, F_NF:F_NF + 384], start=(kt == 0), stop=(kt == 7)),
                     reads=[hk + "_%d" % ti, wkeys[kt]], writes=[pvk])
            vi = n % 2
            P.copy("dve" if ti % 2 else "act", ov16[vi][:], pv[:, 0:384], reads=[pvk], writes=["ov%d" % vi])
            P.dma(Vfox[:, n * 128:(n + 1) * 128, :].rearrange("h p d -> p h d"),
                  ov16[vi][:, 0:256].rearrange("p (h d) -> p h d", h=4), reads=["ov%d" % vi], writes=["Vfox%d" % n], q="pool")
            P.dma(Vtm[n * 128:(n + 1) * 128, :], ov16[vi][:, 256:384], reads=["ov%d" % vi], writes=["Vtm%d" % n], q="pool")

    def groups_block(blk):
        hTb = hT[blk % 2]
        hk = "hT%d" % (blk % 2)
        c0 = blk * 512
        cb, sbn = cosb[blk % 2], sinb[blk % 2]
        ck, sk = "cosb%d" % (blk % 2), "sinb%d" % (blk % 2)
        hkeys = [hk + "_%d" % ti for ti in range(4)]

        col = 0
        pend = {}
        for gi, (kind, idx, M) in enumerate(F_GROUPS):
            pi = st.get("pp", 0) % 6
            st["pp"] = pi + 1
            ps_t = pp[pi]
            pk = "P:pp%d" % pi
            rhs_su = None
            for kt in range(8):
                rhs = hTb[:, kt, :]
                P.op("pe", lambda e, kt=kt, col=col, M=M, ps_t=ps_t, rhs=rhs: e.matmul(
                    ps_t[0:M, :], lhsT=Wb[:, kt, col:col + M], rhs=rhs, start=(kt == 0), stop=(kt == 7)),
                    reads=hkeys + [wkeys[kt]], writes=[pk])
            col += M
            oi = st.get("ob", 0) % 6
            st["ob"] = oi + 1
            o16, o32 = ob16[oi], of32[oi]
            k16, k32 = "ob16_%d" % oi, "of32_%d" % oi
            if kind == "q":
                P.op("act", lambda e, ps_t=ps_t, o16=o16: e.mul(out=o16[0:64, :], in_=ps_t[0:64, :], mul=0.125),
                     reads=[pk], writes=[k16])
                P.dma(qT[idx, 0:64, c0:c0 + 512], o16[0:64, :], reads=[k16], writes=["qT%d_%d" % (idx, blk)], q="pool")
            elif kind == "k":
                P.op("dve", lambda e, ps_t=ps_t, o16=o16: e.tensor_copy(out=o16[0:64, :], in_=ps_t[0:64, :]),
                     reads=[pk], writes=[k16])
                P.dma(kT[idx, 0:64, c0:c0 + 512], o16[0:64, :], reads=[k16], writes=["kT%d_%d" % (idx, blk)], q="pool")
                if idx == 0:
                    P.op("act", lambda e, ps_t=ps_t: e.copy(out=flog[64:68, c0:c0 + 512], in_=ps_t[64:68, :]),
                         reads=[pk], writes=["flog%d" % blk])
            elif kind in ("rq", "rk"):
                P.op("dve", lambda e, ps_t=ps_t, o32=o32: e.tensor_tensor(out=o32[:], in0=ps_t[:], in1=cb[:], op=ALU.mult),
                     reads=[pk, ck], writes=[k32])
                pend[kind] = (o32, k32)
            elif kind in ("rqs", "rks"):
                base = kind[:2]
                t1, t1k = pend[base]
                P.op("dve", lambda e, ps_t=ps_t, o32=o32: e.tensor_tensor(out=o32[:], in0=ps_t[:], in1=sbn[:], op=ALU.mult),
                     reads=[pk, sk], writes=[k32])
                P.op("pool", lambda e, t1=t1, o32=o32, o16=o16: e.tensor_tensor(out=o16[:], in0=t1[:], in1=o32[:], op=ALU.add),
                     reads=[t1k, k32], writes=[k16])
                dst = rqT if base == "rq" else rkT
                P.dma(dst[:, c0:c0 + 512], o16[:], reads=[k16], writes=[base + "T%d" % blk], q="pool")
                if base == "rk":
                    pt = ptr[blk % 2]
                    ptk = "P:ptr%d" % (blk % 2)
                    for t4 in range(4):
                        P.op("pe", lambda e, t4=t4, pt=pt, o16=o16: e.transpose(out=pt[:, t4 * 128:(t4 + 1) * 128],
                                                                            in_=o16[:, t4 * 128:(t4 + 1) * 128], identity=ident[:]),
                             reads=[k16, "ident"], writes=[ptk])
                    kb = kt16[blk % 2]
                    P.copy("act", kb[:], pt[:, 0:512], reads=[ptk], writes=["kt16_%d" % (blk % 2)])
                    P.dma(Ktm[c0:c0 + 512, :].rearrange("(t p) d -> p t d", p=128), kb[:].rearrange("p (t d) -> p t d", t=4),
                          reads=["kt16_%d" % (blk % 2)], writes=["Ktm%d" % blk], q="pool")
            elif kind == "su":
                P.op("act", lambda e, ps_t=ps_t, o32=o32: e.copy(out=o32[:].rearrange("p (s j) -> p s j", s=8),
                                                                  in_=ps_t[:].rearrange("p (j s) -> p s j", s=8)),
                     reads=[pk], writes=[k32])
                P.op("dve", lambda e, o32=o32, o16=o16: e.tensor_copy(out=o16[:], in_=o32[:]), reads=[k32], writes=[k16])
                jr = slice(blk * 64, blk * 64 + 64)
                for gl in range(8):
                    g = idx * 8 + gl
                    prt = slice(16 * gl, 16 * gl + 16)
                    P.dma(suP[g].rearrange("(s h) j -> h s j", s=8)[:, :, jr], o32[prt, :].rearrange("p (s j) -> p s j", s=8),
                          reads=[k32], writes=["suP%d_%d" % (g, blk)])
                    P.dma(suPb[g].rearrange("(s h) j -> h s j", s=8)[:, :, jr], o16[prt, :].rearrange("p (s j) -> p s j", s=8),
                          reads=[k16], writes=["suPb%d_%d" % (g, blk)])
            elif kind == "g":
                P.op("act", lambda e, ps_t=ps_t, o32=o32: e.activation(out=o32[:], in_=ps_t[:], func=AF.Silu),
                     reads=[pk], writes=[k32])
                P.dma(sgT[idx * 128:(idx + 1) * 128, c0:c0 + 512], o32[:], reads=[k32], writes=["sgT%d_%d" % (idx, blk)], q="pool")

    ne_block(0)
    if NB > 1:
        ne_block(1)
    norm_block(0)
    s4 = slice(64, 68)

    def c_chain(lo, hi, tag):
        cs_ = slice(lo, hi)
        fk = ["flog%d" % b for b in range(lo // 512, hi // 512)]
        T = lambda k: k + tag
        P.op("act", lambda e: e.activation(out=ework[s4, cs_], in_=flog[s4, cs_], func=AF.Exp, scale=-1.0, bias=negb[s4, 0:1]),
             reads=fk + ["negb"], writes=[T("ework")])
        P.op("act", lambda e: e.activation(out=flog[s4, cs_], in_=ework[s4, cs_], func=AF.Ln, scale=1.0, bias=1.0),
             reads=[T("ework")], writes=[T("sp")])
        init = 0.0 if lo == 0 else carry[s4, 0:1]
        P.op("dve", lambda e: e.tensor_tensor_scan(out=ework[s4, cs_], data0=onesf[s4, cs_], data1=flog[s4, cs_], initial=init,
                                                   op0=ALU.mult, op1=ALU.add),
             reads=[T("sp"), "onesf", "carry"], writes=[T("cneg")])
        P.op("dve", lambda e: e.tensor_copy(out=carry[s4, 0:1], in_=ework[s4, hi - 1:hi]), reads=[T("cneg")], writes=["carry"])
        P.op("dve", lambda e: e.tensor_scalar_mul(out=chat[s4, cs_], in0=ework[s4, cs_], scalar1=-1.0),
             reads=[T("cneg")], writes=[T("chat")])
        P.dma(qT[:, 64, cs_], chat[s4, cs_], reads=[T("chat")], writes=[T("qT64")])
        P.op("dve", lambda e: e.tensor_copy(out=pA[s4, cs_], in_=ework[s4, cs_]), reads=[T("cneg"), "pA"], writes=[T("pA")])
        P.dma(kT[:, 65, cs_], pA[s4, cs_], reads=[T("pA")], writes=[T("kT65")], q="pool")
        P.op("dve", lambda e: e.tensor_tensor(out=flog[s4, cs_], in0=ework[s4, cs_], in1=pA[s4, cs_], op=ALU.subtract),
             reads=[T("cneg"), T("pA"), T("sp")], writes=[T("r1")])
        P.op("dve", lambda e: e.tensor_copy(out=pB[s4, cs_], in_=flog[s4, cs_]), reads=[T("r1")], writes=[T("pB")])
        P.dma(kT[:, 66, cs_], pB[s4, cs_], reads=[T("pB")], writes=[T("kT66")], q="pool")
        P.op("dve", lambda e: e.tensor_tensor(out=ework[s4, cs_], in0=flog[s4, cs_], in1=pB[s4, cs_], op=ALU.subtract),
             reads=[T("r1"), T("pB"), T("cneg"), T("chat"), T("pA"), "carry"], writes=[T("r2")])
        P.op("dve", lambda e: e.tensor_copy(out=pA[s4, cs_], in_=ework[s4, cs_]), reads=[T("r2"), T("pA"), T("kT65")],
             writes=[T("pA2")])
        P.dma(kT[:, 67, cs_], pA[s4, cs_], reads=[T("pA2")], writes=[T("kT67")])

    for blk in range(NB):
        if blk + 1 < NB:
            norm_block(blk + 1)
        if blk + 2 < NB:
            ne_block(blk + 2)
        if blk == NB - 1 and NB > 1:
            c_chain(0, (NB - 1) * 512, "_a")
        groups_block(blk)
    c_chain((NB - 1) * 512 if NB > 1 else 0, L, "_b")


def emit_c1(P, L=SEQ):
    V = _V(P)
    NJ = L // 8
    oT = P.io["oT"]; yret = P.io["yretT"]; GY = P.io["GY"]; GYb = P.io["GYb"]; sg = P.io["sgT"]
    wglu = P.io["wglu"]; ybuf = P.io["ybuf"]
    Wg = P.sb("Wg", [128, 2, 128], BF16)
    wst = P.sb("wst", [128, 2, 128], F32)
    yc = [P.sb("yc%d" % i, [128, 4, 512], BF16) for i in range(2)]
    la = [P.sb("la%d" % i, [128, 512], F32) for i in range(4)]
    lb = [P.sb("lb%d" % i, [128, 512], F32) for i in range(4)]
    lc = [P.sb("lc%d" % i, [128, 512], F32) for i in range(4)]
    GYs = P.sb("GYs", [128, 8, NJ], F32)
    GBs = P.sb("GBs", [128, 2, 8, NJ], BF16)
    for g in range(16):
        prt = slice(16 * (g % 8), 16 * (g % 8) + 16)
        P.dma(GBs[prt, g // 8, :, :], GYb[g].rearrange("(t h) j -> h t j", t=8), writes=["GBs%d" % g], q="sp")
        if g < 8:
            P.dma(GYs[prt, :, :], GY[g].rearrange("(t h) j -> h t j", t=8), writes=["GYs%d" % g])
    gbk = ["GBs%d" % g for g in range(16)]
    gyk = ["GYs%d" % g for g in range(8)]
    psg = [P.ps("psg%d" % i, [128, 512], F32) for i in range(2)]
    P.dma(wst[:], wglu.rearrange("(k p) c -> p k c", p=128), writes=["wst"])
    P.copy("dve", Wg[:], wst[:], reads=["wst"], writes=["Wg"])
    cnt = 0
    ykeys = []
    for c in range(L // 512):
        cs = slice(c * 512, (c + 1) * 512)
        jr = slice(c * 64, c * 64 + 64)
        y = yc[c % 2]
        yk = "yc%d" % (c % 2)
        for kt in range(4):
            i = cnt % 4
            cnt += 1
            Ak = ["la%d_%d" % (i, q) for q in range(8)]
            Bk = ["lb%d_%d" % (i, q) for q in range(2)]
            A, B, C = (la[i][:], Ak), (lb[i][:], Bk), (lc[i][:], "lc%d" % i)
            P.dma(lc[i][:], sg[kt * 128:(kt + 1) * 128, cs], writes=[C[1]], q="sp")
            yo = (y[:, kt, :], yk)
            if kt < 2:
                for hh in range(2):
                    h = 2 * kt + hh
                    ps_ = slice(64 * hh, 64 * hh + 64)
                    P.dma(la[i][ps_, :], oT[h, 0:64, cs], writes=[Ak[hh]])
                    P.dma(lb[i][ps_, :], oT[h, 64:65, cs].to_broadcast([64, 512]), writes=[Bk[hh]], q="sp")
                P.op("dve", lambda e, i=i: e.reciprocal(out=lb[i][:], in_=lb[i][:]), reads=Bk, writes=Bk)
                V.tt("dve", A, A, B, ALU.mult)
                V.tt("pool", yo, A, C, ALU.mult)
            elif kt == 2:
                pg, pgk = psg[c % 2], "P:psg%d" % (c % 2)
                for k2 in range(2):
                    P.op("pe", lambda e, k2=k2, pg=pg: e.matmul(pg[:], lhsT=Wg[:, k2, :], rhs=GBs[:, k2, :, jr],
                                                             start=(k2 == 0), stop=(k2 == 1)),
                         reads=["Wg"] + gbk, writes=[pgk])
                V.act(B, (pg[:], pgk), AF.Sigmoid)
                V.tt("dve", (la[i][:].rearrange("p (t j) -> p t j", t=8), Ak), (GYs[:, :, jr], gyk),
                     (lb[i][:].rearrange("p (t j) -> p t j", t=8), Bk), ALU.mult)
                V.tt("pool", (y[:, kt, :].rearrange("p (j t) -> p j t", t=8), yk),
                     (la[i][:].rearrange("p (t j) -> p j t", t=8), A[1]),
                     (lc[i][:].rearrange("p (j t) -> p j t", t=8), C[1]), ALU.mult)
            else:
                P.dma(la[i][:], yret[:, cs], writes=Ak)
                V.tt("pool", yo, A, C, ALU.mult)
        PL = P.io["part_len"]
        q = (c * 512) // PL
        lo = c * 512 - q * PL
        P.dma(ybuf[q][:, lo:lo + 512].rearrange("(k p) t -> p k t", p=128), y[:], reads=[yk], writes=["ybuf%d" % c], q="pool")
        ykeys.append("ybuf%d" % c)
        if lo + 512 == PL:
            P.collective("AllGather", [ybuf[q]], [P.io["yall"][q]], PAIR_GROUPS, reads=ykeys, writes=["yall%d" % q])
            ykeys = []
            yield


def emit_c2(P, L=SEQ, final=False):
    V = _V(P)
    yall = P.io["yall"]; wout = P.io["wout"]; x = P.io["x"]; xo = P.io["xo"]; fnw = P.io["fnw"]
    Wo = P.sb("o_Wo", [128, 8, 1024], BF16)
    wst = [P.sb("o_wst%d" % i, [128, 1024], F32) for i in range(2)]
    fn = P.sb("o_fn", [128, D_MODEL], F32)
    yT = [P.sb("o_yT%d" % i, [128, 8, 512], BF16) for i in range(2)]
    xt = [P.sb("o_xt%d" % i, [128, D_MODEL], F32) for i in range(2)]
    xn = [P.sb("o_xn%d" % i, [128, D_MODEL], F32) for i in range(2)]
    sqj = P.sb("o_sqj", [128, D_MODEL], F32)
    ss = [P.sb("o_ss%d" % i, [128, 1], F32) for i in range(2)]
    pso = [P.ps("o_pso%d" % i, [128, 512], F32) for i in range(4)]
    if final:
        P.dma(fn[:], fnw, writes=["fn"])
    for kt in range(8):
        w, wk = wst[kt % 2], "wst%d" % (kt % 2)
        P.dma(w[:], wout[kt * 128:(kt + 1) * 128, :], writes=[wk])
        P.copy("dve" if kt % 2 else "pool", Wo[:, kt, :], w[:], reads=[wk], writes=["Wo%d" % kt])
    yield
    for c in range(L // 512):
        cs = slice(c * 512, (c + 1) * 512)
        y = yT[c % 2]
        yk = "yT%d" % (c % 2)
        PL = P.io["part_len"]
        q = (c * 512) // PL
        lo = c * 512 - q * PL
        if lo == 0 and c > 0:
            yield
        P.dma(y[:], yall[q][:, lo:lo + 512].rearrange("(k p) t -> p k t", p=128), reads=["yall%d" % q], writes=[yk])
        for tt in range(4):
            n = c * 4 + tt
            xi = n % 2
            P.dma(xt[xi][:], x[n * 128:(n + 1) * 128, :], reads=["xrow%d" % n], writes=["xt%d" % xi])
            for ch in range(2):
                po = pso[(n * 2 + ch) % 4]
                pok = "P:pso%d" % ((n * 2 + ch) % 4)
                for kt in range(8):
                    P.op("pe", lambda e, kt=kt, po=po, ch=ch, tt=tt, y=y: e.matmul(
                        po[:], lhsT=y[:, kt, tt * 128:(tt + 1) * 128], rhs=Wo[:, kt, ch * 512:(ch + 1) * 512],
                        start=(kt == 0), stop=(kt == 7)), reads=[yk, "Wo%d" % kt], writes=[pok])
                V.tt("dve", (xn[xi][:, ch * 512:(ch + 1) * 512], "xn%d" % xi), (po[:], pok),
                     (xt[xi][:, ch * 512:(ch + 1) * 512], "xt%d" % xi), ALU.add)
            if final:
                P.op("act", lambda e, xi=xi: e.activation(out=sqj[:], in_=xn[xi][:], func=AF.Square, accum_out=ss[xi][:]),
                     reads=["xn%d" % xi], writes=["sqj", "ss%d" % xi])
                P.op("act", lambda e, xi=xi: e.activation(out=ss[xi][:], in_=ss[xi][:], func=AF.Sqrt, scale=1.0 / D_MODEL, bias=EPS),
                     reads=["ss%d" % xi], writes=["ss%d" % xi])
                P.op("dve", lambda e, xi=xi: e.reciprocal(out=ss[xi][:], in_=ss[xi][:]), reads=["ss%d" % xi], writes=["ss%d" % xi])
                V.stt((xn[xi][:], "xn%d" % xi), (xn[xi][:], "xn%d" % xi), ss[xi][:, 0:1], (fn[:], "fn"), ALU.mult, ALU.mult,
                      extra=["ss%d" % xi])
            P.dma(xo[n * 128:(n + 1) * 128, :], xn[xi][:], reads=["xn%d" % xi], writes=["xrow%d" % n], is_output=final, q="pool")


PAIR_GROUPS = [[0, 1], [2, 3], [4, 5], [6, 7]]


def build_fused(L=SEQ, depth=DEPTH):
    P = Prog()
    NJ = L // 8
    x = P.dram_in("x", [L, D_MODEL], F32)
    wA = P.dram_in("wA", [depth, D_MODEL, F_NCOL], F32)
    normw = P.dram_in("normw", [depth, 128, 8], F32)
    bf = P.dram_in("bf", [depth, 128, 1], F32)
    ident = P.dram_in("ident", [128, 128], F32)
    cosT = P.dram_in("cosT", [128, L], F32)
    sinT = P.dram_in("sinT", [128, L], F32)
    maskneg = P.dram_in("maskneg", [128, 128], F32)
    DT = P.dram_in("DT", [128, 256], F32)
    wqT = P.dram_in("wqT", [128, 128], F32)
    wk = P.dram_in("wk", [128, 2], F32)
    gblk = P.dram_in("gblk", [128, 1], F32)
    BD = P.dram_in("BD", [128, 128], F32)
    gnw = P.dram_in("gnw", [depth, 128, 1], F32)
    s5 = {}
    for nm, shp in (("are", [128, 8]), ("aim", [128, 8]), ("ldt", [128, 8]), ("bre", [128, 8, 16]), ("bim", [128, 8, 16]),
                    ("cre", [128, 8, 16]), ("cim", [128, 8, 16]), ("drep", [128, 16])):
        s5[nm] = P.dram_in(nm, [depth] + shp, F32)
    nvec = P.dram_in("nvec", [128, 24], F32)
    jvec = P.dram_in("jvec", [128, NJ], F32)
    mask01 = P.dram_in("mask01", [128, 128], F32)
    wglu = P.dram_in("wglu", [depth, 256, 128], F32)
    wout = P.dram_in("wout", [depth, D_MODEL, D_MODEL], F32)
    fnw = P.dram_in("fnw", [128, D_MODEL], F32)
    out = P.dram_out("out", [L, D_MODEL], F32)

    T = P.dram_tmp
    xbuf = T("xbuf", [L, D_MODEL], F32)
    sc = dict(qT=T("s_qT", [4, KROWS, L], BF16), kT=T("s_kT", [4, KROWS, L], BF16), Vfox=T("s_Vfox", [4, L, 64], BF16),
              Vtm=T("s_Vtm", [L, 128], BF16), Ktm=T("s_Ktm", [L, 128], BF16), cneg=T("s_cneg", [4, L], F32),
              rqT=T("s_rqT", [128, L], BF16), rkT=T("s_rkT", [128, L], BF16), suP=T("s_suP", [16, 128, NJ], F32),
              suPb=T("s_suPb", [16, 128, NJ], BF16), sgT=T("s_sgT", [512, L], F32), oT=T("s_oT", [4, 65, L], F32),
              yretT=T("s_yretT", [128, L], F32), GY=T("s_GY", [16, 128, NJ], F32), GYb=T("s_GYb", [16, 128, NJ], BF16),
              part_len=min(L, 1024),
              ybuf=[T("s_ybuf%d" % q, [512, min(L, 1024)], BF16) for q in range(max(1, L // 1024))],
              yall=[T("s_yall%d" % q, [1024, min(L, 1024)], BF16) for q in range(max(1, L // 1024))])
    for l in range(depth):
        xin = x if l == 0 else xbuf
        final = (l == depth - 1)
        io = dict(sc)
        io.update(x=xin, wA=wA[l], normw=normw[l], bf=bf[l], ident=ident, cosT=cosT, sinT=sinT)
        with P.scope("a%d_" % l, io):
            emit_phase_a(P, L)
        iofr = dict(qT=sc["qT"], kT=sc["kT"], V=sc["Vfox"], cnegTM=sc["cneg"], maskneg=maskneg, ident=ident, oT=sc["oT"],
                    rqT=sc["rqT"], rkT=sc["rkT"], Ktm=sc["Ktm"], Vtm=sc["Vtm"], DT=DT, wqT=wqT, wk=wk, gblk=gblk, BD=BD,
                    gnw=gnw[l], yretT=sc["yretT"])
        with P.scope("f%d_" % l, iofr):
            build_fox(L, 4, P=P, side=_ret_gen(L, 99, P=P, shared=True))
        io5 = {k: v[l] for k, v in s5.items()}
        io5.update(Ub=sc["suPb"], U32=sc["suP"], nvec=nvec, jvec=jvec, mask01=mask01, ident=ident, GY=sc["GY"], GYb=sc["GYb"])
        with P.scope("s%d_" % l, io5):
            build_s5(L, P=P)
        ioc = dict(oT=sc["oT"], yretT=sc["yretT"], GY=sc["GY"], GYb=sc["GYb"], sgT=sc["sgT"], wglu=wglu[l], ybuf=sc["ybuf"],
                   yall=sc["yall"], part_len=sc["part_len"], wout=wout[l], x=xin, xo=(out if final else xbuf), fnw=fnw)
        with P.scope("c%d_" % l, ioc):
            g1 = emit_c1(P, L)
            g2 = emit_c2(P, L, final)
            next(g2)
            first = True
            for _ in g1:
                if not first:
                    next(g2, None)
                first = False
            for _ in g2:
                pass
    return P.finish()


def _f_cols(r):
    fq, fk, fv, flog, su, rq, rk, rv, gate = 0, 512, 1024, 1536, 1544, 1800, 2056, 2312, 2568
    cols = []
    heads = [4 * r + i for i in range(4)]
    for h in heads:
        cols += list(range(fq + 64 * h, fq + 64 * h + 64))
    cols += list(range(fk + 64 * heads[0], fk + 64 * heads[0] + 64)) + list(range(flog + 4 * r, flog + 4 * r + 4))
    for h in heads[1:]:
        cols += list(range(fk + 64 * h, fk + 64 * h + 64))
    rh = [2 * r, 2 * r + 1]

    def plain(base):
        c = []
        for h in rh:
            c += list(range(base + 64 * h, base + 64 * h + 64))
        return c

    def swapped(base):
        c = []
        for h in rh:
            c += list(range(base + 64 * h + 32, base + 64 * h + 64)) + list(range(base + 64 * h, base + 64 * h + 32))
        return c

    cols += plain(rq) + swapped(rq) + plain(rk) + swapped(rk)
    cols += list(range(su + 128 * r, su + 128 * r + 128)) + list(range(su + 128 * (1 - r), su + 128 * (1 - r) + 128))
    cols += list(range(gate + 256 * r, gate + 256 * r + 256))
    cols += list(range(gate + 512 + 128 * r, gate + 512 + 128 * r + 128))
    cols += list(range(gate + 768 + 128 * r, gate + 768 + 128 * r + 128))
    cols += list(range(fv + 256 * r, fv + 256 * r + 256)) + plain(rv)
    assert len(cols) == F_NCOL
    return np.array(cols)


def _s5_perm(r):
    return np.array(list(range(8 * r, 8 * r + 8)) + list(range(8 * (1 - r), 8 * (1 - r) + 8)))


def _wout_rows():
    rows = []
    for r in range(2):
        rows += list(range(256 * r, 256 * r + 256)) + list(range(512 + 128 * r, 512 + 128 * r + 128)) \
            + list(range(768 + 128 * r, 768 + 128 * r + 128))
    return np.array(rows)


def _fused_inputs(inp, L, depth):
    f = lambda a: np.asarray(a, np.float32)
    ident = np.eye(128, dtype=np.float32)
    cosT, sinT = _rope_tables(L)
    maps = []
    wr = _wout_rows()
    wout_p = _c(f(inp["w_out"])[:depth][:, wr, :])
    fnw_t = _c(np.tile(f(inp["final_norm_w"])[None, :], (128, 1)))
    for b in range(BATCH):
        for r in range(2):
            m = dict(x=_c(f(inp["x"])[b, :L]), ident=ident, cosT=cosT, sinT=sinT, maskneg=_causal_maskneg(), fnw=fnw_t,
                     wout=wout_p)
            cols = _f_cols(r)
            m["wA"] = _c(f(inp["w_in"])[:depth][:, :, cols])
            m["normw"] = _c(f(inp["norm_w"])[:depth].reshape(depth, 8, 128).transpose(0, 2, 1))
            bfv = np.zeros((depth, 128, 1), np.float32)
            bfv[:, 64:68, 0] = f(inp["fox_b_f"])[:depth, 4 * r:4 * r + 4]
            m["bf"] = bfv
            DT, wq, wk, gb, BD = _ret_tables(r)
            m.update(DT=DT, wqT=wq, wk=wk, gblk=gb, BD=BD)
            m["gnw"] = _c(f(inp["ret_gn_w"])[:depth, 128 * r:128 * r + 128].reshape(depth, 128, 1))
            perm = _s5_perm(r)
            per = {k: [] for k in ("are", "aim", "ldt", "bre", "bim", "cre", "cim", "drep")}
            for l in range(depth):
                sub = {k: f(inp[k])[l][perm] for k in ("s5_a_re", "s5_a_im", "s5_b_re", "s5_b_im", "s5_c_re", "s5_c_im", "s5_log_dt")}
                sub["s5_d"] = f(inp["s5_d"])[l].reshape(16, 16)[perm].reshape(256)
                hi = _s5_host_inputs({k: v[None] for k, v in sub.items()}, 0, L)
                for k in per:
                    per[k].append(hi[k])
                m["nvec"], m["jvec"], m["mask01"] = hi["nvec"], hi["jvec"], hi["mask01"]
            for k in per:
                m[k] = _c(np.stack(per[k], 0))
            chan = (perm[:, None] * 16 + np.arange(16)[None, :]).reshape(256)
            m["wglu"] = _c(f(inp["s5_w_glu"])[:depth][:, chan, :][:, :, 128 * r:128 * r + 128])
            maps.append(m)
    return maps


_FUSED = {}


def kernel(x, norm_w, w_in, fox_b_f, s5_a_re, s5_a_im, s5_b_re, s5_b_im, s5_c_re, s5_c_im, s5_d,
           s5_log_dt, s5_w_glu, ret_gn_w, w_out, final_norm_w, _L=SEQ, _depth=DEPTH):
    inp = dict(x=x, norm_w=norm_w, w_in=w_in, fox_b_f=fox_b_f, s5_a_re=s5_a_re, s5_a_im=s5_a_im, s5_b_re=s5_b_re,
               s5_b_im=s5_b_im, s5_c_re=s5_c_re, s5_c_im=s5_c_im, s5_d=s5_d, s5_log_dt=s5_log_dt, s5_w_glu=s5_w_glu,
               ret_gn_w=ret_gn_w, w_out=w_out, final_norm_w=final_norm_w)
    key = (_L, _depth)
    if key not in _FUSED:
        _FUSED[key] = build_fused(_L, _depth)
    maps = _fused_inputs(inp, _L, _depth)
    res = run_bass_kernel_spmd(_FUSED[key], maps, core_ids=list(range(NCORES))).results
    out = np.zeros((BATCH, _L, D_MODEL), np.float32)
    half = _L // 2
    for b in range(BATCH):
        for r in range(2):
            out[b, half * r:half * (r + 1)] = np.asarray(res[2 * b + r]["out"])[half * r:half * (r + 1)]
    return out
```

```python
import math
import os
from contextlib import ExitStack

import numpy as np
import ml_dtypes

import concourse.bass as bass
import concourse.mybir as mybir
from concourse.bass_utils import run_bass_kernel_spmd

F32 = mybir.dt.float32
BF16 = mybir.dt.bfloat16
AF = mybir.ActivationFunctionType
ALU = mybir.AluOpType
AX = mybir.AxisListType

D_MODEL = 1024
BATCH = 4
SEQ = 4096
DEPTH = 4
HD = 64
EPS = 1e-6
NCORES = 8

_NP_BF16 = ml_dtypes.bfloat16


SAME_ENGINE_NOSYNC = tuple(os.environ.get("NOSYNC", "pe,sp").split(","))


class Prog:
    ENGS = ("pe", "act", "dve", "pool", "sp")

    def __init__(self, n_dma_sems=30):
        self.nc = bass.Bass("TRN2", target_bir_lowering=False)
        nc = self.nc
        self.es = ExitStack()
        self.eng = {"pe": nc.tensor, "act": nc.scalar, "dve": nc.vector,
                    "pool": nc.gpsimd, "sp": nc.sync}
        self.sem = {e: self.es.enter_context(nc.semaphore("c_" + e)) for e in self.ENGS}
        self.cnt = {e: 0 for e in self.ENGS}
        self.dsem = [self.es.enter_context(nc.semaphore("d%d" % i)) for i in range(n_dma_sems)]
        self.dcnt = [0] * n_dma_sems
        self.dnext = 0
        self.dnext_pool = 0
        self.known = {e: {} for e in self.ENGS}
        self.last_w = {}
        self.readers = {}
        self.out_tokens = []
        self.n_ins = 0
        self.es_cur = self.es
        self.prefix = ""
        self.io = {}
        self.ccsem = self.es.enter_context(nc.semaphore("c_cc"))
        self.cccnt = 0

    def scope(self, name, io=None):
        prog = self

        class _Scope:
            def __enter__(self_s):
                self_s.old = (prog.es_cur, prog.prefix, prog.io)
                prog.es_cur = ExitStack()
                prog.prefix = name
                prog.io = dict(io or {})
                return prog

            def __exit__(self_s, *exc):
                if exc[0] is None:
                    prog.barrier()
                prog.es_cur.close()
                prog.es_cur, prog.prefix, prog.io = self_s.old
                return False

        return _Scope()

    def barrier(self):
        toks = [(e, self.cnt[e]) for e in self.ENGS if self.cnt[e] > 0]
        toks += [(i, 16 * c) for i, c in enumerate(self.dcnt) if c > 0]
        if self.cccnt > 0:
            toks.append(("cc", self.cccnt))
        for e in self.ENGS:
            for t in toks:
                if t[0] != e:
                    self._wait(e, t)
        self.last_w = {}
        self.readers = {}

    def collective(self, kind, ins, outs, groups, reads=(), writes=()):
        self._deps("pool", list(reads), list(writes))
        ins_ = self.nc.gpsimd.collective_compute(kind, ALU.bypass, replica_groups=groups, ins=ins, outs=outs)
        self.cccnt += 1
        ins_.then_inc(self.ccsem, 1)
        tok = ("cc", self.cccnt)
        self._commit(tok, list(reads), list(writes))
        return tok

    def dram_tmp(self, name, shape, dt):
        return self.nc.dram_tensor(name, list(shape), dt, kind="Internal").ap()

    def sb(self, name, shape, dt):
        return self.es_cur.enter_context(self.nc.sbuf_tensor(self.prefix + name, list(shape), dt))

    def ps(self, name, shape, dt):
        return self.es_cur.enter_context(self.nc.psum_tensor(self.prefix + name, list(shape), dt))

    def dram_in(self, name, shape, dt):
        if name in self.io:
            return self.io[name]
        return self.nc.dram_tensor(name, list(shape), dt, kind="ExternalInput").ap()

    def dram_out(self, name, shape, dt):
        if name in self.io:
            return self.io[name]
        return self.nc.dram_tensor(name, list(shape), dt, kind="ExternalOutput").ap()

    def _semh(self, key):
        if key == "cc":
            return self.ccsem
        if isinstance(key, str):
            return self.sem[key]
        return self.dsem[key]

    def _wait(self, e, tok):
        key, val = tok
        if self.known[e].get(key, 0) >= val:
            return
        if key == e and e in SAME_ENGINE_NOSYNC:
            return
        self.eng[e].wait_ge(self._semh(key), val)
        self.known[e][key] = val

    def _deps(self, e, reads, writes):
        toks = []
        for k in reads:
            if k in self.last_w:
                toks.append(self.last_w[k])
        for k in writes:
            if k in self.last_w:
                toks.append(self.last_w[k])
            toks.extend(self.readers.get(k, ()))
        best = {}
        for key, val in toks:
            if best.get(key, 0) < val:
                best[key] = val
        for key, val in best.items():
            self._wait(e, (key, val))

    def _commit(self, tok, reads, writes):
        for k in reads:
            self.readers.setdefault(k, []).append(tok)
        for k in writes:
            self.last_w[k] = tok
            self.readers[k] = []

    @staticmethod
    def _excl(reads, writes):
        px = [k for k in reads if k.startswith("P:")]
        if px:
            writes = list(writes) + px
        return list(reads), list(writes)

    def op(self, e, fn, reads=(), writes=()):
        reads, writes = self._excl(reads, writes)
        self._deps(e, reads, writes)
        ins = fn(self.eng[e])
        self.cnt[e] += 1
        ins.then_inc(self.sem[e], 1)
        tok = (e, self.cnt[e])
        self._commit(tok, reads, writes)
        self.n_ins += 1
        return tok

    def copy(self, e, out, in_, reads=(), writes=()):
        if e == "act":
            return self.op(e, lambda g: g.copy(out=out, in_=in_), reads, writes)
        return self.op(e, lambda g: g.tensor_copy(out=out, in_=in_), reads, writes)

    def dma(self, out, in_, reads=(), writes=(), q="sp", is_output=False, **kw):
        self._deps(q, reads, writes)
        nsp = (2 * len(self.dsem)) // 3
        if q == "pool":
            i = nsp + self.dnext_pool
            self.dnext_pool = (self.dnext_pool + 1) % (len(self.dsem) - nsp)
        else:
            i = self.dnext
            self.dnext = (self.dnext + 1) % nsp
        if self.dcnt[i] > 0:
            self._wait(q, (i, 16 * self.dcnt[i]))
        ins = self.eng[q].dma_start(out=out, in_=in_, **kw)
        self.dcnt[i] += 1
        ins.then_inc(self.dsem[i], 16)
        tok = (i, 16 * self.dcnt[i])
        self._commit(tok, reads, writes)
        if is_output:
            self.out_tokens.append(tok)
        self.n_ins += 1
        return tok

    def finish(self):
        best = {}
        for key, val in self.out_tokens:
            if best.get(key, 0) < val:
                best[key] = val
        for key, val in best.items():
            self._wait("sp", (key, val))
        self.es.close()
        return self.nc


def _rr(lst, state, name):
    i = state.get(name, 0)
    state[name] = i + 1
    return lst[i % len(lst)]


A_GROUPS = ([("q", h, 64) for h in range(4)] + [("k", 0, 68)] + [("k", h, 64) for h in (1, 2, 3)]
            + [("v", 0, 128), ("v", 1, 128), ("rq", 0, 128), ("rqs", 0, 128), ("rk", 0, 128),
               ("rks", 0, 128), ("rv", 0, 128), ("su", 0, 128), ("su", 1, 128)]
            + [("g", i, 128) for i in range(4)])
A_NCOL = sum(g[2] for g in A_GROUPS)


def build_phase_a(L=SEQ, stop=99, ngrp=99):
    P = Prog()
    nc = P.nc
    NB = L // 512
    x = P.dram_in("x", [L, D_MODEL], F32)
    wA = P.dram_in("wA", [D_MODEL, A_NCOL], F32)
    normw = P.dram_in("normw", [128, 8], F32)
    bf = P.dram_in("bf", [128, 1], F32)
    ident_in = P.dram_in("ident", [128, 128], F32)
    cosT = P.dram_in("cosT", [128, L], F32)
    sinT = P.dram_in("sinT", [128, L], F32)
    qT = P.dram_out("qT", [4, 65, L], BF16)
    kT = P.dram_out("kT", [4, 65, L], BF16)
    vT = P.dram_out("vT", [256, L], BF16)
    cneg_o = P.dram_out("cneg", [4, L], F32)
    rqT = P.dram_out("rqT", [128, L], BF16)
    rkT = P.dram_out("rkT", [128, L], BF16)
    rvT = P.dram_out("rvT", [128, L], BF16)
    suT = P.dram_out("suT", [256, L], F32)
    suTb = P.dram_out("suTb", [256, L], BF16)
    sgT = P.dram_out("sgT", [512, L], F32)

    Wb = P.sb("Wb", [128, 8, A_NCOL], BF16)
    wst = [P.sb("wst%d" % i, [128, A_NCOL], F32) for i in range(2)]
    gsb = P.sb("gsb", [128, 8], F32)
    ident_f = P.sb("ident_f", [128, 128], F32)
    ident = P.sb("ident_b", [128, 128], BF16)
    bsb = P.sb("bsb", [128, 1], F32)
    negb = P.sb("negb", [128, 1], F32)
    xt = [P.sb("xt%d" % i, [128, D_MODEL], F32) for i in range(3)]
    sqj = P.sb("sqj", [128, D_MODEL], F32)
    ss = [P.sb("ss%d" % i, [128, 1], F32) for i in range(3)]
    sd = [P.sb("sd%d" % i, [128, 1], F32) for i in range(3)]
    rs = [P.sb("rs%d" % i, [128, 1], F32) for i in range(3)]
    hb = [P.sb("hb%d" % i, [128, D_MODEL], BF16) for i in range(2)]
    hT = [P.sb("hT%d" % i, [128, 8, 512], BF16) for i in range(2)]
    cosb = [P.sb("cosb%d" % i, [128, 512], F32) for i in range(2)]
    sinb = [P.sb("sinb%d" % i, [128, 512], F32) for i in range(2)]
    ob16 = [P.sb("ob16_%d" % i, [128, 512], BF16) for i in range(6)]
    of32 = [P.sb("of32_%d" % i, [128, 512], F32) for i in range(6)]
    flog = P.sb("flog", [128, L], F32)
    ework = P.sb("ework", [128, L], F32)
    onesf = P.sb("onesf", [128, L], F32)
    chat = P.sb("chat", [128, L], BF16)
    onesb = P.sb("onesb", [128, L], BF16)
    ptr = [P.ps("ptr%d" % i, [128, 8 * 128], BF16) for i in range(2)]
    pp = [P.ps("pp%d" % i, [128, 512], F32) for i in range(6)]
    st = {}

    P.dma(gsb[:], normw, writes=["gsb"])
    P.dma(ident_f[:], ident_in, writes=["ident_f"])
    P.dma(bsb[:], bf, writes=["bsb"])
    P.op("dve", lambda e: e.tensor_copy(out=ident[:], in_=ident_f[:]), reads=["ident_f"], writes=["ident"])
    P.op("dve", lambda e: e.tensor_scalar_mul(out=negb[:], in0=bsb[:], scalar1=-1.0),
         reads=["bsb"], writes=["negb"])
    P.op("pool", lambda e: e.memset(onesf[64:68, :], 1.0), writes=["onesf"])
    P.op("pool", lambda e: e.memset(onesb[64:68, :], 1.0), writes=["onesb"])
    for kt in range(8):
        w = wst[kt % 2]
        wk = "wst%d" % (kt % 2)
        P.dma(w[:], wA[kt * 128:(kt + 1) * 128, :], writes=[wk])
        P.op("dve", lambda e, w=w, kt=kt: e.tensor_scalar_mul(out=Wb[:, kt, :], in0=w[:], scalar1=gsb[:, kt:kt + 1]),
             reads=[wk, "gsb"], writes=["Wb%d" % kt])
    wkeys = ["Wb%d" % kt for kt in range(8)]
    if stop <= 0:
        return P.finish()

    for blk in range(NB):
        hTb = hT[blk % 2]
        hk = "hT%d" % (blk % 2)
        c0 = blk * 512
        cb, sbn = cosb[blk % 2], sinb[blk % 2]
        ck, sk = "cosb%d" % (blk % 2), "sinb%d" % (blk % 2)
        P.dma(cb[:], cosT[:, c0:c0 + 512], writes=[ck])
        P.dma(sbn[:], sinT[:, c0:c0 + 512], writes=[sk])
        for ti in range(4):
            n = blk * 4 + ti
            xi = n % 3
            P.dma(xt[xi][:], x[n * 128:(n + 1) * 128, :], writes=["xt%d" % xi])
            P.op("act", lambda e, xi=xi: e.activation(out=sqj[:], in_=xt[xi][:], func=AF.Square, accum_out=ss[xi][:]),
                 reads=["xt%d" % xi], writes=["sqj", "ss%d" % xi])
            P.op("act", lambda e, xi=xi: e.activation(out=sd[xi][:], in_=ss[xi][:], func=AF.Sqrt,
                                                        scale=1.0 / D_MODEL, bias=EPS),
                 reads=["ss%d" % xi], writes=["sd%d" % xi])
            P.op("dve", lambda e, xi=xi: e.reciprocal(out=rs[xi][:], in_=sd[xi][:]),
                 reads=["sd%d" % xi], writes=["rs%d" % xi])
            hi = n % 2
            P.op("dve", lambda e, xi=xi, hi=hi: e.tensor_scalar_mul(out=hb[hi][:], in0=xt[xi][:], scalar1=rs[xi][:, 0:1]),
                 reads=["xt%d" % xi, "rs%d" % xi], writes=["hb%d" % hi])
            pt = ptr[n % 2]
            ptk = "P:ptr%d" % (n % 2)
            for kt in range(8):
                P.op("pe", lambda e, kt=kt, pt=pt, hi=hi: e.transpose(out=pt[:, kt * 128:(kt + 1) * 128],
                                                                    in_=hb[hi][:, kt * 128:(kt + 1) * 128],
                                                                    identity=ident[:]),
                     reads=["hb%d" % hi, "ident"], writes=[ptk])
            P.copy("act" if ti % 2 else "dve", hTb[:, :, ti * 128:(ti + 1) * 128],
                   pt[:].rearrange("p (k t) -> p k t", k=8), reads=[ptk], writes=[hk + "_%d" % ti])
        hkeys = [hk + "_%d" % ti for ti in range(4)]
        if stop <= 1:
            continue

        col = 0
        pend = {}
        for gi, (kind, idx, M) in enumerate(A_GROUPS):
            if stop == 2 and gi >= ngrp:
                col += M
                continue
            pi = st.get("pp", 0) % 6
            st["pp"] = pi + 1
            ps_t = pp[pi]
            pk = "P:pp%d" % pi
            for kt in range(8):
                P.op("pe", lambda e, kt=kt, col=col, M=M, ps_t=ps_t: e.matmul(
                    ps_t[0:M, :], lhsT=Wb[:, kt, col:col + M], rhs=hTb[:, kt, :],
                    start=(kt == 0), stop=(kt == 7)),
                    reads=hkeys + [wkeys[kt]], writes=[pk])
            col += M
            oi = st.get("ob", 0) % 6
            st["ob"] = oi + 1
            o16, o32 = ob16[oi], of32[oi]
            k16, k32 = "ob16_%d" % oi, "of32_%d" % oi
            if kind == "q":
                P.op("act", lambda e, ps_t=ps_t, o16=o16: e.mul(out=o16[0:64, :], in_=ps_t[0:64, :], mul=0.125),
                     reads=[pk], writes=[k16])
                P.dma(qT[idx, 0:64, c0:c0 + 512], o16[0:64, :], reads=[k16], is_output=True)
            elif kind == "k":
                P.op("dve", lambda e, ps_t=ps_t, o16=o16: e.tensor_copy(out=o16[0:64, :], in_=ps_t[0:64, :]),
                     reads=[pk], writes=[k16])
                P.dma(kT[idx, 0:64, c0:c0 + 512], o16[0:64, :], reads=[k16], is_output=True)
                if idx == 0:
                    P.op("act", lambda e, ps_t=ps_t: e.copy(out=flog[64:68, c0:c0 + 512], in_=ps_t[64:68, :]),
                         reads=[pk], writes=["flog%d" % blk])
            elif kind in ("v", "rv"):
                P.op("dve", lambda e, ps_t=ps_t, o16=o16: e.tensor_copy(out=o16[:], in_=ps_t[:]),
                     reads=[pk], writes=[k16])
                dst = vT[idx * 128:(idx + 1) * 128, c0:c0 + 512] if kind == "v" else rvT[:, c0:c0 + 512]
                P.dma(dst, o16[:], reads=[k16], is_output=True)
            elif kind in ("rq", "rk"):
                P.op("dve", lambda e, ps_t=ps_t, o32=o32: e.tensor_tensor(out=o32[:], in0=ps_t[:], in1=cb[:], op=ALU.mult),
                     reads=[pk, ck], writes=[k32])
                pend[kind] = (o32, k32)
            elif kind in ("rqs", "rks"):
                base = kind[:2]
                t1, t1k = pend[base]
                P.op("dve", lambda e, ps_t=ps_t, o32=o32: e.tensor_tensor(out=o32[:], in0=ps_t[:], in1=sbn[:], op=ALU.mult),
                     reads=[pk, sk], writes=[k32])
                P.op("pool", lambda e, t1=t1, o32=o32, o16=o16: e.tensor_tensor(out=o16[:], in0=t1[:], in1=o32[:], op=ALU.add),
                     reads=[t1k, k32], writes=[k16])
                dst = rqT if base == "rq" else rkT
                P.dma(dst[:, c0:c0 + 512], o16[:], reads=[k16], is_output=True)
            elif kind == "su":
                P.op("act", lambda e, ps_t=ps_t, o32=o32: e.copy(out=o32[:], in_=ps_t[:]), reads=[pk], writes=[k32])
                P.dma(suT[idx * 128:(idx + 1) * 128, c0:c0 + 512], o32[:], reads=[k32], is_output=True)
                P.op("dve", lambda e, o32=o32, o16=o16: e.tensor_copy(out=o16[:], in_=o32[:]), reads=[k32], writes=[k16])
                P.dma(suTb[idx * 128:(idx + 1) * 128, c0:c0 + 512], o16[:], reads=[k16], is_output=True)
            elif kind == "g":
                P.op("act", lambda e, ps_t=ps_t, o32=o32: e.activation(out=o32[:], in_=ps_t[:], func=AF.Silu),
                     reads=[pk], writes=[k32])
                P.dma(sgT[idx * 128:(idx + 1) * 128, c0:c0 + 512], o32[:], reads=[k32], is_output=True)

    if stop <= 2:
        return P.finish()
    fk = ["flog%d" % b for b in range(NB)]
    s4 = slice(64, 68)
    P.op("act", lambda e: e.activation(out=ework[s4, :], in_=flog[s4, :], func=AF.Exp, scale=-1.0, bias=negb[s4, 0:1]),
         reads=fk + ["negb"], writes=["ework"])
    P.op("act", lambda e: e.activation(out=flog[s4, :], in_=ework[s4, :], func=AF.Ln, scale=1.0, bias=1.0),
         reads=["ework"], writes=["sp"])
    P.op("dve", lambda e: e.tensor_tensor_scan(out=ework[s4, :], data0=onesf[s4, :], data1=flog[s4, :], initial=0.0,
                                               op0=ALU.mult, op1=ALU.add),
         reads=["sp", "onesf"], writes=["cneg"])
    P.op("dve", lambda e: e.tensor_scalar_mul(out=chat[s4, :], in0=ework[s4, :], scalar1=-1.0),
         reads=["cneg"], writes=["chat"])
    P.dma(cneg_o, ework[s4, :], reads=["cneg"], is_output=True)
    P.dma(qT[:, 64, :], chat[s4, :], reads=["chat"], is_output=True)
    P.dma(kT[:, 64, :], onesb[s4, :], reads=["onesb"], is_output=True)
    return P.finish()


def _a_cols(r):
    fq, fk, fv, flog, su, rq, rk, rv, gate = 0, 512, 1024, 1536, 1544, 1800, 2056, 2312, 2568
    cols = []
    heads = [4 * r + i for i in range(4)]
    for h in heads:
        cols += list(range(fq + 64 * h, fq + 64 * h + 64))
    cols += list(range(fk + 64 * heads[0], fk + 64 * heads[0] + 64)) + list(range(flog + 4 * r, flog + 4 * r + 4))
    for h in heads[1:]:
        cols += list(range(fk + 64 * h, fk + 64 * h + 64))
    cols += list(range(fv + 256 * r, fv + 256 * r + 256))
    rh = [2 * r, 2 * r + 1]

    def plain(base):
        c = []
        for h in rh:
            c += list(range(base + 64 * h, base + 64 * h + 64))
        return c

    def swapped(base):
        c = []
        for h in rh:
            c += list(range(base + 64 * h + 32, base + 64 * h + 64)) + list(range(base + 64 * h, base + 64 * h + 32))
        return c

    cols += plain(rq) + swapped(rq) + plain(rk) + swapped(rk) + plain(rv)
    cols += list(range(su, su + 256))
    cols += list(range(gate + 256 * r, gate + 256 * r + 256))
    cols += list(range(gate + 512 + 128 * r, gate + 512 + 128 * r + 128))
    cols += list(range(gate + 768 + 128 * r, gate + 768 + 128 * r + 128))
    assert len(cols) == A_NCOL
    return np.array(cols)


def _rope_tables(L):
    half = 32
    freqs = (10000.0 ** (-np.arange(half, dtype=np.float32) / half)).astype(np.float32)
    ang = np.arange(L, dtype=np.float32)[None, :] * freqs[:, None]
    cos = np.cos(ang).astype(np.float32)
    sin = np.sin(ang).astype(np.float32)
    cosT = np.concatenate([cos, cos, cos, cos], axis=0)
    sinT = np.concatenate([-sin, sin, -sin, sin], axis=0)
    return np.ascontiguousarray(cosT), np.ascontiguousarray(sinT)


KROWS = 68


def build_fox(L=SEQ, NH=4, P=None, side=None):
    own = P is None
    P = P or Prog()
    NKT = L // 128
    NQB = L // 512
    qT = P.dram_in("qT", [NH, KROWS, L], BF16)
    kT = P.dram_in("kT", [NH, KROWS, L], BF16)
    V = P.dram_in("V", [NH, L, 64], BF16)
    cn = P.dram_in("cnegTM", [NH, 128, NKT], F32)
    mask_in = P.dram_in("maskneg", [128, 128], F32)
    ident_in = P.dram_in("ident", [128, 128], F32)
    oT = P.dram_out("oT", [NH, 65, L], F32)

    qh = [P.sb("qh%d" % i, [KROWS, L], BF16) for i in range(2)]
    kh = [P.sb("kh%d" % i, [KROWS, L], BF16) for i in range(2)]
    va = [P.sb("va%d" % i, [128, NKT, 65], BF16) for i in range(2)]
    cnh = [P.sb("cnh%d" % i, [128, NKT], F32) for i in range(2)]
    mf = P.sb("mf", [128, 128], F32)
    mb = P.sb("mb", [128, 128], BF16)
    idf = P.sb("idf", [128, 128], F32)
    idb = P.sb("idb", [128, 128], BF16)
    pt = [P.sb("pt%d" % i, [128, 512], BF16) for i in range(4)]
    ob = [P.sb("ob%d" % i, [65, 512], F32) for i in range(2)]
    NS = 3 if side is not None else 4
    sps = [P.ps("sps%d" % i, [128, 512], F32) for i in range(NS)]
    ops = [P.ps("ops%d" % i, [128, 512], F32) for i in range(2)]

    P.dma(mf[:], mask_in, writes=["mf"])
    P.dma(idf[:], ident_in, writes=["idf"])
    P.op("dve", lambda e: e.tensor_copy(out=mb[:], in_=mf[:]), reads=["mf"], writes=["mb"])
    P.op("dve", lambda e: e.tensor_copy(out=idb[:], in_=idf[:]), reads=["idf"], writes=["idb"])
    for i in range(2):
        P.op("pool", lambda e, i=i: e.memset(va[i][:, :, 64:65], 1.0), writes=["va1_%d" % i])

    tiles = []
    for h in range(NH):
        for qb in range(NQB):
            nk = 4 * (qb + 1)
            for kt in range(nk):
                tiles.append((h, qb, kt, nk))

    loaded = set()

    def load_head(h):
        if h in loaded or h >= NH:
            return
        loaded.add(h)
        b = h % 2
        P.dma(qh[b][:], qT[h], writes=["qh%d" % b])
        P.dma(kh[b][:], kT[h], writes=["kh%d" % b])
        P.dma(va[b][:, :, 0:64], V[h].rearrange("(k p) d -> p k d", p=128), writes=["va%d" % b])
        if KROWS == 65:
            P.dma(cnh[b][:], cn[h], writes=["cnh%d" % b])

    def emit_s(i):
        h, qb, kt, nk = tiles[i]
        b = h % 2
        load_head(h)
        jj = kt - 4 * qb
        lo = 128 * jj if jj > 0 else 0
        s = sps[i % NS]
        sk = "P:sps%d" % (i % NS)
        q0 = qb * 512
        P.op("pe", lambda e: e.matmul(s[:, lo:512], lhsT=kh[b][0:KROWS, kt * 128:(kt + 1) * 128],
                                      rhs=qh[b][0:KROWS, q0 + lo:q0 + 512], start=True, stop=(jj < 0)),
             reads=["kh%d" % b, "qh%d" % b], writes=[sk])
        if jj >= 0:
            P.op("pe", lambda e: e.matmul(s[:, lo:lo + 128], lhsT=idb[:], rhs=mb[:], start=False, stop=True),
                 reads=["idb", "mb"], writes=[sk])

    def emit_rest(i):
        h, qb, kt, nk = tiles[i]
        b = h % 2
        jj = kt - 4 * qb
        lo = 128 * jj if jj > 0 else 0
        s = sps[i % NS]
        sk = "P:sps%d" % (i % NS)
        p = pt[i % 4]
        pk = "pt%d" % (i % 4)
        o = ops[(h * NQB + qb) % 2]
        ok = "P:ops%d" % ((h * NQB + qb) % 2)
        if KROWS == 65:
            P.op("act", lambda e: e.activation(out=p[:, lo:512], in_=s[:, lo:512], func=AF.Exp,
                                               bias=cnh[b][:, kt:kt + 1], scale=1.0),
                 reads=[sk, "cnh%d" % b], writes=[pk])
        else:
            P.op("act", lambda e: e.activation(out=p[:, lo:512], in_=s[:, lo:512], func=AF.Exp),
                 reads=[sk], writes=[pk])
        P.op("pe", lambda e: e.matmul(o[0:65, lo:512], lhsT=va[b][:, kt, 0:65], rhs=p[:, lo:512],
                                      start=(kt == 0), stop=(kt == nk - 1)),
             reads=[pk, "va%d" % b, "va1_%d" % b], writes=[ok])
        if kt == nk - 1:
            oi = (h * NQB + qb) % 2
            P.op("dve", lambda e: e.tensor_copy(out=ob[oi][:], in_=o[0:65, :]), reads=[ok], writes=["ob%d" % oi])
            P.dma(oT[h, :, qb * 512:(qb + 1) * 512], ob[oi][:], reads=["ob%d" % oi], is_output=True, q="pool")
            if qb == NQB - 1:
                load_head(h + 2)

    LA = 2
    load_head(0)
    load_head(1)
    n = len(tiles)
    for i in range(min(LA, n)):
        emit_s(i)
    step = 3
    for i in range(n):
        if i + LA < n:
            emit_s(i + LA)
        emit_rest(i)
        if side is not None and i % step == step - 1:
            next(side, None)
    if side is not None:
        for _ in side:
            pass
    return P.finish() if own else None


def _causal_maskneg():
    s = np.arange(128)[:, None]
    t = np.arange(128)[None, :]
    return np.where(s <= t, 0.0, -30000.0).astype(np.float32)


def _ret_gen(L=SEQ, stop=99, P=None, shared=False):
    own = P is None
    P = P or Prog()
    NBLK = L // 128
    rqT = P.dram_in("rqT", [128, L], BF16)
    rkT = P.dram_in("rkT", [128, L], BF16)
    Ktm = P.dram_in("Ktm", [L, 128], BF16)
    Vtm = P.dram_in("Vtm", [L, 128], BF16)
    DT_in = P.dram_in("DT", [128, 256], F32)
    wq_in = P.dram_in("wqT", [128, 128], F32)
    wk_in = P.dram_in("wk", [128, 2], F32)
    gb_in = P.dram_in("gblk", [128, 1], F32)
    BD_in = P.dram_in("BD", [128, 128], F32)
    gn_in = P.dram_in("gnw", [128, 1], F32)
    yT = P.dram_out("yretT", [128, L], F32)

    rq = P.sb("rq", [128, L], BF16)
    rkp = [P.sb("rkp%d" % i, [128, L], BF16) for i in range(2)]
    ks = P.sb("ks", [128, NBLK, 128], BF16)
    vs = P.sb("vs", [128, NBLK, 128], BF16)
    vpad = P.sb("vpad", [128, NBLK, 2, 128], BF16)
    kpad = P.sb("kpad", [128, NBLK, 2, 128], BF16)
    DT = P.sb("DTs", [128, 256], F32)
    wq = P.sb("wqs", [128, 128], F32)
    wk = P.sb("wks", [128, 2], F32)
    gb = P.sb("gbs", [128, 1], F32)
    BD = P.sb("BDs", [128, 128], F32)
    gn = P.sb("gns", [128, 1], F32)
    S = P.sb("S", [128, 64], F32)
    Sbf = [P.sb("Sbf%d" % i, [128, 128], BF16) for i in range(2)]
    AD = [P.sb("AD%d" % i, [128, 256], BF16) for i in range(2)]
    qp = [P.sb("qp%d" % i, [128, 128], BF16) for i in range(2)]
    oT = P.sb("oT", [128, L], F32)
    sq = [P.sb("sq%d" % i, [128, 512], F32) for i in range(2)]
    mS = [P.sb("mS%d" % i, [128, 512], F32) for i in range(2)]
    t1 = [P.sb("t1_%d" % i, [128, 512], F32) for i in range(2)]
    t2 = [P.sb("t2_%d" % i, [128, 512], F32) for i in range(2)]
    yo = [P.sb("yo%d" % i, [128, 512], F32) for i in range(2)]
    nb_ = 1 if shared else 2
    psA = [P.ps("psA%d" % i, [128, 512], F32) for i in range(nb_)] * (2 // nb_)
    psO = [P.ps("psO%d" % i, [128, 512], F32) for i in range(nb_)] * (2 // nb_)
    psU = [P.ps("psU%d" % i, [128, 512], F32) for i in range(nb_)] * (2 // nb_)
    if shared:
        psM, psQ = psA[0], psO[0]
        psMk, psQk = "P:psA0", "P:psO0"
    else:
        psM = P.ps("psM", [128, 512], F32)
        psQ = P.ps("psQ", [128, 512], F32)
        psMk, psQk = "P:psM", "P:psQ"
    cpe = "dve" if shared else "act"

    P.dma(rq[:], rqT, writes=["rq"])
    for h in range(2):
        hs = slice(64 * h, 64 * h + 64)
        zs = slice(64 * (1 - h), 64 * (1 - h) + 64)
        P.op("pool", lambda e, h=h, zs=zs: e.memset(rkp[h][zs, :], 0.0), writes=["rkz%d" % h])
        P.dma(rkp[h][hs, :], rkT[hs, :], writes=["rk%d" % h])
    P.dma(ks[:], Ktm.rearrange("(b p) d -> p b d", p=128), writes=["ks"])
    P.dma(vs[:], Vtm.rearrange("(b p) d -> p b d", p=128), writes=["vs"])
    for t, src, k in ((DT, DT_in, "DT"), (wq, wq_in, "wq"), (wk, wk_in, "wk"), (gb, gb_in, "gb"),
                      (BD, BD_in, "BD"), (gn, gn_in, "gn")):
        P.dma(t[:], src, writes=[k])
    P.op("pool", lambda e: e.memset(vpad[:], 0.0), writes=["vpad"])
    P.op("pool", lambda e: e.memset(kpad[:], 0.0), writes=["kpad"])
    P.op("pool", lambda e: e.memset(S[:], 0.0), writes=["S"])
    for i in range(2):
        P.op("pool", lambda e, i=i: e.memset(Sbf[i][:], 0.0), writes=["Sbf%d" % i])
    for h in range(2):
        hs = slice(64 * h, 64 * h + 64)
        P.op("dve", lambda e, h=h, hs=hs: e.tensor_copy(out=vpad[:, :, h, hs], in_=vs[:, :, hs]),
             reads=["vs", "vpad"], writes=["vpad"])
        P.op("dve", lambda e, h=h, hs=hs: e.tensor_scalar_mul(out=kpad[:, :, h, hs], in0=ks[:, :, hs], scalar1=wk[:, h:h + 1]),
             reads=["ks", "wk", "kpad"], writes=["kpad"])

    if stop <= 0:
        P.dma(yT[:, 0:128], DT[:, 0:128], reads=["DT"], is_output=True)
        return
    for blk in range(NBLK):
        c0 = blk * 128
        cs = slice(c0, c0 + 128)
        a = psA[blk % 2]
        ak = "P:psA%d" % (blk % nb_)
        for h in range(2):
            hs = slice(64 * h, 64 * h + 64)
            P.op("pe", lambda e, h=h, hs=hs: e.matmul(a[:, h * 128:(h + 1) * 128], lhsT=rkp[h][:, cs], rhs=rq[:, cs],
                                                     start=True, stop=True),
                 reads=["rk%d" % h, "rkz%d" % h, "rq"], writes=[ak])
        yield
        ad = AD[blk % 2]
        adk = "AD%d" % (blk % 2)
        P.op("dve", lambda e: e.tensor_tensor(out=ad[:], in0=a[:, 0:256], in1=DT[:], op=ALU.mult),
             reads=[ak, "DT"], writes=[adk])
        if stop <= 1:
            continue
        qpb = qp[blk % 2]
        qpk = "qp%d" % (blk % 2)
        P.op("pool", lambda e: e.tensor_tensor(out=qpb[:], in0=rq[:, cs], in1=wq[:], op=ALU.mult),
             reads=["rq", "wq"], writes=[qpk])
        yield
        o = psO[blk % 2]
        ok = "P:psO%d" % (blk % nb_)
        cur, nxt = Sbf[blk % 2], Sbf[(blk + 1) % 2]
        curk, nxtk = "Sbf%d" % (blk % 2), "Sbf%d" % ((blk + 1) % 2)
        P.op("pe", lambda e: e.matmul(o[:, 0:128], lhsT=vpad[:, blk, 0, :], rhs=ad[:, 0:128], start=True, stop=False),
             reads=["vpad", adk], writes=[ok])
        P.op("pe", lambda e: e.matmul(o[:, 0:128], lhsT=vpad[:, blk, 1, :], rhs=ad[:, 128:256], start=False, stop=False),
             reads=["vpad", adk], writes=[ok])
        P.op("pe", lambda e: e.matmul(o[:, 0:128], lhsT=cur[:], rhs=qpb[:], start=False, stop=True),
             reads=[curk, qpk], writes=[ok])
        if blk < NBLK - 1 and stop > 2:
            u = psU[blk % 2]
            uk = "P:psU%d" % (blk % nb_)
            P.op("pe", lambda e: e.matmul(u[:, 0:64], lhsT=kpad[:, blk, 0, :], rhs=vs[:, blk, 0:64], start=True, stop=False),
                 reads=["kpad", "vs"], writes=[uk])
            P.op("pe", lambda e: e.matmul(u[:, 0:64], lhsT=kpad[:, blk, 1, :], rhs=vs[:, blk, 64:128], start=False, stop=True),
                 reads=["kpad", "vs"], writes=[uk])
        yield
        P.copy(cpe, oT[:, cs], o[:, 0:128], reads=[ok], writes=["oT%d" % (blk // 4)])
        if blk < NBLK - 1 and stop > 2:
            u = psU[blk % 2]
            uk = "P:psU%d" % (blk % nb_)
            P.op("dve", lambda e: e.scalar_tensor_tensor(out=S[:], in0=S[:], scalar=gb[:, 0:1], in1=u[:, 0:64],
                                                         op0=ALU.mult, op1=ALU.add),
                 reads=["S", "gb", uk], writes=["S"])
            P.copy("pool" if shared else "act", nxt[0:64, 0:64], S[0:64, :], reads=["S"], writes=[nxtk])
            P.op("pool", lambda e: e.tensor_copy(out=nxt[64:128, 64:128], in_=S[64:128, :]), reads=["S"], writes=[nxtk])
        yield

    if stop <= 3:
        P.dma(yT[:, 0:128], oT[:, 0:128], reads=["oT0"], is_output=True)
        return P.finish() if own else None
    for c in range(L // 512):
        cs = slice(c * 512, (c + 1) * 512)
        i = c % 2
        ok = ["oT%d" % c]
        if shared:
            P.op("pool", lambda e: e.tensor_tensor(out=sq[i][:], in0=oT[:, cs], in1=oT[:, cs], op=ALU.mult), reads=ok, writes=["sq%d" % i])
        else:
            P.op("act", lambda e: e.activation(out=sq[i][:], in_=oT[:, cs], func=AF.Square), reads=ok, writes=["sq%d" % i])
        yield
        P.op("pe", lambda e: e.matmul(psM[:], lhsT=BD[:], rhs=oT[:, cs], start=True, stop=True),
             reads=ok + ["BD"], writes=[psMk])
        P.op("pe", lambda e: e.matmul(psQ[:], lhsT=BD[:], rhs=sq[i][:], start=True, stop=True),
             reads=["sq%d" % i, "BD"], writes=[psQk])
        yield
        P.copy(cpe, mS[i][:], psM[:], reads=[psMk], writes=["mS%d" % i])
        P.op("dve", lambda e: e.tensor_tensor(out=t1[i][:], in0=mS[i][:], in1=mS[i][:], op=ALU.mult),
             reads=["mS%d" % i], writes=["t1_%d" % i])
        P.op("dve", lambda e: e.tensor_tensor(out=t1[i][:], in0=psQ[:], in1=t1[i][:], op=ALU.subtract),
             reads=[psQk, "t1_%d" % i], writes=["t1_%d" % i])
        P.op("act", lambda e: e.activation(out=t1[i][:], in_=t1[i][:], func=AF.Sqrt, scale=1.0, bias=EPS),
             reads=["t1_%d" % i], writes=["t1_%d" % i])
        P.op("dve", lambda e: e.reciprocal(out=t1[i][:], in_=t1[i][:]), reads=["t1_%d" % i], writes=["t1_%d" % i])
        P.op("pool", lambda e: e.tensor_tensor(out=t2[i][:], in0=oT[:, cs], in1=mS[i][:], op=ALU.subtract),
             reads=ok + ["mS%d" % i], writes=["t2_%d" % i])
        P.op("dve", lambda e: e.scalar_tensor_tensor(out=yo[i][:], in0=t2[i][:], scalar=gn[:, 0:1], in1=t1[i][:],
                                                     op0=ALU.mult, op1=ALU.mult),
             reads=["t2_%d" % i, "t1_%d" % i, "gn"], writes=["yo%d" % i])
        P.dma(yT[:, cs], yo[i][:], reads=["yo%d" % i], is_output=True, q="pool")
        yield
    if own:
        P.nc_done = P.finish()


def build_ret(L=SEQ, stop=99, P=None):
    own = P is None
    P = P or Prog()
    for _ in _ret_gen(L, stop, P=P):
        pass
    return P.finish() if own else None


def _ret_tables(r):
    pos = np.arange(128, dtype=np.float64)
    DT = np.zeros((128, 2, 128), np.float64)
    wq = np.zeros((128, 128), np.float64)
    wk = np.zeros((128, 2), np.float64)
    gb = np.zeros((128, 1), np.float64)
    for hh in range(2):
        H = 2 * r + hh
        lg = np.log1p(-(2.0 ** (-5.0 - H)))
        s = pos[:, None]
        t = pos[None, :]
        ok = (np.floor(s / 64) <= np.floor(t / 64))
        DT[:, hh, :] = np.where(ok, np.exp(lg * np.abs(t - s)), 0.0) * 0.125
        wq[64 * hh:64 * hh + 64, :] = np.exp(lg * (pos + 1.0))[None, :] * 0.125
        wk[:, hh] = np.exp(lg * (127.0 - pos))
        gb[64 * hh:64 * hh + 64, 0] = np.exp(lg * 128.0)
    BD = np.zeros((128, 128), np.float32)
    BD[:64, :64] = 1.0 / 64
    BD[64:, 64:] = 1.0 / 64
    return (DT.reshape(128, 256).astype(np.float32), wq.astype(np.float32), wk.astype(np.float32),
            gb.astype(np.float32), BD)


TWO_PI = 2.0 * math.pi


def _kl(k):
    return list(k) if isinstance(k, (list, tuple)) else [k]


class _V:
    def __init__(self, P):
        self.P = P

    def tt(self, e, out, a, b, op):
        self.P.op(e, lambda g: g.tensor_tensor(out=out[0], in0=a[0], in1=b[0], op=op),
                  reads=_kl(a[1]) + _kl(b[1]), writes=_kl(out[1]))

    def ts(self, e, out, a, s1, op0, s2=None, op1=None, extra=()):
        if op1 is None:
            self.P.op(e, lambda g: g.tensor_scalar(out=out[0], in0=a[0], scalar1=s1, scalar2=None, op0=op0),
                      reads=_kl(a[1]) + list(extra), writes=_kl(out[1]))
        else:
            self.P.op(e, lambda g: g.tensor_scalar(out=out[0], in0=a[0], scalar1=s1, scalar2=s2, op0=op0, op1=op1),
                      reads=_kl(a[1]) + list(extra), writes=_kl(out[1]))

    def stt(self, out, a, scalar, b, op0, op1, extra=()):
        self.P.op("dve", lambda g: g.scalar_tensor_tensor(out=out[0], in0=a[0], scalar=scalar, in1=b[0], op0=op0, op1=op1),
                  reads=_kl(a[1]) + _kl(b[1]) + list(extra), writes=_kl(out[1]))

    def act(self, out, a, func, scale=1.0, bias=0.0, extra=()):
        self.P.op("act", lambda g: g.activation(out=out[0], in_=a[0], func=func, scale=scale, bias=bias),
                  reads=_kl(a[1]) + list(extra), writes=_kl(out[1]))

    def cp(self, e, out, a):
        self.P.copy(e, out[0], a[0], reads=_kl(a[1]), writes=_kl(out[1]))


def _range_reduce(V, x, ki, kf, r, y, m, sarg, carg):
    V.ts("dve", ki, x, 1.0 / TWO_PI, ALU.mult)
    V.cp("dve", kf, ki)
    V.stt(r, kf, -TWO_PI, x, ALU.mult, ALU.add)

    def wrap(dst, src):
        V.ts("dve", m, src, math.pi, ALU.is_gt, -TWO_PI, ALU.mult)
        V.tt("dve", y, src, m, ALU.add)
        V.ts("dve", m, y, -math.pi, ALU.is_lt, TWO_PI, ALU.mult)
        V.tt("dve", dst, y, m, ALU.add)

    wrap(sarg, r)
    V.ts("dve", r, r, math.pi / 2, ALU.add)
    wrap(carg, r)


def build_s5(L=SEQ, P=None):
    own = P is None
    P = P or Prog()
    V = _V(P)
    NJ = L // 8
    NG = 16
    Ub_in = P.dram_in("Ub", [NG, 128, NJ], BF16)
    U32_in = P.dram_in("U32", [NG, 128, NJ], F32)
    are_in = P.dram_in("are", [128, 8], F32)
    aim_in = P.dram_in("aim", [128, 8], F32)
    ldt_in = P.dram_in("ldt", [128, 8], F32)
    bre_in = P.dram_in("bre", [128, 8, 16], F32)
    bim_in = P.dram_in("bim", [128, 8, 16], F32)
    cre_in = P.dram_in("cre", [128, 8, 16], F32)
    cim_in = P.dram_in("cim", [128, 8, 16], F32)
    drep_in = P.dram_in("drep", [128, NG], F32)
    nvec_in = P.dram_in("nvec", [128, 24], F32)
    jvec_in = P.dram_in("jvec", [128, NJ], F32)
    mask_in = P.dram_in("mask01", [128, 128], F32)
    ident_in = P.dram_in("ident", [128, 128], F32)
    GY = P.dram_out("GY", [NG, 128, NJ], F32)
    GYb = P.dram_out("GYb", [NG, 128, NJ], BF16)

    def sbt(name, shape, dt=F32):
        return P.sb("s_" + name, shape, dt)

    are, aim, ldt = sbt("are", [128, 8]), sbt("aim", [128, 8]), sbt("ldt", [128, 8])
    bre, bim = sbt("bre", [128, 8, 16]), sbt("bim", [128, 8, 16])
    cre, cim = sbt("cre", [128, 8, 16]), sbt("cim", [128, 8, 16])
    drep = sbt("drep", [128, NG])
    nvec = sbt("nvec", [128, 24])
    jvec = sbt("jvec", [128, NJ])
    mask = sbt("mask", [128, 128])
    idf = sbt("idf", [128, 128])
    for t, src, k in ((are, are_in, "are"), (aim, aim_in, "aim"), (ldt, ldt_in, "ldt"), (bre, bre_in, "bre"),
                      (bim, bim_in, "bim"), (cre, cre_in, "cre"), (cim, cim_in, "cim"), (drep, drep_in, "drep"),
                      (nvec, nvec_in, "nvec"), (jvec, jvec_in, "jvec"), (mask, mask_in, "mask"), (idf, ident_in, "idf")):
        P.dma(t[:], src, writes=[k])

    dt = sbt("dt", [128, 8]); lm = sbt("lm", [128, 8]); th = sbt("th", [128, 8])
    V.act((dt[:], "dt"), (ldt[:], "ldt"), AF.Exp)
    V.tt("dve", (lm[:], "lm"), (are[:], "are"), (dt[:], "dt"), ALU.mult)
    V.tt("dve", (th[:], "th"), (aim[:], "aim"), (dt[:], "dt"), ALU.mult)
    NE = 24
    em = sbt("em", [128, 8, NE]); ea = sbt("ea", [128, 8, NE])
    eki = sbt("eki", [128, 8, NE], mybir.dt.int32); ekf = sbt("ekf", [128, 8, NE]); er = sbt("er", [128, 8, NE])
    ey = sbt("ey", [128, 8, NE]); emk = sbt("emk", [128, 8, NE]); esa = sbt("esa", [128, 8, NE]); eca = sbt("eca", [128, 8, NE])
    Ere = sbt("Ere", [128, 8, NE]); Eim = sbt("Eim", [128, 8, NE])
    nb = nvec[:, :].unsqueeze(1).to_broadcast([128, 8, NE])
    V.tt("dve", (em[:], "em"), (lm[:, :].unsqueeze(2).to_broadcast([128, 8, NE]), "lm"), (nb, "nvec"), ALU.mult)
    V.tt("dve", (ea[:], "ea"), (th[:, :].unsqueeze(2).to_broadcast([128, 8, NE]), "th"), (nb, "nvec"), ALU.mult)
    V.act((em[:], "em"), (em[:], "em"), AF.Exp)
    _range_reduce(V, (ea[:], "ea"), (eki[:], "eki"), (ekf[:], "ekf"), (er[:], "er"), (ey[:], "ey"), (emk[:], "emk"),
                  (esa[:], "esa"), (eca[:], "eca"))
    V.act((esa[:], "esa"), (esa[:], "esa"), AF.Sin)
    V.act((eca[:], "eca"), (eca[:], "eca"), AF.Sin)
    V.tt("dve", (Ere[:], "Ere"), (em[:], "em"), (eca[:], "eca"), ALU.mult)
    V.tt("dve", (Eim[:], "Eim"), (em[:], "em"), (esa[:], "esa"), ALU.mult)
    lr1 = sbt("lr1", [128, 8]); den = sbt("den", [128, 8]); tmpa = sbt("tmpa", [128, 8]); tmpb = sbt("tmpb", [128, 8])
    fr = sbt("fr", [128, 8]); fi = sbt("fi", [128, 8])
    li = (Eim[:, :, 16], "Eim")
    V.ts("dve", (lr1[:], "lr1"), (Ere[:, :, 16], "Ere"), -1.0, ALU.add)
    V.tt("dve", (den[:], "den"), (are[:], "are"), (are[:], "are"), ALU.mult)
    V.tt("dve", (tmpa[:], "tmpa"), (aim[:], "aim"), (aim[:], "aim"), ALU.mult)
    V.tt("dve", (den[:], "den"), (den[:], "den"), (tmpa[:], "tmpa"), ALU.add)
    P.op("dve", lambda g: g.reciprocal(out=den[:], in_=den[:]), reads=["den"], writes=["den"])
    V.tt("dve", (tmpa[:], "tmpa"), (lr1[:], "lr1"), (are[:], "are"), ALU.mult)
    V.tt("dve", (tmpb[:], "tmpb"), li, (aim[:], "aim"), ALU.mult)
    V.tt("dve", (tmpa[:], "tmpa"), (tmpa[:], "tmpa"), (tmpb[:], "tmpb"), ALU.add)
    V.tt("dve", (fr[:], "fr"), (tmpa[:], "tmpa"), (den[:], "den"), ALU.mult)
    V.tt("dve", (tmpa[:], "tmpa"), li, (are[:], "are"), ALU.mult)
    V.tt("dve", (tmpb[:], "tmpb"), (lr1[:], "lr1"), (aim[:], "aim"), ALU.mult)
    V.tt("dve", (tmpa[:], "tmpa"), (tmpa[:], "tmpa"), (tmpb[:], "tmpb"), ALU.subtract)
    V.tt("dve", (fi[:], "fi"), (tmpa[:], "tmpa"), (den[:], "den"), ALU.mult)
    bbr = sbt("bbr", [128, 8, 16]); bbi = sbt("bbi", [128, 8, 16]); t16a = sbt("t16a", [128, 8, 16]); t16b = sbt("t16b", [128, 8, 16])
    frb = (fr[:, :].unsqueeze(2).to_broadcast([128, 8, 16]), "fr")
    fib = (fi[:, :].unsqueeze(2).to_broadcast([128, 8, 16]), "fi")
    V.tt("dve", (t16a[:], "t16a"), frb, (bre[:], "bre"), ALU.mult)
    V.tt("dve", (t16b[:], "t16b"), fib, (bim[:], "bim"), ALU.mult)
    V.tt("dve", (bbr[:], "bbr"), (t16a[:], "t16a"), (t16b[:], "t16b"), ALU.subtract)
    V.tt("dve", (t16a[:], "t16a"), frb, (bim[:], "bim"), ALU.mult)
    V.tt("dve", (t16b[:], "t16b"), fib, (bre[:], "bre"), ALU.mult)
    V.tt("dve", (bbi[:], "bbi"), (t16a[:], "t16a"), (t16b[:], "t16b"), ALU.add)

    NJ_ = NJ
    rho = sbt("rho", [128, 8, NJ]); cosM = sbt("cosM", [128, 8, NJ]); sinM = sbt("sinM", [128, 8, NJ])
    WR = sbt("WR", [128, 8, NJ]); WI = sbt("WI", [128, 8, NJ]); ZR = sbt("ZR", [128, 8, NJ]); ZI = sbt("ZI", [128, 8, NJ])
    if 8 * NJ >= 1024:
        big1v = WR[:].rearrange("p a j -> p (a j)")[:, 0:1024].rearrange("p (a s h) -> p a s h", a=8, s=8)
        big2v = WI[:].rearrange("p a j -> p (a j)")[:, 0:1024].rearrange("p (a s h) -> p a s h", a=8, s=8)
    else:
        big1v = sbt("big1", [128, 8, 8, 16])[:]
        big2v = sbt("big2", [128, 8, 8, 16])[:]
    B1 = (big1v, "WR")
    B2 = (big2v, "WI")

    def cprod(n0, xr, xi, xrk, xik, outr, outi, neg_im):
        er_b = (Ere[:, :, n0:n0 + 8].unsqueeze(3).to_broadcast([128, 8, 8, 16]), "Ere")
        ei_b = (Eim[:, :, n0:n0 + 8].unsqueeze(3).to_broadcast([128, 8, 8, 16]), "Eim")
        xr_b = (xr[:, :, :].unsqueeze(2).to_broadcast([128, 8, 8, 16]), xrk)
        xi_b = (xi[:, :, :].unsqueeze(2).to_broadcast([128, 8, 8, 16]), xik)
        V.tt("dve", B1, er_b, xr_b, ALU.mult)
        V.tt("dve", B2, ei_b, xi_b, ALU.mult)
        V.tt("pool", outr, B1, B2, ALU.subtract)
        V.tt("dve", B1, er_b, xi_b, ALU.mult)
        V.tt("dve", B2, ei_b, xr_b, ALU.mult)
        if neg_im:
            V.stt(outi, B1, -1.0, B2, ALU.mult, ALU.subtract)
        else:
            V.tt("pool", outi, B1, B2, ALU.add)

    bsr = sbt("bsr", [128, 8, 128], BF16); bsi = sbt("bsi", [128, 8, 128], BF16)
    wir = sbt("wir", [128, 8, 128]); wii = sbt("wii", [128, 8, 128])
    wor = sbt("wor", [128, 8, 128], BF16); woi = sbt("woi", [128, 8, 128], BF16)

    def v4(t):
        return t[:].rearrange("p a (s h) -> p a s h", s=8)

    cprod(0, bbr, bbi, "bbr", "bbi", (v4(bsr), "bsr"), (v4(bsi), "bsi"), False)
    cprod(8, bbr, bbi, "bbr", "bbi", (v4(wir), "wir"), (v4(wii), "wii"), False)
    cprod(16, cre, cim, "cre", "cim", (v4(wor), "wor"), (v4(woi), "woi"), True)

    wop_r = sbt("wop_r", [128, NG, 128], BF16); wop_i = sbt("wop_i", [128, NG, 128], BF16)
    wip_r = sbt("wip_r", [128, NG, 128], BF16); wip_i = sbt("wip_i", [128, NG, 128], BF16)
    m0 = sbt("m0", [128, NG, 128], BF16)
    for t, k in ((wop_r, "wop_r"), (wop_i, "wop_i"), (wip_r, "wip_r"), (wip_i, "wip_i")):
        P.op("pool", lambda e, t=t: e.memset(t[:], 0.0), writes=[k])
    pst = [P.ps("pst%d" % i, [128, 512], F32) for i in range(2)]
    psy = [P.ps("psy%d" % i, [128, 512], F32) for i in range(2)]
    pss = [P.ps("pss%d" % i, [128, 512], F32) for i in range(4)]
    for g in range(NG):
        Pp, gp = g // 2, g % 2
        hs = slice(64 * gp, 64 * gp + 64)
        P.op("pool", lambda e, g=g, Pp=Pp, hs=hs: e.tensor_copy(out=wop_r[hs, g, :], in_=wor[hs, Pp, :]),
             reads=["wor", "wop_r"], writes=["wop_r"])
        P.op("pool", lambda e, g=g, Pp=Pp, hs=hs: e.tensor_copy(out=wop_i[hs, g, :], in_=woi[hs, Pp, :]),
             reads=["woi", "wop_i"], writes=["wop_i"])
    for Pp in range(8):
        for (src, srck, dst, dstk) in ((wir, "wir", wip_r, "wip_r"), (wii, "wii", wip_i, "wip_i")):
            ti = st_i = (Pp * 2 + (0 if src is wir else 1)) % 2
            tp = pst[ti]
            tk = "P:pst%d" % ti
            P.op("pe", lambda e, src=src, Pp=Pp, tp=tp: e.transpose(out=tp[:, 0:128], in_=src[:, Pp, :], identity=idf[:]),
                 reads=[srck, "idf"], writes=[tk])
            for gp in range(2):
                g = 2 * Pp + gp
                cs = slice(64 * gp, 64 * gp + 64)
                P.copy("act" if gp else "dve", dst[:, g, cs], tp[:, cs], reads=[tk, dstk], writes=[dstk])
    for g in range(NG):
        Pp = g // 2
        tp = pst[g % 2]
        tk = "P:pst%d" % (g % 2)
        P.op("pe", lambda e, g=g, Pp=Pp, tp=tp: e.matmul(tp[:, 0:128], lhsT=bsr[:, Pp, :], rhs=wop_r[:, g, :], start=True, stop=False),
             reads=["bsr", "wop_r"], writes=[tk])
        P.op("pe", lambda e, g=g, Pp=Pp, tp=tp: e.matmul(tp[:, 0:128], lhsT=bsi[:, Pp, :], rhs=wop_i[:, g, :], start=False, stop=True),
             reads=["bsi", "wop_i"], writes=[tk])
        P.op("dve", lambda e, g=g, tp=tp: e.tensor_tensor(out=m0[:, g, :], in0=tp[:, 0:128], in1=mask[:], op=ALU.mult),
             reads=[tk, "mask", "m0"], writes=["m0"])

    NT = 8 * NJ
    kiB_ap = ZI[:].bitcast(mybir.dt.int32)
    ph = sbt("ph", [128, 8]); phi_ = sbt("phi_", [128, 8], mybir.dt.int32); phf = sbt("phf", [128, 8]); rho1 = sbt("rho1", [128, 8])
    V.ts("dve", (ph[:], "ph"), (th[:], "th"), 8.0, ALU.mult)
    V.ts("dve", (phi_[:], "phi_"), (ph[:], "ph"), 1.0 / TWO_PI, ALU.mult)
    V.cp("dve", (phf[:], "phf"), (phi_[:], "phi_"))
    V.stt((ph[:], "ph"), (phf[:], "phf"), -TWO_PI, (ph[:], "ph"), ALU.mult, ALU.add)
    V.act((rho1[:], "rho1"), (lm[:], "lm"), AF.Exp, scale=8.0)
    V.cp("dve", (rho[:], "rho"), (rho1[:, :].unsqueeze(2).to_broadcast([128, 8, NJ]), "rho1"))
    P.op("pool", lambda e: e.memset(rho[:, :, 0:1], 0.0), reads=["rho"], writes=["rho"])
    V.tt("dve", (WR[:], "WR"), (ph[:, :].unsqueeze(2).to_broadcast([128, 8, NJ]), "ph"),
         (jvec[:, :].unsqueeze(1).to_broadcast([128, 8, NJ]), "jvec"), ALU.mult)
    _range_reduce(V, (WR[:], "WR"), (kiB_ap, "ZI"), (WI[:], "WI"), (ZR[:], "ZR"), (ZI[:], "ZI"), (WI[:], "WI"),
                  (sinM[:], "sinM"), (cosM[:], "cosM"))
    V.act((sinM[:], "sinM"), (sinM[:], "sinM"), AF.Sin)
    V.act((cosM[:], "cosM"), (cosM[:], "cosM"), AF.Sin)

    ub = [sbt("ub%d" % i, [128, NJ], BF16) for i in range(4)]
    u32 = [sbt("u32_%d" % i, [128, NJ]) for i in range(2)]
    if NJ >= 512:
        ta = [ZR[:, i, 0:512] for i in range(2)]
        tb = [ZI[:, i, 0:512] for i in range(2)]
        P.op("dve", lambda e: e.memset(ZR[:, 0:2, 0:1], 0.0), reads=["ZR", "ZI"], writes=["ta0", "ta1", "tb0", "tb1", "ZR"])
        P.op("dve", lambda e: e.memset(ZI[:, 0:2, 0:1], 0.0), reads=["ZR", "ZI"], writes=["ta0", "ta1", "tb0", "tb1", "ZI"])
    else:
        ta = [sbt("ta%d" % i, [128, 512])[:] for i in range(2)]
        tb = [sbt("tb%d" % i, [128, 512])[:] for i in range(2)]
    NC5 = NJ // 512 if NJ >= 512 else 1
    CW = min(NJ, 512)
    for Pp in range(8):
        for gp in range(2):
            g = 2 * Pp + gp
            P.dma(ub[g % 4][:], Ub_in[g], writes=["ub%d" % (g % 4)])
        for c in range(NC5):
            cs = slice(c * CW, (c + 1) * CW)
            sr, si = pss[(2 * Pp) % 4], pss[(2 * Pp + 1) % 4]
            srk, sik = "P:pss%d" % ((2 * Pp) % 4), "P:pss%d" % ((2 * Pp + 1) % 4)
            for (ps_, psk, wt, wtk) in ((sr, srk, wip_r, "wip_r"), (si, sik, wip_i, "wip_i")):
                for gp in range(2):
                    g = 2 * Pp + gp
                    P.op("pe", lambda e, ps_=ps_, wt=wt, g=g, gp=gp: e.matmul(ps_[:, 0:CW], lhsT=wt[:, g, :], rhs=ub[g % 4][:, cs],
                                                                           start=(gp == 0), stop=(gp == 1)),
                         reads=[wtk, "ub%d" % (g % 4)], writes=[psk])
            i = Pp % 2
            cM, sM = (cosM[:, Pp, cs], "cosM"), (sinM[:, Pp, cs], "sinM")
            V.tt("dve", (ta[i][:, 0:CW], "ta%d" % i), (sr[:, 0:CW], srk), cM, ALU.mult)
            V.tt("dve", (tb[i][:, 0:CW], "tb%d" % i), (si[:, 0:CW], sik), sM, ALU.mult)
            V.tt("pool", (WR[:, Pp, cs], "WRo"), (ta[i][:, 0:CW], "ta%d" % i), (tb[i][:, 0:CW], "tb%d" % i), ALU.add)
            V.tt("dve", (ta[i][:, 0:CW], "ta%d" % i), (si[:, 0:CW], sik), cM, ALU.mult)
            V.tt("dve", (tb[i][:, 0:CW], "tb%d" % i), (sr[:, 0:CW], srk), sM, ALU.mult)
            V.tt("pool", (WI[:, Pp, cs], "WIo"), (ta[i][:, 0:CW], "ta%d" % i), (tb[i][:, 0:CW], "tb%d" % i), ALU.subtract)

    def flat(t):
        return t[:].rearrange("p a j -> p (a j)")

    P.op("dve", lambda e: e.tensor_tensor_scan(out=flat(ZR), data0=flat(rho), data1=flat(WR), initial=0.0, op0=ALU.mult, op1=ALU.add),
         reads=["rho", "WRo", "WR", "ZR"], writes=["ZR", "ta0", "ta1"])
    P.op("dve", lambda e: e.tensor_tensor_scan(out=flat(ZI), data0=flat(rho), data1=flat(WI), initial=0.0, op0=ALU.mult, op1=ALU.add),
         reads=["rho", "WIo", "WI", "ZI"], writes=["ZI", "tb0", "tb1"])
    XR = sbt("XR", [128, 8, NJ], BF16); XI = sbt("XI", [128, 8, NJ], BF16)
    P.op("pool", lambda e: e.memset(XR[:, :, 0:1], 0.0), writes=["XR"])
    P.op("pool", lambda e: e.memset(XI[:, :, 0:1], 0.0), writes=["XI"])
    n1 = NJ - 1
    zr, zi = (ZR[:, :, 0:n1], "ZR"), (ZI[:, :, 0:n1], "ZI")
    cm, sm = (cosM[:, :, 0:n1], "cosM"), (sinM[:, :, 0:n1], "sinM")
    V.tt("dve", (WR[:, :, 0:n1], "WR2"), zr, cm, ALU.mult)
    V.tt("pool", (WI[:, :, 0:n1], "WI2"), zi, sm, ALU.mult)
    V.tt("dve", (XR[:, :, 1:NJ], "XR"), (WR[:, :, 0:n1], "WR2"), (WI[:, :, 0:n1], "WI2"), ALU.subtract)
    V.tt("dve", (WR[:, :, 0:n1], "WR2"), zr, sm, ALU.mult)
    V.tt("pool", (WI[:, :, 0:n1], "WI2"), zi, cm, ALU.mult)
    V.tt("dve", (XI[:, :, 1:NJ], "XI"), (WR[:, :, 0:n1], "WR2"), (WI[:, :, 0:n1], "WI2"), ALU.add)

    NBUF = 3
    yv = [sbt("yv%d" % i, [128, 512]) for i in range(NBUF)]
    go = [sbt("go%d" % i, [128, 512]) for i in range(NBUF)]
    gob = [sbt("gob%d" % i, [128, 512], BF16) for i in range(NBUF)]
    GC = math.sqrt(2.0 / math.pi)
    for g in range(NG):
        Pp = g // 2
        ui, u3 = g % 4, g % 2
        P.dma(ub[ui][:], Ub_in[g], writes=["ub%d" % ui])
        P.dma(u32[u3][:], U32_in[g], writes=["u32_%d" % u3])
        for c in range(NC5):
            cs = slice(c * CW, (c + 1) * CW)
            i = (g * NC5 + c) % NBUF
            y = psy[i % 2]
            yk = "P:psy%d" % (i % 2)
            P.op("pe", lambda e: e.matmul(y[:, 0:CW], lhsT=m0[:, g, :], rhs=ub[ui][:, cs], start=True, stop=False),
                 reads=["m0", "ub%d" % ui], writes=[yk])
            P.op("pe", lambda e: e.matmul(y[:, 0:CW], lhsT=wop_r[:, g, :], rhs=XR[:, Pp, cs], start=False, stop=False),
                 reads=["wop_r", "XR"], writes=[yk])
            P.op("pe", lambda e: e.matmul(y[:, 0:CW], lhsT=wop_i[:, g, :], rhs=XI[:, Pp, cs], start=False, stop=True),
                 reads=["wop_i", "XI"], writes=[yk])
            W = slice(0, CW)
            yvk, gok, gobk = "yv%d" % i, "go%d" % i, "gob%d" % i
            V.stt((yv[i][:, W], yvk), (u32[u3][:, cs], "u32_%d" % u3), drep[:, g:g + 1], (y[:, W], yk), ALU.mult, ALU.add,
                  extra=["drep"])
            V.act((go[i][:, W], gok), (yv[i][:, W], yvk), AF.Gelu_apprx_tanh)
            V.cp("pool" if g % 2 else "dve", (gob[i][:, W], gobk), (go[i][:, W], gok))
            P.dma(GY[g][:, cs], go[i][:, W], reads=[gok], is_output=True, q="pool")
            P.dma(GYb[g][:, cs], gob[i][:, W], reads=[gobk], is_output=True, q="pool")
    return P.finish() if own else None


def _s5_host_inputs(inp, l, L):
    def pair(a):
        a = np.asarray(a, np.float32)
        tail = a.shape[2:]
        return np.ascontiguousarray(a.reshape(8, 2, 64, *tail).transpose(1, 2, 0, *range(3, 3 + len(tail))).reshape(128, 8, *tail))
    d = {}
    d["are"] = pair(inp["s5_a_re"][l])
    d["aim"] = pair(inp["s5_a_im"][l])
    d["ldt"] = pair(np.repeat(inp["s5_log_dt"][l][:, None], 64, axis=1))
    d["bre"] = pair(inp["s5_b_re"][l])
    d["bim"] = pair(inp["s5_b_im"][l])
    d["cre"] = pair(np.transpose(inp["s5_c_re"][l], (0, 2, 1)))
    d["cim"] = pair(np.transpose(inp["s5_c_im"][l], (0, 2, 1)))
    dd = np.asarray(inp["s5_d"][l], np.float32).reshape(16, 16)
    d["drep"] = np.ascontiguousarray(np.tile(dd.T, (8, 1)))
    nv = np.concatenate([-(np.arange(8) + 1.0), 7.0 - np.arange(8), np.arange(8) + 1.0]).astype(np.float32)
    d["nvec"] = np.ascontiguousarray(np.tile(nv[None, :], (128, 1)))
    d["jvec"] = np.ascontiguousarray(np.tile((np.arange(L // 8, dtype=np.float32) + 1.0)[None, :], (128, 1)))
    s = np.arange(128)[:, None] // 16
    t = np.arange(128)[None, :] // 16
    d["mask01"] = (t >= s).astype(np.float32)
    d["ident"] = np.eye(128, dtype=np.float32)
    return d


def _shuffle_in(suT, L):
    a = np.asarray(suT).reshape(16, 16, L // 8, 8)
    return np.ascontiguousarray(a.transpose(0, 3, 1, 2).reshape(16, 128, L // 8))


def _shuffle_out(G, L):
    a = np.asarray(G).reshape(16, 8, 16, L // 8)
    return np.ascontiguousarray(a.transpose(0, 2, 3, 1).reshape(256, L))


def build_phase_c(Tc=2048, final=False):
    P = Prog()
    V = _V(P)
    NCH = Tc // 512
    oT8 = P.dram_in("oT8", [8, 65, Tc], F32)
    yret = P.dram_in("yret", [256, Tc], F32)
    gy = P.dram_in("gy", [256, Tc], F32)
    gyb = P.dram_in("gyb", [256, Tc], BF16)
    sg = P.dram_in("sg", [1024, Tc], F32)
    wglu = P.dram_in("wglu", [256, 256], F32)
    wout = P.dram_in("wout", [1024, 1024], F32)
    x = P.dram_in("x", [Tc, D_MODEL], F32)
    fnw = P.dram_in("fnw", [128, D_MODEL], F32)
    xo = P.dram_out("xo", [Tc, D_MODEL], F32)

    Wo = P.sb("Wo", [128, 8, 1024], BF16)
    Wg = P.sb("Wg", [128, 2, 256], BF16)
    wst = [P.sb("wst%d" % i, [128, 1024], F32) for i in range(2)]
    fn = P.sb("fn", [128, D_MODEL], F32)
    yT = [P.sb("yT%d" % i, [128, 8, 512], BF16) for i in range(2)]
    la = [P.sb("la%d" % i, [128, 512], F32) for i in range(2)]
    lb = [P.sb("lb%d" % i, [128, 512], F32) for i in range(2)]
    lc = [P.sb("lc%d" % i, [128, 512], F32) for i in range(2)]
    gbs = [P.sb("gbs%d" % i, [128, 2, 512], BF16) for i in range(2)]
    xt = [P.sb("xt%d" % i, [128, D_MODEL], F32) for i in range(2)]
    xn = [P.sb("xn%d" % i, [128, D_MODEL], F32) for i in range(2)]
    sqj = P.sb("sqj", [128, D_MODEL], F32)
    ss = [P.sb("ss%d" % i, [128, 1], F32) for i in range(2)]
    psg = [P.ps("psg%d" % i, [128, 512], F32) for i in range(2)]
    pso = [P.ps("pso%d" % i, [128, 512], F32) for i in range(4)]

    if final:
        P.dma(fn[:], fnw, writes=["fn"])
    for kt in range(8):
        w, wk = wst[kt % 2], "wst%d" % (kt % 2)
        P.dma(w[:], wout[kt * 128:(kt + 1) * 128, :], writes=[wk])
        P.copy("dve", Wo[:, kt, :], w[:], reads=[wk], writes=["Wo"])
    for kt in range(2):
        w, wk = wst[kt % 2], "wst%d" % (kt % 2)
        P.dma(w[:, 0:256], wglu[kt * 128:(kt + 1) * 128, :], writes=[wk])
        P.copy("dve", Wg[:, kt, :], w[:, 0:256], reads=[wk], writes=["Wg"])

    cnt = 0
    for c in range(NCH):
        cs = slice(c * 512, (c + 1) * 512)
        y = yT[c % 2]
        yk = "yT%d" % (c % 2)
        P.dma(gbs[c % 2][:], gyb[:, cs].rearrange("(k p) t -> p k t", p=128), writes=["gbs%d" % (c % 2)])
        for kt in range(8):
            i = cnt % 2
            cnt += 1
            A, B, C = (la[i][:], "la%d" % i), (lb[i][:], "lb%d" % i), (lc[i][:], "lc%d" % i)
            P.dma(lc[i][:], sg[kt * 128:(kt + 1) * 128, cs], writes=[C[1]])
            yo = (y[:, kt, :], yk + "_%d" % kt)
            if kt < 4:
                for hh in range(2):
                    h = 2 * kt + hh
                    ps_ = slice(64 * hh, 64 * hh + 64)
                    P.dma(la[i][ps_, :], oT8[h, 0:64, cs], writes=[A[1]])
                    P.dma(lb[i][ps_, :], oT8[h, 64:65, cs].to_broadcast([64, 512]), writes=[B[1]])
                P.op("dve", lambda e, i=i: e.reciprocal(out=lb[i][:], in_=lb[i][:]), reads=[B[1]], writes=[B[1]])
                V.tt("dve", A, A, B, ALU.mult)
                V.tt("pool", yo, A, C, ALU.mult)
            elif kt < 6:
                j = kt - 4
                P.dma(la[i][:], gy[j * 128:(j + 1) * 128, cs], writes=[A[1]])
                pg, pgk = psg[j], "P:psg%d" % j
                for k2 in range(2):
                    P.op("pe", lambda e, k2=k2, j=j, pg=pg: e.matmul(pg[:], lhsT=Wg[:, k2, j * 128:(j + 1) * 128],
                                                                  rhs=gbs[c % 2][:, k2, :], start=(k2 == 0), stop=(k2 == 1)),
                         reads=["Wg", "gbs%d" % (c % 2)], writes=[pgk])
                V.act(B, (pg[:], pgk), AF.Sigmoid)
                V.tt("dve", A, A, B, ALU.mult)
                V.tt("pool", yo, A, C, ALU.mult)
            else:
                j = kt - 6
                P.dma(la[i][:], yret[j * 128:(j + 1) * 128, cs], writes=[A[1]])
                V.tt("pool", yo, A, C, ALU.mult)
        ykeys = [yk + "_%d" % kt for kt in range(8)]
        for tt in range(4):
            n = c * 4 + tt
            xi = n % 2
            P.dma(xt[xi][:], x[n * 128:(n + 1) * 128, :], writes=["xt%d" % xi])
            for ch in range(2):
                po = pso[(n * 2 + ch) % 4]
                pok = "P:pso%d" % ((n * 2 + ch) % 4)
                for kt in range(8):
                    P.op("pe", lambda e, kt=kt, po=po, ch=ch, tt=tt: e.matmul(
                        po[:], lhsT=y[:, kt, tt * 128:(tt + 1) * 128], rhs=Wo[:, kt, ch * 512:(ch + 1) * 512],
                        start=(kt == 0), stop=(kt == 7)), reads=[ykeys[kt], "Wo"], writes=[pok])
                V.tt("dve", (xn[xi][:, ch * 512:(ch + 1) * 512], "xn%d" % xi), (po[:], pok),
                     (xt[xi][:, ch * 512:(ch + 1) * 512], "xt%d" % xi), ALU.add)
            if final:
                P.op("act", lambda e, xi=xi: e.activation(out=sqj[:], in_=xn[xi][:], func=AF.Square, accum_out=ss[xi][:]),
                     reads=["xn%d" % xi], writes=["sqj", "ss%d" % xi])
                P.op("act", lambda e, xi=xi: e.activation(out=ss[xi][:], in_=ss[xi][:], func=AF.Sqrt, scale=1.0 / D_MODEL, bias=EPS),
                     reads=["ss%d" % xi], writes=["ss%d" % xi])
                P.op("dve", lambda e, xi=xi: e.reciprocal(out=ss[xi][:], in_=ss[xi][:]), reads=["ss%d" % xi], writes=["ss%d" % xi])
                V.stt((xn[xi][:], "xn%d" % xi), (xn[xi][:], "xn%d" % xi), ss[xi][:, 0:1], (fn[:], "fn"), ALU.mult, ALU.mult,
                      extra=["ss%d" % xi])
            P.dma(xo[n * 128:(n + 1) * 128, :], xn[xi][:], reads=["xn%d" % xi], is_output=True)
    return P.finish()


def _run(nc, in_maps):
    res = run_bass_kernel_spmd(nc, in_maps, core_ids=list(range(len(in_maps))))
    return res.results


def _c(a):
    return np.ascontiguousarray(a)


F_GROUPS = ([("q", h, 64) for h in range(4)] + [("k", 0, 68)] + [("k", h, 64) for h in (1, 2, 3)]
            + [("rq", 0, 128), ("rqs", 0, 128), ("rk", 0, 128), ("rks", 0, 128), ("su", 0, 128), ("su", 1, 128)]
            + [("g", i, 128) for i in range(4)])
F_NF = sum(g[2] for g in F_GROUPS)
F_NCOL = F_NF + 384


def emit_phase_a(P, L=SEQ):
    NB = L // 512
    NJ = L // 8
    x = P.io["x"]; wA = P.io["wA"]; normw = P.io["normw"]; bf = P.io["bf"]; ident_in = P.io["ident"]
    cosT = P.io["cosT"]; sinT = P.io["sinT"]
    qT = P.io["qT"]; kT = P.io["kT"]; Vfox = P.io["Vfox"]; Vtm = P.io["Vtm"]; Ktm = P.io["Ktm"]
    cneg_o = P.io["cneg"]; rqT = P.io["rqT"]; rkT = P.io["rkT"]; suP = P.io["suP"]; suPb = P.io["suPb"]; sgT = P.io["sgT"]

    Wb = P.sb("Wb", [128, 8, F_NCOL], BF16)
    wst = [P.sb("wst%d" % i, [128, F_NCOL], F32) for i in range(2)]
    gsb = P.sb("gsb", [128, 8], F32)
    ident_f = P.sb("ident_f", [128, 128], F32)
    ident = P.sb("ident_b", [128, 128], BF16)
    bsb = P.sb("bsb", [128, 1], F32)
    negb = P.sb("negb", [128, 1], F32)
    xt = [P.sb("xt%d" % i, [128, D_MODEL], F32) for i in range(3)]
    sqj = P.sb("sqj", [128, D_MODEL], F32)
    ss = [P.sb("ss%d" % i, [128, 1], F32) for i in range(3)]
    sd = [P.sb("sd%d" % i, [128, 1], F32) for i in range(3)]
    rs = [P.sb("rs%d" % i, [128, 1], F32) for i in range(3)]
    hb = [P.sb("hb%d" % i, [128, D_MODEL], BF16) for i in range(8)]
    hT = [P.sb("hT%d" % i, [128, 8, 512], BF16) for i in range(2)]
    cosb = [P.sb("cosb%d" % i, [128, 512], F32) for i in range(2)]
    sinb = [P.sb("sinb%d" % i, [128, 512], F32) for i in range(2)]
    ob16 = [P.sb("ob16_%d" % i, [128, 512], BF16) for i in range(6)]
    of32 = [P.sb("of32_%d" % i, [128, 512], F32) for i in range(6)]
    ov16 = [P.sb("ov16_%d" % i, [128, 384], BF16) for i in range(2)]
    kt16 = [P.sb("kt16_%d" % i, [128, 512], BF16) for i in range(2)]
    flog = P.sb("flog", [128, L], F32)
    ework = P.sb("ework", [128, L], F32)
    onesf = P.sb("onesf", [128, L], F32)
    chat = P.sb("chat", [128, L], BF16)
    ptr = [P.ps("ptr%d" % i, [128, 8 * 128], BF16) for i in range(2)]
    pp = [P.ps("pp%d" % i, [128, 512], F32) for i in range(6)]
    st = {}

    P.dma(gsb[:], normw, writes=["gsb"])
    P.dma(ident_f[:], ident_in, writes=["ident_f"])
    P.dma(bsb[:], bf, writes=["bsb"])
    P.op("dve", lambda e: e.tensor_copy(out=ident[:], in_=ident_f[:]), reads=["ident_f"], writes=["ident"])
    P.op("dve", lambda e: e.tensor_scalar_mul(out=negb[:], in0=bsb[:], scalar1=-1.0), reads=["bsb"], writes=["negb"])
    P.op("pool", lambda e: e.memset(onesf[64:68, :], 1.0), writes=["onesf"])
    pA = P.sb("pA", [128, L], BF16)
    pB = P.sb("pB", [128, L], BF16)
    carry = P.sb("carry", [128, 1], F32)
    P.op("pool", lambda e: e.memset(pA[64:68, :], 1.0), writes=["pA"])
    P.dma(kT[:, 64, :], pA[64:68, :], reads=["pA"], writes=["kT64"], q="pool")
    for rr in (65, 66, 67):
        P.dma(qT[:, rr, :], pA[64:68, :], reads=["pA"], writes=["qT%d" % rr], q="pool")
    for kt in range(8):
        w = wst[kt % 2]
        wk = "wst%d" % (kt % 2)
        P.dma(w[:], wA[kt * 128:(kt + 1) * 128, :], writes=[wk])
        P.op("dve", lambda e, w=w, kt=kt: e.tensor_scalar_mul(out=Wb[:, kt, :], in0=w[:], scalar1=gsb[:, kt:kt + 1]),
             reads=[wk, "gsb"], writes=["Wb%d" % kt])
    wkeys = ["Wb%d" % kt for kt in range(8)]

    def ne_block(blk):
        for ti in range(4):
            n = blk * 4 + ti
            xi = n % 3
            P.dma(xt[xi][:], x[n * 128:(n + 1) * 128, :], writes=["xt%d" % xi])
            P.op("act", lambda e, xi=xi: e.activation(out=sqj[:], in_=xt[xi][:], func=AF.Square, accum_out=ss[xi][:]),
                 reads=["xt%d" % xi], writes=["sqj", "ss%d" % xi])
            P.op("act", lambda e, xi=xi: e.activation(out=sd[xi][:], in_=ss[xi][:], func=AF.Sqrt, scale=1.0 / D_MODEL, bias=EPS),
                 reads=["ss%d" % xi], writes=["sd%d" % xi])
            P.op("dve", lambda e, xi=xi: e.reciprocal(out=rs[xi][:], in_=sd[xi][:]), reads=["sd%d" % xi], writes=["rs%d" % xi])
            hi = n % 8
            P.op("dve", lambda e, xi=xi, hi=hi: e.tensor_scalar_mul(out=hb[hi][:], in0=xt[xi][:], scalar1=rs[xi][:, 0:1]),
                 reads=["xt%d" % xi, "rs%d" % xi], writes=["hb%d" % hi])

    def norm_block(blk):
        hTb = hT[blk % 2]
        hk = "hT%d" % (blk % 2)
        c0 = blk * 512
        cb, sbn = cosb[blk % 2], sinb[blk % 2]
        ck, sk = "cosb%d" % (blk % 2), "sinb%d" % (blk % 2)
        P.dma(cb[:], cosT[:, c0:c0 + 512], writes=[ck])
        P.dma(sbn[:], sinT[:, c0:c0 + 512], writes=[sk])
        for ti in range(4):
            n = blk * 4 + ti
            hi = n % 8
            pt = ptr[n % 2]
            ptk = "P:ptr%d" % (n % 2)
            for kt in range(8):
                P.op("pe", lambda e, kt=kt, pt=pt, hi=hi: e.transpose(out=pt[:, kt * 128:(kt + 1) * 128],
                                                                    in_=hb[hi][:, kt * 128:(kt + 1) * 128], identity=ident[:]),
                     reads=["hb%d" % hi, "ident"], writes=[ptk])
            P.copy("act" if ti % 2 else "dve", hTb[:, :, ti * 128:(ti + 1) * 128],
                   pt[:].rearrange("p (k t) -> p k t", k=8), reads=[ptk], writes=[hk + "_%d" % ti])
            pi = st.get("pp", 0) % 6
            st["pp"] = pi + 1
            pv, pvk = pp[pi], "P:pp%d" % pi
            for kt in range(8):
                P.op("pe", lambda e, kt=kt, pv=pv, ti=ti: e.matmul(pv[:, 0:384], lhsT=hTb[:, kt, ti * 128:(ti + 1) * 128],
                                                                 rhs=Wb[:, kt, F_NF:F_NF + 384], start=(kt == 0), stop=(kt == 7)),
                     reads=[hk + "_%d" % ti, wkeys[kt]], writes=[pvk])
            vi = n % 2
            P.copy("dve" if ti % 2 else "act", ov16[vi][:], pv[:, 0:384], reads=[pvk], writes=["ov%d" % vi])
            P.dma(Vfox[:, n * 128:(n + 1) * 128, :].rearrange("h p d -> p h d"),
                  ov16[vi][:, 0:256].rearrange("p (h d) -> p h d", h=4), reads=["ov%d" % vi], writes=["Vfox%d" % n], q="pool")
            P.dma(Vtm[n * 128:(n + 1) * 128, :], ov16[vi][:, 256:384], reads=["ov%d" % vi], writes=["Vtm%d" % n], q="pool")

    def groups_block(blk):
        hTb = hT[blk % 2]
        hk = "hT%d" % (blk % 2)
        c0 = blk * 512
        cb, sbn = cosb[blk % 2], sinb[blk % 2]
        ck, sk = "cosb%d" % (blk % 2), "sinb%d" % (blk % 2)
        hkeys = [hk + "_%d" % ti for ti in range(4)]

        col = 0
        pend = {}
        for gi, (kind, idx, M) in enumerate(F_GROUPS):
            pi = st.get("pp", 0) % 6
            st["pp"] = pi + 1
            ps_t = pp[pi]
            pk = "P:pp%d" % pi
            rhs_su = None
            for kt in range(8):
                rhs = hTb[:, kt, :]
                P.op("pe", lambda e, kt=kt, col=col, M=M, ps_t=ps_t, rhs=rhs: e.matmul(
                    ps_t[0:M, :], lhsT=Wb[:, kt, col:col + M], rhs=rhs, start=(kt == 0), stop=(kt == 7)),
                    reads=hkeys + [wkeys[kt]], writes=[pk])
            col += M
            oi = st.get("ob", 0) % 6
            st["ob"] = oi + 1
            o16, o32 = ob16[oi], of32[oi]
            k16, k32 = "ob16_%d" % oi, "of32_%d" % oi
            if kind == "q":
                P.op("act", lambda e, ps_t=ps_t, o16=o16: e.mul(out=o16[0:64, :], in_=ps_t[0:64, :], mul=0.125),
                     reads=[pk], writes=[k16])
                P.dma(qT[idx, 0:64, c0:c0 + 512], o16[0:64, :], reads=[k16], writes=["qT%d_%d" % (idx, blk)], q="pool")
            elif kind == "k":
                P.op("dve", lambda e, ps_t=ps_t, o16=o16: e.tensor_copy(out=o16[0:64, :], in_=ps_t[0:64, :]),
                     reads=[pk], writes=[k16])
                P.dma(kT[idx, 0:64, c0:c0 + 512], o16[0:64, :], reads=[k16], writes=["kT%d_%d" % (idx, blk)], q="pool")
                if idx == 0:
                    P.op("act", lambda e, ps_t=ps_t: e.copy(out=flog[64:68, c0:c0 + 512], in_=ps_t[64:68, :]),
                         reads=[pk], writes=["flog%d" % blk])
            elif kind in ("rq", "rk"):
                P.op("dve", lambda e, ps_t=ps_t, o32=o32: e.tensor_tensor(out=o32[:], in0=ps_t[:], in1=cb[:], op=ALU.mult),
                     reads=[pk, ck], writes=[k32])
                pend[kind] = (o32, k32)
            elif kind in ("rqs", "rks"):
                base = kind[:2]
                t1, t1k = pend[base]
                P.op("dve", lambda e, ps_t=ps_t, o32=o32: e.tensor_tensor(out=o32[:], in0=ps_t[:], in1=sbn[:], op=ALU.mult),
                     reads=[pk, sk], writes=[k32])
                P.op("pool", lambda e, t1=t1, o32=o32, o16=o16: e.tensor_tensor(out=o16[:], in0=t1[:], in1=o32[:], op=ALU.add),
                     reads=[t1k, k32], writes=[k16])
                dst = rqT if base == "rq" else rkT
                P.dma(dst[:, c0:c0 + 512], o16[:], reads=[k16], writes=[base + "T%d" % blk], q="pool")
                if base == "rk":
                    pt = ptr[blk % 2]
                    ptk = "P:ptr%d" % (blk % 2)
                    for t4 in range(4):
                        P.op("pe", lambda e, t4=t4, pt=pt, o16=o16: e.transpose(out=pt[:, t4 * 128:(t4 + 1) * 128],
                                                                            in_=o16[:, t4 * 128:(t4 + 1) * 128], identity=ident[:]),
                             reads=[k16, "ident"], writes=[ptk])
                    kb = kt16[blk % 2]
                    P.copy("act", kb[:], pt[:, 0:512], reads=[ptk], writes=["kt16_%d" % (blk % 2)])
                    P.dma(Ktm[c0:c0 + 512, :].rearrange("(t p) d -> p t d", p=128), kb[:].rearrange("p (t d) -> p t d", t=4),
                          reads=["kt16_%d" % (blk % 2)], writes=["Ktm%d" % blk], q="pool")
            elif kind == "su":
                P.op("act", lambda e, ps_t=ps_t, o32=o32: e.copy(out=o32[:].rearrange("p (s j) -> p s j", s=8),
                                                                  in_=ps_t[:].rearrange("p (j s) -> p s j", s=8)),
                     reads=[pk], writes=[k32])
                P.op("dve", lambda e, o32=o32, o16=o16: e.tensor_copy(out=o16[:], in_=o32[:]), reads=[k32], writes=[k16])
                jr = slice(blk * 64, blk * 64 + 64)
                for gl in range(8):
                    g = idx * 8 + gl
                    prt = slice(16 * gl, 16 * gl + 16)
                    P.dma(suP[g].rearrange("(s h) j -> h s j", s=8)[:, :, jr], o32[prt, :].rearrange("p (s j) -> p s j", s=8),
                          reads=[k32], writes=["suP%d_%d" % (g, blk)])
                    P.dma(suPb[g].rearrange("(s h) j -> h s j", s=8)[:, :, jr], o16[prt, :].rearrange("p (s j) -> p s j", s=8),
                          reads=[k16], writes=["suPb%d_%d" % (g, blk)])
            elif kind == "g":
                P.op("act", lambda e, ps_t=ps_t, o32=o32: e.activation(out=o32[:], in_=ps_t[:], func=AF.Silu),
                     reads=[pk], writes=[k32])
                P.dma(sgT[idx * 128:(idx + 1) * 128, c0:c0 + 512], o32[:], reads=[k32], writes=["sgT%d_%d" % (idx, blk)], q="pool")

    ne_block(0)
    if NB > 1:
        ne_block(1)
    norm_block(0)
    s4 = slice(64, 68)

    def c_chain(lo, hi, tag):
        cs_ = slice(lo, hi)
        fk = ["flog%d" % b for b in range(lo // 512, hi // 512)]
        T = lambda k: k + tag
        P.op("act", lambda e: e.activation(out=ework[s4, cs_], in_=flog[s4, cs_], func=AF.Exp, scale=-1.0, bias=negb[s4, 0:1]),
             reads=fk + ["negb"], writes=[T("ework")])
        P.op("act", lambda e: e.activation(out=flog[s4, cs_], in_=ework[s4, cs_], func=AF.Ln, scale=1.0, bias=1.0),
             reads=[T("ework")], writes=[T("sp")])
        init = 0.0 if lo == 0 else carry[s4, 0:1]
        P.op("dve", lambda e: e.tensor_tensor_scan(out=ework[s4, cs_], data0=onesf[s4, cs_], data1=flog[s4, cs_], initial=init,
                                                   op0=ALU.mult, op1=ALU.add),
             reads=[T("sp"), "onesf", "carry"], writes=[T("cneg")])
        P.op("dve", lambda e: e.tensor_copy(out=carry[s4, 0:1], in_=ework[s4, hi - 1:hi]), reads=[T("cneg")], writes=["carry"])
        P.op("dve", lambda e: e.tensor_scalar_mul(out=chat[s4, cs_], in0=ework[s4, cs_], scalar1=-1.0),
             reads=[T("cneg")], writes=[T("chat")])
        P.dma(qT[:, 64, cs_], chat[s4, cs_], reads=[T("chat")], writes=[T("qT64")])
        P.op("dve", lambda e: e.tensor_copy(out=pA[s4, cs_], in_=ework[s4, cs_]), reads=[T("cneg"), "pA"], writes=[T("pA")])
        P.dma(kT[:, 65, cs_], pA[s4, cs_], reads=[T("pA")], writes=[T("kT65")], q="pool")
        P.op("dve", lambda e: e.tensor_tensor(out=flog[s4, cs_], in0=ework[s4, cs_], in1=pA[s4, cs_], op=ALU.subtract),
             reads=[T("cneg"), T("pA"), T("sp")], writes=[T("r1")])
        P.op("dve", lambda e: e.tensor_copy(out=pB[s4, cs_], in_=flog[s4, cs_]), reads=[T("r1")], writes=[T("pB")])
        P.dma(kT[:, 66, cs_], pB[s4, cs_], reads=[T("pB")], writes=[T("kT66")], q="pool")
        P.op("dve", lambda e: e.tensor_tensor(out=ework[s4, cs_], in0=flog[s4, cs_], in1=pB[s4, cs_], op=ALU.subtract),
             reads=[T("r1"), T("pB"), T("cneg"), T("chat"), T("pA"), "carry"], writes=[T("r2")])
        P.op("dve", lambda e: e.tensor_copy(out=pA[s4, cs_], in_=ework[s4, cs_]), reads=[T("r2"), T("pA"), T("kT65")],
             writes=[T("pA2")])
        P.dma(kT[:, 67, cs_], pA[s4, cs_], reads=[T("pA2")], writes=[T("kT67")])

    for blk in range(NB):
        if blk + 1 < NB:
            norm_block(blk + 1)
        if blk + 2 < NB:
            ne_block(blk + 2)
        if blk == NB - 1 and NB > 1:
            c_chain(0, (NB - 1) * 512, "_a")
        groups_block(blk)
    c_chain((NB - 1) * 512 if NB > 1 else 0, L, "_b")


def emit_c1(P, L=SEQ):
    V = _V(P)
    NJ = L // 8
    oT = P.io["oT"]; yret = P.io["yretT"]; GY = P.io["GY"]; GYb = P.io["GYb"]; sg = P.io["sgT"]
    wglu = P.io["wglu"]; ybuf = P.io["ybuf"]
    Wg = P.sb("Wg", [128, 2, 128], BF16)
    wst = P.sb("wst", [128, 2, 128], F32)
    yc = [P.sb("yc%d" % i, [128, 4, 512], BF16) for i in range(2)]
    la = [P.sb("la%d" % i, [128, 512], F32) for i in range(4)]
    lb = [P.sb("lb%d" % i, [128, 512], F32) for i in range(4)]
    lc = [P.sb("lc%d" % i, [128, 512], F32) for i in range(4)]
    GYs = P.sb("GYs", [128, 8, NJ], F32)
    GBs = P.sb("GBs", [128, 2, 8, NJ], BF16)
    for g in range(16):
        prt = slice(16 * (g % 8), 16 * (g % 8) + 16)
        P.dma(GBs[prt, g // 8, :, :], GYb[g].rearrange("(t h) j -> h t j", t=8), writes=["GBs%d" % g], q="sp")
        if g < 8:
            P.dma(GYs[prt, :, :], GY[g].rearrange("(t h) j -> h t j", t=8), writes=["GYs%d" % g])
    gbk = ["GBs%d" % g for g in range(16)]
    gyk = ["GYs%d" % g for g in range(8)]
    psg = [P.ps("psg%d" % i, [128, 512], F32) for i in range(2)]
    P.dma(wst[:], wglu.rearrange("(k p) c -> p k c", p=128), writes=["wst"])
    P.copy("dve", Wg[:], wst[:], reads=["wst"], writes=["Wg"])
    cnt = 0
    ykeys = []
    for c in range(L // 512):
        cs = slice(c * 512, (c + 1) * 512)
        jr = slice(c * 64, c * 64 + 64)
        y = yc[c % 2]
        yk = "yc%d" % (c % 2)
        for kt in range(4):
            i = cnt % 4
            cnt += 1
            Ak = ["la%d_%d" % (i, q) for q in range(8)]
            Bk = ["lb%d_%d" % (i, q) for q in range(2)]
            A, B, C = (la[i][:], Ak), (lb[i][:], Bk), (lc[i][:], "lc%d" % i)
            P.dma(lc[i][:], sg[kt * 128:(kt + 1) * 128, cs], writes=[C[1]], q="sp")
            yo = (y[:, kt, :], yk)
            if kt < 2:
                for hh in range(2):
                    h = 2 * kt + hh
                    ps_ = slice(64 * hh, 64 * hh + 64)
                    P.dma(la[i][ps_, :], oT[h, 0:64, cs], writes=[Ak[hh]])
                    P.dma(lb[i][ps_, :], oT[h, 64:65, cs].to_broadcast([64, 512]), writes=[Bk[hh]], q="sp")
                P.op("dve", lambda e, i=i: e.reciprocal(out=lb[i][:], in_=lb[i][:]), reads=Bk, writes=Bk)
                V.tt("dve", A, A, B, ALU.mult)
                V.tt("pool", yo, A, C, ALU.mult)
            elif kt == 2:
                pg, pgk = psg[c % 2], "P:psg%d" % (c % 2)
                for k2 in range(2):
                    P.op("pe", lambda e, k2=k2, pg=pg: e.matmul(pg[:], lhsT=Wg[:, k2, :], rhs=GBs[:, k2, :, jr],
                                                             start=(k2 == 0), stop=(k2 == 1)),
                         reads=["Wg"] + gbk, writes=[pgk])
                V.act(B, (pg[:], pgk), AF.Sigmoid)
                V.tt("dve", (la[i][:].rearrange("p (t j) -> p t j", t=8), Ak), (GYs[:, :, jr], gyk),
                     (lb[i][:].rearrange("p (t j) -> p t j", t=8), Bk), ALU.mult)
                V.tt("pool", (y[:, kt, :].rearrange("p (j t) -> p j t", t=8), yk),
                     (la[i][:].rearrange("p (t j) -> p j t", t=8), A[1]),
                     (lc[i][:].rearrange("p (j t) -> p j t", t=8), C[1]), ALU.mult)
            else:
                P.dma(la[i][:], yret[:, cs], writes=Ak)
                V.tt("pool", yo, A, C, ALU.mult)
        PL = P.io["part_len"]
        q = (c * 512) // PL
        lo = c * 512 - q * PL
        P.dma(ybuf[q][:, lo:lo + 512].rearrange("(k p) t -> p k t", p=128), y[:], reads=[yk], writes=["ybuf%d" % c], q="pool")
        ykeys.append("ybuf%d" % c)
        if lo + 512 == PL:
            P.collective("AllGather", [ybuf[q]], [P.io["yall"][q]], PAIR_GROUPS, reads=ykeys, writes=["yall%d" % q])
            ykeys = []
            yield


def emit_c2(P, L=SEQ, final=False):
    V = _V(P)
    yall = P.io["yall"]; wout = P.io["wout"]; x = P.io["x"]; xo = P.io["xo"]; fnw = P.io["fnw"]
    Wo = P.sb("o_Wo", [128, 8, 1024], BF16)
    wst = [P.sb("o_wst%d" % i, [128, 1024], F32) for i in range(2)]
    fn = P.sb("o_fn", [128, D_MODEL], F32)
    yT = [P.sb("o_yT%d" % i, [128, 8, 512], BF16) for i in range(2)]
    xt = [P.sb("o_xt%d" % i, [128, D_MODEL], F32) for i in range(2)]
    xn = [P.sb("o_xn%d" % i, [128, D_MODEL], F32) for i in range(2)]
    sqj = P.sb("o_sqj", [128, D_MODEL], F32)
    ss = [P.sb("o_ss%d" % i, [128, 1], F32) for i in range(2)]
    pso = [P.ps("o_pso%d" % i, [128, 512], F32) for i in range(4)]
    if final:
        P.dma(fn[:], fnw, writes=["fn"])
    for kt in range(8):
        w, wk = wst[kt % 2], "wst%d" % (kt % 2)
        P.dma(w[:], wout[kt * 128:(kt + 1) * 128, :], writes=[wk])
        P.copy("dve" if kt % 2 else "pool", Wo[:, kt, :], w[:], reads=[wk], writes=["Wo%d" % kt])
    yield
    for c in range(L // 512):
        cs = slice(c * 512, (c + 1) * 512)
        y = yT[c % 2]
        yk = "yT%d" % (c % 2)
        PL = P.io["part_len"]
        q = (c * 512) // PL
        lo = c * 512 - q * PL
        if lo == 0 and c > 0:
            yield
        P.dma(y[:], yall[q][:, lo:lo + 512].rearrange("(k p) t -> p k t", p=128), reads=["yall%d" % q], writes=[yk])
        for tt in range(4):
            n = c * 4 + tt
            xi = n % 2
            P.dma(xt[xi][:], x[n * 128:(n + 1) * 128, :], reads=["xrow%d" % n], writes=["xt%d" % xi])
            for ch in range(2):
                po = pso[(n * 2 + ch) % 4]
                pok = "P:pso%d" % ((n * 2 + ch) % 4)
                for kt in range(8):
                    P.op("pe", lambda e, kt=kt, po=po, ch=ch, tt=tt, y=y: e.matmul(
                        po[:], lhsT=y[:, kt, tt * 128:(tt + 1) * 128], rhs=Wo[:, kt, ch * 512:(ch + 1) * 512],
                        start=(kt == 0), stop=(kt == 7)), reads=[yk, "Wo%d" % kt], writes=[pok])
                V.tt("dve", (xn[xi][:, ch * 512:(ch + 1) * 512], "xn%d" % xi), (po[:], pok),
                     (xt[xi][:, ch * 512:(ch + 1) * 512], "xt%d" % xi), ALU.add)
            if final:
                P.op("act", lambda e, xi=xi: e.activation(out=sqj[:], in_=xn[xi][:], func=AF.Square, accum_out=ss[xi][:]),
                     reads=["xn%d" % xi], writes=["sqj", "ss%d" % xi])
                P.op("act", lambda e, xi=xi: e.activation(out=ss[xi][:], in_=ss[xi][:], func=AF.Sqrt, scale=1.0 / D_MODEL, bias=EPS),
                     reads=["ss%d" % xi], writes=["ss%d" % xi])
                P.op("dve", lambda e, xi=xi: e.reciprocal(out=ss[xi][:], in_=ss[xi][:]), reads=["ss%d" % xi], writes=["ss%d" % xi])
                V.stt((xn[xi][:], "xn%d" % xi), (xn[xi][:], "xn%d" % xi), ss[xi][:, 0:1], (fn[:], "fn"), ALU.mult, ALU.mult,
                      extra=["ss%d" % xi])
            P.dma(xo[n * 128:(n + 1) * 128, :], xn[xi][:], reads=["xn%d" % xi], writes=["xrow%d" % n], is_output=final, q="pool")


PAIR_GROUPS = [[0, 1], [2, 3], [4, 5], [6, 7]]


def build_fused(L=SEQ, depth=DEPTH):
    P = Prog()
    NJ = L // 8
    x = P.dram_in("x", [L, D_MODEL], F32)
    wA = P.dram_in("wA", [depth, D_MODEL, F_NCOL], F32)
    normw = P.dram_in("normw", [depth, 128, 8], F32)
    bf = P.dram_in("bf", [depth, 128, 1], F32)
    ident = P.dram_in("ident", [128, 128], F32)
    cosT = P.dram_in("cosT", [128, L], F32)
    sinT = P.dram_in("sinT", [128, L], F32)
    maskneg = P.dram_in("maskneg", [128, 128], F32)
    DT = P.dram_in("DT", [128, 256], F32)
    wqT = P.dram_in("wqT", [128, 128], F32)
    wk = P.dram_in("wk", [128, 2], F32)
    gblk = P.dram_in("gblk", [128, 1], F32)
    BD = P.dram_in("BD", [128, 128], F32)
    gnw = P.dram_in("gnw", [depth, 128, 1], F32)
    s5 = {}
    for nm, shp in (("are", [128, 8]), ("aim", [128, 8]), ("ldt", [128, 8]), ("bre", [128, 8, 16]), ("bim", [128, 8, 16]),
                    ("cre", [128, 8, 16]), ("cim", [128, 8, 16]), ("drep", [128, 16])):
        s5[nm] = P.dram_in(nm, [depth] + shp, F32)
    nvec = P.dram_in("nvec", [128, 24], F32)
    jvec = P.dram_in("jvec", [128, NJ], F32)
    mask01 = P.dram_in("mask01", [128, 128], F32)
    wglu = P.dram_in("wglu", [depth, 256, 128], F32)
    wout = P.dram_in("wout", [depth, D_MODEL, D_MODEL], F32)
    fnw = P.dram_in("fnw", [128, D_MODEL], F32)
    out = P.dram_out("out", [L, D_MODEL], F32)

    T = P.dram_tmp
    xbuf = T("xbuf", [L, D_MODEL], F32)
    sc = dict(qT=T("s_qT", [4, KROWS, L], BF16), kT=T("s_kT", [4, KROWS, L], BF16), Vfox=T("s_Vfox", [4, L, 64], BF16),
              Vtm=T("s_Vtm", [L, 128], BF16), Ktm=T("s_Ktm", [L, 128], BF16), cneg=T("s_cneg", [4, L], F32),
              rqT=T("s_rqT", [128, L], BF16), rkT=T("s_rkT", [128, L], BF16), suP=T("s_suP", [16, 128, NJ], F32),
              suPb=T("s_suPb", [16, 128, NJ], BF16), sgT=T("s_sgT", [512, L], F32), oT=T("s_oT", [4, 65, L], F32),
              yretT=T("s_yretT", [128, L], F32), GY=T("s_GY", [16, 128, NJ], F32), GYb=T("s_GYb", [16, 128, NJ], BF16),
              part_len=min(L, 1024),
              ybuf=[T("s_ybuf%d" % q, [512, min(L, 1024)], BF16) for q in range(max(1, L // 1024))],
              yall=[T("s_yall%d" % q, [1024, min(L, 1024)], BF16) for q in range(max(1, L // 1024))])
    for l in range(depth):
        xin = x if l == 0 else xbuf
        final = (l == depth - 1)
        io = dict(sc)
        io.update(x=xin, wA=wA[l], normw=normw[l], bf=bf[l], ident=ident, cosT=cosT, sinT=sinT)
        with P.scope("a%d_" % l, io):
            emit_phase_a(P, L)
        iofr = dict(qT=sc["qT"], kT=sc["kT"], V=sc["Vfox"], cnegTM=sc["cneg"], maskneg=maskneg, ident=ident, oT=sc["oT"],
                    rqT=sc["rqT"], rkT=sc["rkT"], Ktm=sc["Ktm"], Vtm=sc["Vtm"], DT=DT, wqT=wqT, wk=wk, gblk=gblk, BD=BD,
                    gnw=gnw[l], yretT=sc["yretT"])
        with P.scope("f%d_" % l, iofr):
            build_fox(L, 4, P=P, side=_ret_gen(L, 99, P=P, shared=True))
        io5 = {k: v[l] for k, v in s5.items()}
        io5.update(Ub=sc["suPb"], U32=sc["suP"], nvec=nvec, jvec=jvec, mask01=mask01, ident=ident, GY=sc["GY"], GYb=sc["GYb"])
        with P.scope("s%d_" % l, io5):
            build_s5(L, P=P)
        ioc = dict(oT=sc["oT"], yretT=sc["yretT"], GY=sc["GY"], GYb=sc["GYb"], sgT=sc["sgT"], wglu=wglu[l], ybuf=sc["ybuf"],
                   yall=sc["yall"], part_len=sc["part_len"], wout=wout[l], x=xin, xo=(out if final else xbuf), fnw=fnw)
        with P.scope("c%d_" % l, ioc):
            g1 = emit_c1(P, L)
            g2 = emit_c2(P, L, final)
            next(g2)
            first = True
            for _ in g1:
                if not first:
                    next(g2, None)
                first = False
            for _ in g2:
                pass
    return P.finish()


def _f_cols(r):
    fq, fk, fv, flog, su, rq, rk, rv, gate = 0, 512, 1024, 1536, 1544, 1800, 2056, 2312, 2568
    cols = []
    heads = [4 * r + i for i in range(4)]
    for h in heads:
        cols += list(range(fq + 64 * h, fq + 64 * h + 64))
    cols += list(range(fk + 64 * heads[0], fk + 64 * heads[0] + 64)) + list(range(flog + 4 * r, flog + 4 * r + 4))
    for h in heads[1:]:
        cols += list(range(fk + 64 * h, fk + 64 * h + 64))
    rh = [2 * r, 2 * r + 1]

    def plain(base):
        c = []
        for h in rh:
            c += list(range(base + 64 * h, base + 64 * h + 64))
        return c

    def swapped(base):
        c = []
        for h in rh:
            c += list(range(base + 64 * h + 32, base + 64 * h + 64)) + list(range(base + 64 * h, base + 64 * h + 32))
        return c

    cols += plain(rq) + swapped(rq) + plain(rk) + swapped(rk)
    cols += list(range(su + 128 * r, su + 128 * r + 128)) + list(range(su + 128 * (1 - r), su + 128 * (1 - r) + 128))
    cols += list(range(gate + 256 * r, gate + 256 * r + 256))
    cols += list(range(gate + 512 + 128 * r, gate + 512 + 128 * r + 128))
    cols += list(range(gate + 768 + 128 * r, gate + 768 + 128 * r + 128))
    cols += list(range(fv + 256 * r, fv + 256 * r + 256)) + plain(rv)
    assert len(cols) == F_NCOL
    return np.array(cols)


def _s5_perm(r):
    return np.array(list(range(8 * r, 8 * r + 8)) + list(range(8 * (1 - r), 8 * (1 - r) + 8)))


def _wout_rows():
    rows = []
    for r in range(2):
        rows += list(range(256 * r, 256 * r + 256)) + list(range(512 + 128 * r, 512 + 128 * r + 128)) \
            + list(range(768 + 128 * r, 768 + 128 * r + 128))
    return np.array(rows)


def _fused_inputs(inp, L, depth):
    f = lambda a: np.asarray(a, np.float32)
    ident = np.eye(128, dtype=np.float32)
    cosT, sinT = _rope_tables(L)
    maps = []
    wr = _wout_rows()
    wout_p = _c(f(inp["w_out"])[:depth][:, wr, :])
    fnw_t = _c(np.tile(f(inp["final_norm_w"])[None, :], (128, 1)))
    for b in range(BATCH):
        for r in range(2):
            m = dict(x=_c(f(inp["x"])[b, :L]), ident=ident, cosT=cosT, sinT=sinT, maskneg=_causal_maskneg(), fnw=fnw_t,
                     wout=wout_p)
            cols = _f_cols(r)
            m["wA"] = _c(f(inp["w_in"])[:depth][:, :, cols])
            m["normw"] = _c(f(inp["norm_w"])[:depth].reshape(depth, 8, 128).transpose(0, 2, 1))
            bfv = np.zeros((depth, 128, 1), np.float32)
            bfv[:, 64:68, 0] = f(inp["fox_b_f"])[:depth, 4 * r:4 * r + 4]
            m["bf"] = bfv
            DT, wq, wk, gb, BD = _ret_tables(r)
            m.update(DT=DT, wqT=wq, wk=wk, gblk=gb, BD=BD)
            m["gnw"] = _c(f(inp["ret_gn_w"])[:depth, 128 * r:128 * r + 128].reshape(depth, 128, 1))
            perm = _s5_perm(r)
            per = {k: [] for k in ("are", "aim", "ldt", "bre", "bim", "cre", "cim", "drep")}
            for l in range(depth):
                sub = {k: f(inp[k])[l][perm] for k in ("s5_a_re", "s5_a_im", "s5_b_re", "s5_b_im", "s5_c_re", "s5_c_im", "s5_log_dt")}
                sub["s5_d"] = f(inp["s5_d"])[l].reshape(16, 16)[perm].reshape(256)
                hi = _s5_host_inputs({k: v[None] for k, v in sub.items()}, 0, L)
                for k in per:
                    per[k].append(hi[k])
                m["nvec"], m["jvec"], m["mask01"] = hi["nvec"], hi["jvec"], hi["mask01"]
            for k in per:
                m[k] = _c(np.stack(per[k], 0))
            chan = (perm[:, None] * 16 + np.arange(16)[None, :]).reshape(256)
            m["wglu"] = _c(f(inp["s5_w_glu"])[:depth][:, chan, :][:, :, 128 * r:128 * r + 128])
            maps.append(m)
    return maps


_FUSED = {}


def kernel(x, norm_w, w_in, fox_b_f, s5_a_re, s5_a_im, s5_b_re, s5_b_im, s5_c_re, s5_c_im, s5_d,
           s5_log_dt, s5_w_glu, ret_gn_w, w_out, final_norm_w, _L=SEQ, _depth=DEPTH):
    inp = dict(x=x, norm_w=norm_w, w_in=w_in, fox_b_f=fox_b_f, s5_a_re=s5_a_re, s5_a_im=s5_a_im, s5_b_re=s5_b_re,
               s5_b_im=s5_b_im, s5_c_re=s5_c_re, s5_c_im=s5_c_im, s5_d=s5_d, s5_log_dt=s5_log_dt, s5_w_glu=s5_w_glu,
               ret_gn_w=ret_gn_w, w_out=w_out, final_norm_w=final_norm_w)
    key = (_L, _depth)
    if key not in _FUSED:
        _FUSED[key] = build_fused(_L, _depth)
    maps = _fused_inputs(inp, _L, _depth)
    res = run_bass_kernel_spmd(_FUSED[key], maps, core_ids=list(range(NCORES))).results
    out = np.zeros((BATCH, _L, D_MODEL), np.float32)
    half = _L // 2
    for b in range(BATCH):
        for r in range(2):
            out[b, half * r:half * (r + 1)] = np.asarray(res[2 * b + r]["out"])[half * r:half * (r + 1)]
    return out
```
